# Optimizing a Trainium2 kernel written in Bass

```python
import jax, jax.numpy as jnp
from jax import lax
import numpy as np

D_MODEL = 1024
BATCH = 8
SEQ = 4096
DEPTH = 4
DEC_BATCH = 8
DEC_SEQ = 16
PAST_LEN = 2048

CHUNK = 64
N_MIXERS = 2
N_GMLP = (DEPTH + 1) // 2
N_FOX = DEPTH // 2
EXPAND = 2
D_BRANCH = EXPAND * D_MODEL
GMLP_BLOCK = 128
GMLP_GROUPS = 16
GMLP_GROUP_DIM = D_BRANCH // GMLP_GROUPS
FOX_HEADS = 16
FOX_HEAD_DIM = D_BRANCH // FOX_HEADS
Q_BLOCK = 128
PLE_DIM = 256
FORGET_BIAS_INIT = 4.0
RMS_EPS = 1e-6
LN_EPS = 1e-5

kernel_name = "hybrid_gmlp_fox_streaming_step"


def rms_norm(x, g):
    xf = x.astype(jnp.float32)
    y = xf * lax.rsqrt(jnp.mean(xf * xf, axis=-1, keepdims=True) + RMS_EPS)
    return (y * g.astype(jnp.float32)).astype(x.dtype)


def layer_norm(x, g, b):
    xf = x.astype(jnp.float32)
    mu = jnp.mean(xf, axis=-1, keepdims=True)
    var = jnp.mean(jnp.square(xf - mu), axis=-1, keepdims=True)
    y = (xf - mu) * lax.rsqrt(var + LN_EPS) * g.astype(jnp.float32) + b.astype(jnp.float32)
    return y.astype(x.dtype)


def chunk_causal_mask(n):
    i = jnp.arange(n)
    return (i[None, :] // CHUNK) <= (i[:, None] // CHUNK)


def gmlp_branch(h, w_in, ln_g, ln_b, w_s, b_s, w_out):
    B, T, _ = h.shape
    E = D_BRANCH
    proj = h @ w_in
    uv = jax.nn.gelu(proj[..., :2 * E])
    z = proj[..., 2 * E:]
    u = uv[..., :E]
    v = layer_norm(uv[..., E:], ln_g, ln_b)
    n = min(T, GMLP_BLOCK)
    nb = T // n
    wm = jnp.where(chunk_causal_mask(n)[None], w_s[:, :n, :n], 0.0).astype(v.dtype)
    vb = v.reshape(B, nb, n, GMLP_GROUPS, GMLP_GROUP_DIM)
    s = jnp.einsum('gij,bnjgc->bnigc', wm, vb) + b_s[:, :n].T[None, None, :, :, None].astype(v.dtype)
    y = u * s.reshape(B, T, E) * jax.nn.silu(z)
    return y @ w_out, v


def fox_project(h, w_in, b_f):
    B, T, _ = h.shape
    E = D_BRANCH
    proj = h @ w_in
    shp = (B, T, FOX_HEADS, FOX_HEAD_DIM)
    q = proj[..., :E].reshape(shp)
    k = proj[..., E:2 * E].reshape(shp)
    v = proj[..., 2 * E:3 * E].reshape(shp)
    z = proj[..., 3 * E:4 * E]
    logf = jax.nn.log_sigmoid(proj[..., 4 * E:].astype(jnp.float32) + b_f.astype(jnp.float32))
    return q, k, v, z, logf


def fox_attend_prompt(q, k, v, logf):
    B, S, H, dh = q.shape
    scale = dh ** -0.5
    cT = jnp.cumsum(logf, axis=1).transpose(0, 2, 1)
    nq = S // Q_BLOCK
    qb = q.reshape(B, nq, Q_BLOCK, H, dh).transpose(1, 0, 2, 3, 4)
    cqb = cT.reshape(B, H, nq, Q_BLOCK).transpose(2, 0, 1, 3)
    key_pos = jnp.arange(S)

    def block(args):
        qi, ci, start = args
        s = jnp.einsum('bqhd,bkhd->bhqk', qi, k).astype(jnp.float32) * scale
        s = s + ci[..., :, None] - cT[:, :, None, :]
        qpos = start + jnp.arange(Q_BLOCK)
        s = jnp.where(key_pos[None, :] <= qpos[:, None], s, -jnp.inf)
        p = jax.nn.softmax(s, axis=-1).astype(v.dtype)
        return jnp.einsum('bhqk,bkhd->bqhd', p, v)

    o = lax.map(block, (qb, cqb, jnp.arange(nq, dtype=jnp.int32) * Q_BLOCK))
    return o.transpose(1, 0, 2, 3, 4).reshape(B, S, H * dh)


def fox_attend_sample(q, k_new, v_new, logf_new, k_cache, v_cache, logf_cache):
    B, T, H, dh = q.shape
    P = k_cache.shape[1]
    scale = dh ** -0.5
    k = jnp.concatenate([k_cache.astype(k_new.dtype), k_new], axis=1)
    v = jnp.concatenate([v_cache.astype(v_new.dtype), v_new], axis=1)
    cT = jnp.cumsum(jnp.concatenate([logf_cache.astype(jnp.float32), logf_new], axis=1), axis=1).transpose(0, 2, 1)
    s = jnp.einsum('bqhd,bkhd->bhqk', q, k).astype(jnp.float32) * scale
    s = s + cT[:, :, P:, None] - cT[:, :, None, :]
    mask = jnp.arange(P + T)[None, :] <= (P + jnp.arange(T))[:, None]
    s = jnp.where(mask, s, -jnp.inf)
    p = jax.nn.softmax(s, axis=-1).astype(v.dtype)
    return jnp.einsum('bhqk,bkhd->bqhd', p, v).reshape(B, T, H * dh)


def setup_inputs(seed: int = 0) -> dict:
    key = jax.random.key(seed)
    ks = jax.random.split(key, 20)
    f32 = jnp.float32
    E = D_BRANCH

    def nrm(k, shape, scale=1.0):
        return jax.random.normal(k, shape, f32) * scale

    return {
        "x_prompt": nrm(ks[0], (BATCH, SEQ, D_MODEL)),
        "x_sample": nrm(ks[1], (DEC_BATCH, DEC_SEQ, D_MODEL)),
        "cache_fox_k": nrm(ks[2], (N_FOX, DEC_BATCH, PAST_LEN, FOX_HEADS, FOX_HEAD_DIM)),
        "cache_fox_v": nrm(ks[3], (N_FOX, DEC_BATCH, PAST_LEN, FOX_HEADS, FOX_HEAD_DIM)),
        "cache_fox_logf": jax.nn.log_sigmoid(FORGET_BIAS_INIT + nrm(ks[4], (N_FOX, DEC_BATCH, PAST_LEN, FOX_HEADS))),
        "p_prompt": nrm(ks[5], (DEPTH, BATCH, SEQ, PLE_DIM)),
        "p_sample": nrm(ks[6], (DEPTH, DEC_BATCH, DEC_SEQ, PLE_DIM)),
        "norm_pre": 1.0 + 0.1 * nrm(ks[7], (DEPTH, D_MODEL)),
        "norm_post": 1.0 + 0.1 * nrm(ks[8], (DEPTH, D_MODEL)),
        "gmlp_w_in": nrm(ks[9], (N_GMLP, D_MODEL, 3 * E), D_MODEL ** -0.5),
        "gmlp_ln_g": 1.0 + 0.1 * nrm(ks[10], (N_GMLP, E)),
        "gmlp_ln_b": 0.02 * nrm(ks[11], (N_GMLP, E)),
        "gmlp_w_s": nrm(ks[12], (N_GMLP, GMLP_GROUPS, GMLP_BLOCK, GMLP_BLOCK), GMLP_BLOCK ** -0.5),
        "gmlp_b_s": 1.0 + 0.1 * nrm(ks[13], (N_GMLP, GMLP_GROUPS, GMLP_BLOCK)),
        "gmlp_w_out": nrm(ks[14], (N_GMLP, E, D_MODEL), E ** -0.5),
        "fox_w_in": nrm(ks[15], (N_FOX, D_MODEL, 4 * E + FOX_HEADS), D_MODEL ** -0.5),
        "fox_b_f": FORGET_BIAS_INIT + 0.5 * nrm(ks[16], (N_FOX, FOX_HEADS)),
        "fox_w_out": nrm(ks[17], (N_FOX, E, D_MODEL), E ** -0.5),
        "ple_w_proj": nrm(ks[18], (DEPTH, PLE_DIM, D_MODEL), PLE_DIM ** -0.5),
        "ple_w_gate": nrm(ks[19], (DEPTH, D_MODEL, D_MODEL), D_MODEL ** -0.5),
    }


def reference(x_prompt, x_sample, cache_fox_k, cache_fox_v, cache_fox_logf, p_prompt, p_sample,
              norm_pre, norm_post, gmlp_w_in, gmlp_ln_g, gmlp_ln_b, gmlp_w_s, gmlp_b_s, gmlp_w_out,
              fox_w_in, fox_b_f, fox_w_out, ple_w_proj, ple_w_gate):
    xp, xs = x_prompt, x_sample
    gmlp_v_s = []
    fk_p, fv_p, flf_p = [], [], []
    fk_s, fv_s, flf_s = [], [], []
    for i in range(DEPTH):
        j = i // N_MIXERS
        hp = rms_norm(xp, norm_pre[i])
        hs = rms_norm(xs, norm_pre[i])
        if i % N_MIXERS == 0:
            op, _ = gmlp_branch(hp, gmlp_w_in[j], gmlp_ln_g[j], gmlp_ln_b[j], gmlp_w_s[j], gmlp_b_s[j], gmlp_w_out[j])
            os_, vs = gmlp_branch(hs, gmlp_w_in[j], gmlp_ln_g[j], gmlp_ln_b[j], gmlp_w_s[j], gmlp_b_s[j], gmlp_w_out[j])
            gmlp_v_s.append(vs)
        else:
            qp, kp, vp, zp, lfp = fox_project(hp, fox_w_in[j], fox_b_f[j])
            op = (fox_attend_prompt(qp, kp, vp, lfp) * jax.nn.silu(zp)) @ fox_w_out[j]
            qs, ks_, vs_, zs, lfs = fox_project(hs, fox_w_in[j], fox_b_f[j])
            att_s = fox_attend_sample(qs, ks_, vs_, lfs, cache_fox_k[j], cache_fox_v[j], cache_fox_logf[j])
            os_ = (att_s * jax.nn.silu(zs)) @ fox_w_out[j]
            fk_p.append(kp); fv_p.append(vp); flf_p.append(lfp)
            fk_s.append(ks_); fv_s.append(vs_); flf_s.append(lfs)
        xp = xp + rms_norm(op, norm_post[i])
        xs = xs + rms_norm(os_, norm_post[i])
        xp = xp + jax.nn.sigmoid(xp @ ple_w_gate[i]) * (p_prompt[i] @ ple_w_proj[i])
        xs = xs + jax.nn.sigmoid(xs @ ple_w_gate[i]) * (p_sample[i] @ ple_w_proj[i])
    state_gmlp_v_sample = jnp.stack(gmlp_v_s)
    fox_k_prompt = jnp.stack(fk_p)
    fox_v_prompt = jnp.stack(fv_p)
    fox_logf_prompt = jnp.stack(flf_p)
    fox_k_sample = jnp.stack(fk_s)
    fox_v_sample = jnp.stack(fv_s)
    fox_logf_sample = jnp.stack(flf_s)
    return (xp, xs, state_gmlp_v_sample, fox_k_prompt, fox_v_prompt, fox_logf_prompt, fox_k_sample, fox_v_sample, fox_logf_sample)
```

```python
import contextlib
import numpy as np
import concourse.bass as bass
import concourse.mybir as mybir
from concourse.bass_utils import run_bass_kernel_spmd

F32 = mybir.dt.float32
BF16 = mybir.dt.bfloat16
AF = mybir.ActivationFunctionType
ALU = mybir.AluOpType

D = 1024
E = 2048
SEQ = 4096
TS = 16
PAST = 2048
NH = 16
DH = 128
PLE = 256
NT = 33
TOK = 4224
SCALE = float(DH) ** -0.5
RMS_EPS = 1e-6
LN_EPS = 1e-5
FW = 4 * E + NH


class Sem:
    def __init__(self, h):
        self.h = h
        self.count = 0
        self.last_op = None


class Op:
    def __init__(self, eng, fns, is_dma):
        self.eng = eng
        self.fns = fns
        self.deps = []
        self.needs_inc = False
        self.sem = None
        self.ticket = None
        self.is_dma = is_dma


class Res:
    def __init__(self, name, arena=None, lo=0, hi=0):
        self.name = name
        self.arena = arena
        self.lo = lo
        self.hi = hi
        self.writers = []
        self.readers = []
        self.overlaps = []


class Prog:
    ENGS = ("sync", "act", "dve", "pool", "pe")
    BLK = {"sync": "sync", "act": "scalar", "dve": "vector", "pool": "gpsimd", "pe": "tensor"}

    def __init__(self, nc, stack):
        self.nc = nc
        self.stack = stack
        self.ops = {e: [] for e in self.ENGS}
        self.eng_sem = {}
        self.pools = {}
        self.pool_idx = {}
        self.arena_res = {}
        self.nsem = 0
        self.new_epoch()

    def new_sem(self, name):
        self.nsem += 1
        return Sem(self.stack.enter_context(self.nc.semaphore(f"{name}_{self.nsem}")))

    def new_epoch(self):
        for e in ("act", "dve", "pool", "pe"):
            self.eng_sem[e] = self.new_sem("e_" + e)

    def dma_pool(self, name, n):
        self.pools[name] = [self.new_sem("d_" + name) for _ in range(n)]
        self.pool_idx[name] = 0

    def res(self, name, arena=None, lo=0, hi=0):
        r = Res(name, arena, lo, hi)
        if arena is not None:
            lst = self.arena_res.setdefault(arena, [])
            for o in lst:
                if o.lo < hi and lo < o.hi:
                    r.overlaps.append(o)
                    o.overlaps.append(r)
            lst.append(r)
        return r

    def _dep(self, op, prod, raw):
        if prod is op:
            return
        if (not prod.is_dma) and (not op.is_dma) and prod.eng == op.eng:
            if not raw:
                return
            if op.eng == "pe":
                return
        prod.needs_inc = True
        op.deps.append(prod)

    def op(self, eng, fns, reads=(), writes=(), pool=None):
        if not isinstance(fns, (list, tuple)):
            fns = [fns]
        is_dma = pool is not None
        o = Op(eng, list(fns), is_dma)
        if is_dma:
            ps = self.pools[pool]
            i = self.pool_idx[pool]
            self.pool_idx[pool] = (i + 1) % len(ps)
            o.sem = ps[i]
            if o.sem.last_op is not None:
                o.sem.last_op.needs_inc = True
                o.deps.append(o.sem.last_op)
            o.sem.last_op = o
        else:
            o.sem = self.eng_sem[eng]
        for r in reads:
            for rr in [r] + r.overlaps:
                for w in rr.writers:
                    self._dep(o, w, True)
        for r in writes:
            for rr in [r] + r.overlaps:
                for w in rr.writers:
                    self._dep(o, w, False)
                for w in rr.readers:
                    self._dep(o, w, False)
        for r in reads:
            r.readers.append(o)
        for r in writes:
            r.writers = [o]
            r.readers = []
        self.ops[eng].append(o)
        return o

    def emit(self, final_ops=()):
        nc = self.nc
        for o in final_ops:
            o.needs_inc = True
        for e in self.ENGS:
            for o in self.ops[e]:
                if o.is_dma:
                    o.sem.count += 16 * len(o.fns)
                    o.ticket = o.sem.count
                elif o.needs_inc:
                    o.sem.count += 1
                    o.ticket = o.sem.count

        import os as _os
        if _os.environ.get("KCHECK"):
            self.check()

        def need_of(deps):
            need = {}
            for d in deps:
                k = id(d.sem)
                if d.ticket > need.get(k, (None, 0))[1]:
                    need[k] = (d.sem, d.ticket)
            return need

        with nc.Block() as block:
            for e in self.ENGS:
                ops = self.ops[e]
                if not ops and e != "sync":
                    continue

                def body(eng, ops=ops, e=e):
                    waited = {}
                    for o in ops:
                        for k, (s, v) in need_of(o.deps).items():
                            if waited.get(k, 0) < v:
                                eng.wait_ge(s.h, v)
                                waited[k] = v
                        n = len(o.fns)
                        for i, f in enumerate(o.fns):
                            ins = f(eng)
                            if o.is_dma:
                                ins.then_inc(o.sem.h, 16)
                            elif o.needs_inc and i == n - 1:
                                ins.then_inc(o.sem.h, 1)
                    if e == "sync":
                        for k, (s, v) in need_of(final_ops).items():
                            if waited.get(k, 0) < v:
                                eng.wait_ge(s.h, v)
                                waited[k] = v

                getattr(block, self.BLK[e])(body)


def _prog_check(self):
    pos = {e: 0 for e in self.ENGS}
    done = set()
    semval = {}
    total = sum(len(v) for v in self.ops.values())
    ndone = 0
    progress = True
    while progress:
        progress = False
        for e in self.ENGS:
            while pos[e] < len(self.ops[e]):
                o = self.ops[e][pos[e]]
                ok = True
                for d in o.deps:
                    if d.ticket is None:
                        raise RuntimeError("dep without ticket")
                    if semval.get(id(d.sem), 0) < d.ticket:
                        ok = False
                        break
                if not ok:
                    break
                if o.ticket is not None:
                    prev = semval.get(id(o.sem), 0)
                    exp = o.ticket - (16 * len(o.fns) if o.is_dma else 1)
                    if prev != exp:
                        raise RuntimeError(f"ticket order violation on {e}: prev={prev} exp={exp}")
                    semval[id(o.sem)] = o.ticket
                pos[e] += 1
                ndone += 1
                progress = True
    print("CHECK: done", ndone, "of", total, {e: (pos[e], len(self.ops[e])) for e in self.ENGS})
    if ndone != total:
        raise RuntimeError("DEADLOCK in abstract simulation")


Prog.check = _prog_check


class Arena:
    def __init__(self, P, name, ap2d_bf16, nbytes):
        self.P = P
        self.name = name
        self.ap = ap2d_bf16
        self.nbytes = nbytes
        self.off = 0

    def reset(self, off):
        self.off = off

    def alloc(self, name, free_shape, dtype):
        esz = 4 if dtype == F32 else 2
        n = int(np.prod(free_shape))
        nb = n * esz
        lo = (self.off + 63) // 64 * 64
        hi = lo + nb
        assert hi <= self.nbytes, f"arena overflow {name}: {hi} > {self.nbytes}"
        self.off = hi
        v = self.ap[:, lo // 2:hi // 2]
        if dtype == F32:
            v = v.bitcast(F32)
        if len(free_shape) == 2:
            v = v.rearrange("p (a b) -> p a b", b=free_shape[1])
        elif len(free_shape) == 3:
            v = v.rearrange("p (a b c) -> p a b c", b=free_shape[1], c=free_shape[2])
        r = self.P.res(name, self.name, lo, hi)
        return v, r


def run_stages(stages, tiles):
    K = len(stages)
    n = len(tiles)
    for step in range(n + K - 1):
        for k in range(K - 1, -1, -1):
            i = step - k
            if 0 <= i < n:
                stages[k](tiles[i])


def nrows(t):
    return 128 if t < 32 else TS


ARENA_BYTES = 207 * 1024


def build(n_layers=4, dbg=None):
    dbg = dbg or {}
    TILES = dbg.get('tiles', list(range(NT)))
    PHASES = dbg.get('phases', 'GTFAS')
    nc = bass.Bass("TRN2", target_bir_lowering=False)

    def din(name, shape, dt=F32):
        return nc.dram_tensor(name, shape, dt, kind="ExternalInput").ap()

    def dout(name, shape, dt=F32):
        return nc.dram_tensor(name, shape, dt, kind="ExternalOutput").ap()

    def dscr(name, shape, dt):
        return nc.dram_tensor(name, shape, dt, kind="Internal").ap()

    x_p = din("x_p", [SEQ, D]); x_s = din("x_s", [TS, D])
    ck = din("ck", [2, PAST, E]); cv = din("cv", [2, PAST, E]); clf = din("clf", [2, PAST, NH])
    p_p = din("p_p", [4, SEQ, PLE]); p_s = din("p_s", [4, TS, PLE])
    norm_pre = din("norm_pre", [4, D]); norm_post = din("norm_post", [4, D])
    g_w_in = din("g_w_in", [2, D, 3 * E]); g_ln_g = din("g_ln_g", [2, E]); g_ln_b = din("g_ln_b", [2, E])
    g_w_s = din("g_w_s", [2, 16, 128, 128]); g_b_s = din("g_b_s", [2, 16, 128]); g_w_out = din("g_w_out", [2, E, D])
    f_w_in = din("f_w_in", [2, D, FW]); f_b_f = din("f_b_f", [2, NH]); f_w_out = din("f_w_out", [2, E, D])
    ple_proj = din("ple_proj", [4, PLE, D]); ple_gate = din("ple_gate", [4, D, D])
    consts = din("consts", [128, 640])

    y_p = dout("y_p", [SEQ, D]); y_s = dout("y_s", [TS, D]); gv_s = dout("gv_s", [2, TS, E])
    fk_p = dout("fk_p", [2, SEQ, E]); fv_p = dout("fv_p", [2, SEQ, E]); flf_p = dout("flf_p", [2, SEQ, NH])
    fk_s = dout("fk_s", [2, TS, E]); fv_s = dout("fv_s", [2, TS, E]); flf_s = dout("flf_s", [2, TS, NH])

    Ysc = dscr("Ysc", [TOK, E], BF16)
    QTs = dscr("QTs", [NH, DH, TOK], BF16)
    KTs = dscr("KTs", [NH, DH, TOK], BF16)
    Vbs = dscr("Vbs", [NH, TOK, DH], BF16)
    SZs = dscr("SZs", [NH, TOK, DH], F32)

    with contextlib.ExitStack() as st:
        P = Prog(nc, st)
        P.dma_pool("ld", 6)
        P.dma_pool("st", 8)
        P.dma_pool("w", 4)
        ar_t = st.enter_context(nc.sbuf_tensor("arena", [128, ARENA_BYTES // 2], BF16))
        A = Arena(P, "sb", ar_t[:], ARENA_BYTES)
        ps_t = [st.enter_context(nc.psum_tensor(f"ps{i}", [128, 1024], F32)) for i in range(4)]
        psr = [P.res(f"bank{i}", "psum", i, i + 1) for i in range(8)]

        def bank(i):
            return ps_t[i // 2][:, (i % 2) * 512:(i % 2 + 1) * 512]

        def bank_bf(i):
            return ps_t[i // 2][:].bitcast(BF16)[:, (i % 2) * 1024:(i % 2 + 1) * 1024]

        def pair(i):
            return ps_t[i][:]

        def pair_bf(i):
            return ps_t[i][:].bitcast(BF16)

        XR = [P.res(f"xr{t}") for t in range(NT)]
        YS = [P.res(f"ysc{t}") for t in range(NT)]
        QTr = [P.res(f"qts{t}") for t in range(NT)]
        KTr = [P.res(f"kts{t}") for t in range(NT)]
        VBr = [P.res(f"vbs{t}") for t in range(NT)]
        SZr = [P.res(f"szs{t}") for t in range(NT)]
        LFSr = P.res("flf_s")
        OUTr = P.res("outs")
        stores = []

        def store(fns, reads, writes):
            o = P.op("sync", fns, reads=reads, writes=writes, pool="st")
            stores.append(o)
            return o

        def load(fns, reads, writes):
            return P.op("sync", fns, reads=reads, writes=writes, pool="ld")

        def wload(fns, writes):
            return P.op("pool", fns, reads=[], writes=writes, pool="w")

        def xsrc(layer, t):
            nr = nrows(t)
            if layer == 0:
                return x_p[t * 128:(t + 1) * 128, :] if t < 32 else x_s[0:nr, :]
            return y_p[t * 128:(t + 1) * 128, :] if t < 32 else y_s[0:nr, :]

        def xdst(t):
            return y_p[t * 128:(t + 1) * 128, :] if t < 32 else y_s[0:TS, :]

        def psrc(layer, t):
            return p_p[layer, t * 128:(t + 1) * 128, :] if t < 32 else p_s[layer, 0:TS, :]

        cs, r_cs = A.alloc("cs", [640], F32)
        idb, r_idb = A.alloc("idb", [128], BF16)
        mTb, r_mTb = A.alloc("mTb", [128], BF16)
        nh, r_nh = A.alloc("nh", [1], F32)
        ETab, r_ETab = A.alloc("ETab", [32, 16], F32)
        TcTab, r_TcTab = A.alloc("TcTab", [32, 16], F32)
        ident = cs[:, 0:128]
        tstr = cs[:, 128:256]
        ones = cs[:, 256:384]
        maskT = cs[:, 384:512]
        maskC = cs[:, 512:640]
        PH0 = A.off

        load([lambda e: e.dma_start(out=cs, in_=consts[:, :])], [], [r_cs])
        P.op("dve", lambda e: e.tensor_copy(out=idb, in_=ident), reads=[r_cs], writes=[r_idb])
        P.op("dve", lambda e: e.tensor_copy(out=mTb, in_=maskT), reads=[r_cs], writes=[r_mTb])
        P.op("pool", lambda e: e.memset(nh, -0.5), writes=[r_nh])

        def rstd_ops(ss, r_ss, v1, r_v1, rstd, r_rstd, nr, mul, eps):
            P.op("pool", lambda e: e.tensor_scalar(out=v1[0:nr], in0=ss[0:nr], scalar1=mul, scalar2=eps,
                                                    op0=ALU.mult, op1=ALU.add), reads=[r_ss], writes=[r_v1])
            P.op("pool", lambda e: e.tensor_tensor(out=rstd[0:nr], in0=v1[0:nr], in1=nh[0:nr], op=ALU.pow),
                 reads=[r_v1, r_nh], writes=[r_rstd])

        def head_a(B, t, xt, r_xt):
            nr = nrows(t)
            s = t % 2
            junk, r_junk = B["junk"]
            ss, r_ss = B["ss"][s]
            v1, r_v1 = B["v1"][s]
            rstd, r_rstd = B["rstd"][s]
            hb, r_hb = B["hb"]
            hT, r_hT = B["hT"][s]
            gpre, r_gpre = B["gpre"]
            P.op("act", lambda e: e.activation(out=junk[0:nr], in_=xt[0:nr], func=AF.Square, accum_out=ss[0:nr]),
                 reads=[r_xt], writes=[r_junk, r_ss])
            rstd_ops(ss, r_ss, v1, r_v1, rstd, r_rstd, nr, 1.0 / D, RMS_EPS)
            P.op("dve", lambda e: e.scalar_tensor_tensor(out=hb[0:nr], in0=xt[0:nr], scalar=rstd[0:nr], in1=gpre[0:nr],
                                                          op0=ALU.mult, op1=ALU.mult),
                 reads=[r_xt, r_rstd, r_gpre], writes=[r_hb])

        def head_b(B, t, tb):
            nr = nrows(t)
            s = t % 2
            hb, r_hb = B["hb"]
            hT, r_hT = B["hT"][s]
            pT = bank_bf(tb)
            P.op("pe", [(lambda e, c=c: e.transpose(out=pT[:, c * 128:c * 128 + nr], in_=hb[0:nr, c * 128:(c + 1) * 128],
                                                    identity=idb[0:nr, 0:nr])) for c in range(8)],
                 reads=[r_hb, r_idb], writes=[psr[tb]])
            P.op("act", lambda e: e.activation(out=hT[:, :, 0:nr], in_=pT.rearrange("p (a b) -> p a b", b=128)[:, :, 0:nr],
                                               func=AF.Copy), reads=[psr[tb]], writes=[r_hT])
            return hT, r_hT

        def common_small(B):
            B["junk"] = A.alloc("junk", [1024], BF16)
            for k in ("ss", "v1", "rstd"):
                B[k] = [A.alloc(f"{k}{i}", [1], F32) for i in range(2)]

        def phase_G(layer):
            jl = layer // 2
            P.new_epoch()
            A.reset(PH0)
            B = {}
            Win, r_Win = A.alloc("Win", [8, 3 * E], BF16)
            wsT, r_wsT = A.alloc("wsT", [16, 128], BF16)
            lng, r_lng = A.alloc("lng", [E], F32)
            lnb, r_lnb = A.alloc("lnb", [E], F32)
            bsT, r_bsT = A.alloc("bsT", [16], F32)
            B["gpre"] = A.alloc("gpre", [D], F32)
            gpre, r_gpre = B["gpre"]
            common_small(B)
            xts = [A.alloc(f"xt{i}", [D], F32) for i in range(2)]
            B["hb"] = A.alloc("hb", [D], BF16)
            B["hT"] = [A.alloc(f"hT{i}", [8, 128], BF16) for i in range(2)]
            u, r_u = A.alloc("u", [E], F32)
            vg, r_vg = A.alloc("vg", [E], F32)
            sz, r_sz = A.alloc("sz", [E], F32)
            vb, r_vb = A.alloc("vb", [E], BF16)
            ys = [A.alloc(f"y{i}", [E], BF16) for i in range(2)]
            stats, r_stats = A.alloc("stats", [4, 6], F32)
            mv, r_mv = A.alloc("mv", [2], F32)
            v2, r_v2 = A.alloc("v2", [1], F32)
            rs2, r_rs2 = A.alloc("rs2", [1], F32)

            wload([(lambda e, c=c: e.dma_start(out=Win[:, c, :], in_=g_w_in[jl, c * 128:(c + 1) * 128, :])) for c in range(8)],
                  [r_Win])
            load([lambda e: e.dma_start(out=lng, in_=g_ln_g[jl:jl + 1, :].to_broadcast([128, E])),
                  lambda e: e.dma_start(out=lnb, in_=g_ln_b[jl:jl + 1, :].to_broadcast([128, E])),
                  lambda e: e.dma_start(out=gpre, in_=norm_pre[layer:layer + 1, :].to_broadcast([128, D])),
                  lambda e: e.dma_start(out=bsT, in_=g_b_s[jl].rearrange("g i -> i g"), allow_slow_non_contiguous=True)],
                 [], [r_lng, r_lnb, r_gpre, r_bsT])
            wst = u.rearrange("p (g j) -> p g j", j=128)
            load([lambda e: e.dma_start(out=wst, in_=g_w_s[jl].rearrange("g i j -> i g j"))], [], [r_u])
            for hf in range(2):
                pp_ = pair(hf)
                P.op("pe", [(lambda e, g=g, pp_=pp_: e.transpose(out=pp_[:, (g % 8) * 128:(g % 8 + 1) * 128], in_=wst[:, g, :], identity=ident))
                            for g in range(hf * 8, hf * 8 + 8)], reads=[r_u, r_cs], writes=[psr[2 * hf], psr[2 * hf + 1]])
                P.op("dve", lambda e, hf=hf, pp_=pp_: e.tensor_tensor(
                    out=wsT[:, hf * 8:(hf + 1) * 8, :], in0=pp_.rearrange("p (g i) -> p g i", i=128),
                    in1=maskC.unsqueeze(1).to_broadcast([128, 8, 128]), op=ALU.mult),
                    reads=[psr[2 * hf], psr[2 * hf + 1], r_cs], writes=[r_wsT])

            def sL(t):
                nr = nrows(t)
                xt, r_xt = xts[t % 2]
                load([lambda e: e.dma_start(out=xt[0:nr], in_=xsrc(layer, t))], [XR[t]], [r_xt])

            def s0a(t):
                xt, r_xt = xts[t % 2]
                head_a(B, t, xt, r_xt)

            def s0b(t):
                head_b(B, t, 0)

            def s1(t):
                nr = nrows(t)
                hT, r_hT = B["hT"][t % 2]
                for n in range(12):
                    bk = 1 + (n % 3)
                    P.op("pe", [(lambda e, c=c, n=n, bk=bk: e.matmul(bank(bk)[0:nr, :], lhsT=hT[:, c, 0:nr],
                                                                      rhs=Win[:, c, n * 512:(n + 1) * 512],
                                                                      start=(c == 0), stop=(c == 7))) for c in range(8)],
                         reads=[r_hT, r_Win], writes=[psr[bk]])
                    if n < 4:
                        dst, r_dst, fn = u, r_u, AF.Gelu_apprx_tanh
                    elif n < 8:
                        dst, r_dst, fn = vg, r_vg, AF.Gelu_apprx_tanh
                    else:
                        dst, r_dst, fn = sz, r_sz, AF.Silu
                    c0 = (n % 4) * 512
                    P.op("act", lambda e, dst=dst, fn=fn, c0=c0, bk=bk: e.activation(out=dst[0:nr, c0:c0 + 512], in_=bank(bk)[0:nr, :], func=fn),
                         reads=[psr[bk]], writes=[r_dst])
                    if n == 7:
                        P.op("dve", [(lambda e, q=q: e.bn_stats(out=stats[0:nr, q, :], in_=vg[0:nr, q * 512:(q + 1) * 512])) for q in range(4)],
                             reads=[r_vg], writes=[r_stats])
                        P.op("dve", lambda e: e.bn_aggr(out=mv[0:nr], in_=stats[0:nr].rearrange("p a b -> p (a b)")),
                             reads=[r_stats], writes=[r_mv])
                        P.op("pool", lambda e: e.tensor_scalar(out=v2[0:nr], in0=mv[0:nr, 1:2], scalar1=1.0, scalar2=LN_EPS,
                                                                op0=ALU.mult, op1=ALU.add), reads=[r_mv], writes=[r_v2])
                        P.op("pool", lambda e: e.tensor_tensor(out=rs2[0:nr], in0=v2[0:nr], in1=nh[0:nr], op=ALU.pow),
                             reads=[r_v2, r_nh], writes=[r_rs2])
                        P.op("dve", lambda e: e.tensor_scalar(out=vg[0:nr], in0=vg[0:nr], scalar1=mv[0:nr, 0:1], scalar2=rs2[0:nr],
                                                               op0=ALU.subtract, op1=ALU.mult), reads=[r_vg, r_mv, r_rs2], writes=[r_vg])
                        P.op("dve", lambda e: e.tensor_tensor(out=vg[0:nr], in0=vg[0:nr], in1=lng[0:nr], op=ALU.mult),
                             reads=[r_vg, r_lng], writes=[r_vg])
                        if t < 32:
                            P.op("dve", lambda e: e.tensor_tensor(out=vb[0:nr], in0=vg[0:nr], in1=lnb[0:nr], op=ALU.add),
                                 reads=[r_vg, r_lnb], writes=[r_vb])
                        else:
                            P.op("dve", lambda e: e.tensor_tensor(out=vg[0:nr], in0=vg[0:nr], in1=lnb[0:nr], op=ALU.add),
                                 reads=[r_vg, r_lnb], writes=[r_vg])
                            P.op("dve", lambda e: e.tensor_copy(out=vb[0:nr], in_=vg[0:nr]), reads=[r_vg], writes=[r_vb])
                            store([lambda e: e.dma_start(out=gv_s[jl, :, :], in_=vg[0:nr])], [r_vg], [])

            def s2(t):
                nr = nrows(t)
                y, r_y = ys[t % 2]
                P.op("pe", [(lambda e, g=g: e.matmul(bank(4 + g // 4)[0:nr, (g % 4) * 128:(g % 4 + 1) * 128],
                                                     lhsT=wsT[0:nr, g, 0:nr], rhs=vb[0:nr, g * 128:(g + 1) * 128],
                                                     start=True, stop=True)) for g in range(16)],
                     reads=[r_wsT, r_vb], writes=[psr[4], psr[5], psr[6], psr[7]])
                P.op("dve", [(lambda e, g=g: e.scalar_tensor_tensor(
                    out=u[0:nr, g * 128:(g + 1) * 128], in0=bank(4 + g // 4)[0:nr, (g % 4) * 128:(g % 4 + 1) * 128],
                    scalar=bsT[0:nr, g:g + 1], in1=u[0:nr, g * 128:(g + 1) * 128], op0=ALU.add, op1=ALU.mult)) for g in range(16)],
                    reads=[psr[4], psr[5], psr[6], psr[7], r_bsT, r_u], writes=[r_u])
                P.op("dve", lambda e: e.tensor_tensor(out=y[0:nr], in0=u[0:nr], in1=sz[0:nr], op=ALU.mult),
                     reads=[r_u, r_sz], writes=[r_y])
                store([lambda e: e.dma_start(out=Ysc[t * 128:t * 128 + nr, :], in_=y[0:nr])], [r_y], [YS[t]])

            run_stages([sL, s0a, s0b, s1, s2], TILES)

        def phase_T(layer, w_out_ap):
            P.new_epoch()
            A.reset(PH0)
            Wout, r_Wout = A.alloc("Wout", [16, D], BF16)
            Wg, r_Wg = A.alloc("Wg", [8, D], BF16)
            Wp, r_Wp = A.alloc("Wp", [2, D], BF16)
            gpost, r_gpost = A.alloc("gpost", [D], F32)
            junk, r_junk = A.alloc("junk", [D], BF16)
            NS = 4
            xts = [A.alloc(f"xt{i}", [D], F32) for i in range(NS)]
            yts = [A.alloc(f"yt{i}", [E], BF16) for i in range(NS)]
            pts = [A.alloc(f"pt{i}", [PLE], F32) for i in range(NS)]
            yTs = [A.alloc(f"yT{i}", [16, 128], BF16) for i in range(2)]
            sss = [A.alloc(f"ss{i}", [1], F32) for i in range(2)]
            v1s = [A.alloc(f"v1{i}", [1], F32) for i in range(2)]
            rss = [A.alloc(f"rstd{i}", [1], F32) for i in range(2)]
            tmp, r_tmp = A.alloc("tmp", [D], F32)
            x1s = [A.alloc(f"x1{i}", [D], F32) for i in range(2)]
            x1b, r_x1b = A.alloc("x1b", [D], BF16)
            x1T, r_x1T = A.alloc("x1T", [8, 128], BF16)
            sg, r_sg = A.alloc("sg", [D], F32)
            pb, r_pb = A.alloc("pb", [PLE], BF16)
            pT, r_pT = A.alloc("pT", [2, 128], BF16)
            x2s = [A.alloc(f"x2{i}", [D], F32) for i in range(2)]

            wload([(lambda e, q=q: e.dma_start(out=Wout[:, q * 4:(q + 1) * 4, :],
                                               in_=w_out_ap[q * 512:(q + 1) * 512, :].rearrange("(e p) n -> p e n", p=128)))
                   for q in range(4)], [r_Wout])
            wload([(lambda e, q=q: e.dma_start(out=Wg[:, q * 4:(q + 1) * 4, :],
                                               in_=ple_gate[layer, q * 512:(q + 1) * 512, :].rearrange("(e p) n -> p e n", p=128)))
                   for q in range(2)] +
                  [lambda e: e.dma_start(out=Wp, in_=ple_proj[layer].rearrange("(e p) n -> p e n", p=128))], [r_Wg, r_Wp])
            load([lambda e: e.dma_start(out=gpost, in_=norm_post[layer:layer + 1, :].to_broadcast([128, D]))], [], [r_gpost])

            def sL(t):
                nr = nrows(t)
                xt, r_xt = xts[t % NS]
                yt, r_yt = yts[t % NS]
                pt, r_pt = pts[t % NS]
                load([lambda e: e.dma_start(out=xt[0:nr], in_=xsrc(layer, t)),
                      lambda e: e.dma_start(out=yt[0:nr], in_=Ysc[t * 128:t * 128 + nr, :]),
                      lambda e: e.dma_start(out=pt[0:nr], in_=psrc(layer, t))],
                     [XR[t], YS[t]], [r_xt, r_yt, r_pt])

            def s0(t):
                nr = nrows(t)
                yt, r_yt = yts[t % NS]
                yT, r_yT = yTs[t % 2]
                pTb = pair_bf(0)
                P.op("pe", [(lambda e, c=c: e.transpose(out=pTb[:, c * 128:c * 128 + nr], in_=yt[0:nr, c * 128:(c + 1) * 128],
                                                        identity=idb[0:nr, 0:nr])) for c in range(16)],
                     reads=[r_yt, r_idb], writes=[psr[0], psr[1]])
                P.op("act", lambda e: e.activation(out=yT[:, :, 0:nr], in_=pTb.rearrange("p (a b) -> p a b", b=128)[:, :, 0:nr], func=AF.Copy),
                     reads=[psr[0], psr[1]], writes=[r_yT])
                po = pair(1)
                P.op("pe", [(lambda e, n=n, c=c: e.matmul(po[0:nr, n * 512:(n + 1) * 512], lhsT=yT[:, c, 0:nr],
                                                          rhs=Wout[:, c, n * 512:(n + 1) * 512], start=(c == 0), stop=(c == 15)))
                            for n in range(2) for c in range(16)],
                     reads=[r_yT, r_Wout], writes=[psr[2], psr[3]])

            def s1a(t):
                nr = nrows(t)
                s = t % 2
                xt, r_xt = xts[t % NS]
                ss, r_ss = sss[s]
                v1, r_v1 = v1s[s]
                rstd, r_rstd = rss[s]
                x1, r_x1 = x1s[s]
                po = pair(1)
                P.op("act", lambda e: e.activation(out=junk[0:nr], in_=po[0:nr], func=AF.Square, accum_out=ss[0:nr]),
                     reads=[psr[2], psr[3]], writes=[r_junk, r_ss])
                rstd_ops(ss, r_ss, v1, r_v1, rstd, r_rstd, nr, 1.0 / D, RMS_EPS)
                P.op("dve", lambda e: e.scalar_tensor_tensor(out=tmp[0:nr], in0=po[0:nr], scalar=rstd[0:nr], in1=gpost[0:nr],
                                                              op0=ALU.mult, op1=ALU.mult),
                     reads=[psr[2], psr[3], r_rstd, r_gpost], writes=[r_tmp])
                P.op("dve", lambda e: e.tensor_tensor(out=x1[0:nr], in0=tmp[0:nr], in1=xt[0:nr], op=ALU.add),
                     reads=[r_tmp, r_xt], writes=[r_x1])
                P.op("act", lambda e: e.activation(out=x1b[0:nr], in_=x1[0:nr], func=AF.Copy), reads=[r_x1], writes=[r_x1b])

            def s1b(t):
                nr = nrows(t)
                pt, r_pt = pts[t % NS]
                pTb = bank_bf(4)
                P.op("pe", [(lambda e, c=c: e.transpose(out=pTb[:, c * 128:c * 128 + nr], in_=x1b[0:nr, c * 128:(c + 1) * 128],
                                                        identity=idb[0:nr, 0:nr])) for c in range(8)],
                     reads=[r_x1b, r_idb], writes=[psr[4]])
                P.op("act", lambda e: e.activation(out=x1T[:, :, 0:nr], in_=pTb.rearrange("p (a b) -> p a b", b=128)[:, :, 0:nr], func=AF.Copy),
                     reads=[psr[4]], writes=[r_x1T])
                pg = pair(3)
                P.op("pe", [(lambda e, n=n, c=c: e.matmul(pg[0:nr, n * 512:(n + 1) * 512], lhsT=x1T[:, c, 0:nr],
                                                          rhs=Wg[:, c, n * 512:(n + 1) * 512], start=(c == 0), stop=(c == 7)))
                            for n in range(2) for c in range(8)],
                     reads=[r_x1T, r_Wg], writes=[psr[6], psr[7]])
                P.op("act", lambda e: e.activation(out=pb[0:nr], in_=pt[0:nr], func=AF.Copy), reads=[r_pt], writes=[r_pb])
                pPb = bank_bf(5)
                P.op("pe", [(lambda e, c=c: e.transpose(out=pPb[:, c * 128:c * 128 + nr], in_=pb[0:nr, c * 128:(c + 1) * 128],
                                                        identity=idb[0:nr, 0:nr])) for c in range(2)],
                     reads=[r_pb, r_idb], writes=[psr[5]])
                P.op("act", lambda e: e.activation(out=pT[:, :, 0:nr], in_=pPb[:, 0:256].rearrange("p (a b) -> p a b", b=128)[:, :, 0:nr], func=AF.Copy),
                     reads=[psr[5]], writes=[r_pT])

            def s2(t):
                nr = nrows(t)
                s = t % 2
                x1, r_x1 = x1s[s]
                x2, r_x2 = x2s[s]
                pg = pair(3)
                po = pair(0)
                TSUB = dbg.get('tsub', 4)
                P.op("act", lambda e: e.activation(out=sg[0:nr], in_=pg[0:nr], func=AF.Sigmoid), reads=[psr[6], psr[7]], writes=[r_sg])
                if TSUB < 2:
                    return
                P.op("pe", [(lambda e, n=n, c=c: e.matmul(po[0:nr, n * 512:(n + 1) * 512], lhsT=pT[:, c, 0:nr],
                                                          rhs=Wp[:, c, n * 512:(n + 1) * 512], start=(c == 0), stop=(c == 1)))
                            for n in range(2) for c in range(2)],
                     reads=[r_pT, r_Wp], writes=[psr[0], psr[1]])
                if TSUB < 3:
                    return
                P.op("dve", lambda e: e.tensor_tensor(out=tmp[0:nr], in0=po[0:nr], in1=sg[0:nr], op=ALU.mult),
                     reads=[r_sg, psr[0], psr[1]], writes=[r_tmp])
                P.op("dve", lambda e: e.tensor_tensor(out=x2[0:nr], in0=tmp[0:nr], in1=x1[0:nr], op=ALU.add),
                     reads=[r_tmp, r_x1], writes=[r_x2])
                if TSUB < 4:
                    return
                store([lambda e: e.dma_start(out=xdst(t), in_=x2[0:nr])], [r_x2], [XR[t]])

            TSTOP = dbg.get('tstop', 4)
            run_stages([sL, s0, s1a, s1b, s2], TILES)

        def phase_F(layer):
            jl = layer // 2
            P.new_epoch()
            A.reset(PH0)
            B = {}
            Wf, r_Wf = A.alloc("Wf", [8, FW], BF16)
            B["gpre"] = A.alloc("gpre", [D], F32)
            gpre, r_gpre = B["gpre"]
            bfb, r_bfb = A.alloc("bfb", [NH], F32)
            common_small(B)
            xts = [A.alloc(f"xt{i}", [D], F32) for i in range(2)]
            B["hb"] = A.alloc("hb", [D], BF16)
            B["hT"] = [A.alloc(f"hT{i}", [8, 128], BF16) for i in range(2)]
            qb, r_qb = A.alloc("qb", [E], BF16)
            kf, r_kf = A.alloc("kf", [E], F32)
            kb_, r_kb = A.alloc("kb", [E], BF16)
            vf, r_vf = A.alloc("vf", [E], F32)
            vb, r_vb = A.alloc("vb", [E], BF16)
            szt, r_szt = A.alloc("szt", [E], F32)
            qT, r_qT = A.alloc("qT", [16, 128], BF16)
            kT, r_kT = A.alloc("kT", [16, 128], BF16)
            t16, r_t16 = A.alloc("t16", [NH], F32)
            e16, r_e16 = A.alloc("e16", [NH], F32)
            l16, r_l16 = A.alloc("l16", [NH], F32)
            lfs_ = [A.alloc(f"lf{i}", [NH], F32) for i in range(2)]

            wload([(lambda e, c=c: e.dma_start(out=Wf[:, c, :], in_=f_w_in[jl, c * 128:(c + 1) * 128, :])) for c in range(8)], [r_Wf])
            load([lambda e: e.dma_start(out=gpre, in_=norm_pre[layer:layer + 1, :].to_broadcast([128, D])),
                  lambda e: e.dma_start(out=bfb, in_=f_b_f[jl:jl + 1, :].to_broadcast([128, NH]))], [], [r_gpre, r_bfb])

            def sL(t):
                nr = nrows(t)
                xt, r_xt = xts[t % 2]
                load([lambda e: e.dma_start(out=xt[0:nr], in_=xsrc(layer, t))], [XR[t]], [r_xt])

            def s0a(t):
                xt, r_xt = xts[t % 2]
                head_a(B, t, xt, r_xt)

            def s0b(t):
                head_b(B, t, 0)

            def s1(t):
                nr = nrows(t)
                r0 = t * 128
                hT, r_hT = B["hT"][t % 2]
                lf, r_lf = lfs_[t % 2]
                for n in range(17):
                    bk = 1 + (n % 3)
                    ncol = 512 if n < 16 else NH
                    P.op("pe", [(lambda e, c=c, n=n, bk=bk, ncol=ncol: e.matmul(bank(bk)[0:nr, 0:ncol], lhsT=hT[:, c, 0:nr],
                                                                                   rhs=Wf[:, c, n * 512:n * 512 + ncol],
                                                                                   start=(c == 0), stop=(c == 7))) for c in range(8)],
                         reads=[r_hT, r_Wf], writes=[psr[bk]])
                    c0 = (n % 4) * 512
                    if n < 4:
                        P.op("act", lambda e, c0=c0, bk=bk: e.activation(out=qb[0:nr, c0:c0 + 512], in_=bank(bk)[0:nr, :], func=AF.Copy),
                             reads=[psr[bk]], writes=[r_qb])
                    elif n < 8:
                        P.op("act", lambda e, c0=c0, bk=bk: e.activation(out=kf[0:nr, c0:c0 + 512], in_=bank(bk)[0:nr, :], func=AF.Copy),
                             reads=[psr[bk]], writes=[r_kf])
                    elif n < 12:
                        P.op("dve", lambda e, c0=c0, bk=bk: e.tensor_copy(out=vf[0:nr, c0:c0 + 512], in_=bank(bk)[0:nr, :]),
                             reads=[psr[bk]], writes=[r_vf])
                    elif n < 16:
                        P.op("act", lambda e, c0=c0, bk=bk: e.activation(out=szt[0:nr, c0:c0 + 512], in_=bank(bk)[0:nr, :], func=AF.Silu),
                             reads=[psr[bk]], writes=[r_szt])
                    else:
                        P.op("dve", lambda e, bk=bk: e.tensor_tensor(out=t16[0:nr], in0=bank(bk)[0:nr, 0:NH], in1=bfb[0:nr], op=ALU.add),
                             reads=[psr[bk], r_bfb], writes=[r_t16])
                    if n == 3:
                        pTb = pair_bf(2)
                        P.op("pe", [(lambda e, c=c: e.transpose(out=pTb[:, c * 128:c * 128 + nr], in_=qb[0:nr, c * 128:(c + 1) * 128],
                                                                identity=idb[0:nr, 0:nr])) for c in range(16)],
                             reads=[r_qb, r_idb], writes=[psr[4], psr[5]])
                        P.op("act", lambda e, pTb=pTb: e.activation(out=qT[:, :, 0:nr], in_=pTb.rearrange("p (a b) -> p a b", b=128)[:, :, 0:nr], func=AF.Copy),
                             reads=[psr[4], psr[5]], writes=[r_qT])
                        store([lambda e: e.dma_start(out=QTs[:, :, r0:r0 + nr].rearrange("h d r -> d h r"), in_=qT[:, :, 0:nr])],
                              [r_qT], [QTr[t]])
                    if n == 7:
                        if t < 32:
                            store([lambda e: e.dma_start(out=fk_p[jl, r0:r0 + nr, :], in_=kf[0:nr])], [r_kf], [])
                        else:
                            store([lambda e: e.dma_start(out=fk_s[jl, :, :], in_=kf[0:nr])], [r_kf], [])
                        P.op("pool", lambda e: e.tensor_copy(out=kb_[0:nr], in_=kf[0:nr]), reads=[r_kf], writes=[r_kb])
                        pTb2 = pair_bf(3)
                        P.op("pe", [(lambda e, c=c: e.transpose(out=pTb2[:, c * 128:c * 128 + nr], in_=kb_[0:nr, c * 128:(c + 1) * 128],
                                                                identity=idb[0:nr, 0:nr])) for c in range(16)],
                             reads=[r_kb, r_idb], writes=[psr[6], psr[7]])
                        P.op("act", lambda e, pTb2=pTb2: e.activation(out=kT[:, :, 0:nr], in_=pTb2.rearrange("p (a b) -> p a b", b=128)[:, :, 0:nr], func=AF.Copy),
                             reads=[psr[6], psr[7]], writes=[r_kT])
                        store([lambda e: e.dma_start(out=KTs[:, :, r0:r0 + nr].rearrange("h d r -> d h r"), in_=kT[:, :, 0:nr])],
                              [r_kT], [KTr[t]])
                    if n == 11:
                        if t < 32:
                            store([lambda e: e.dma_start(out=fv_p[jl, r0:r0 + nr, :], in_=vf[0:nr])], [r_vf], [])
                        else:
                            store([lambda e: e.dma_start(out=fv_s[jl, :, :], in_=vf[0:nr])], [r_vf], [])
                        P.op("pool", lambda e: e.tensor_copy(out=vb[0:nr], in_=vf[0:nr]), reads=[r_vf], writes=[r_vb])
                        store([lambda e: e.dma_start(out=Vbs[:, r0:r0 + nr, :].rearrange("h r e -> r h e"),
                                                     in_=vb[0:nr].rearrange("p (h e) -> p h e", e=DH))], [r_vb], [VBr[t]])
                    if n == 15:
                        store([lambda e: e.dma_start(out=SZs[:, r0:r0 + nr, :].rearrange("h r e -> r h e"),
                                                     in_=szt[0:nr].rearrange("p (h e) -> p h e", e=DH))], [r_szt], [SZr[t]])
                    if n == 16:
                        P.op("act", lambda e: e.activation(out=e16[0:nr], in_=t16[0:nr], func=AF.Exp, scale=-1.0), reads=[r_t16], writes=[r_e16])
                        P.op("act", lambda e: e.activation(out=l16[0:nr], in_=e16[0:nr], func=AF.Ln, bias=1.0, scale=1.0), reads=[r_e16], writes=[r_l16])
                        P.op("dve", lambda e: e.tensor_scalar(out=lf[0:nr], in0=l16[0:nr], scalar1=-1.0, scalar2=None, op0=ALU.mult),
                             reads=[r_l16], writes=[r_lf])
                        if t < 32:
                            store([lambda e: e.dma_start(out=flf_p[jl, r0:r0 + nr, :], in_=lf[0:nr])], [r_lf], [])
                        else:
                            store([lambda e: e.dma_start(out=flf_s[jl, :, :], in_=lf[0:nr])], [r_lf], [LFSr])

            def s2(t):
                if t >= 32:
                    return
                lf, r_lf = lfs_[t % 2]
                bk = 1 + (t % 3)
                P.op("pe", [lambda e: e.matmul(bank(bk)[:, 0:NH], lhsT=tstr, rhs=lf, start=True, stop=True),
                            lambda e: e.matmul(bank(bk)[:, NH:2 * NH], lhsT=ones, rhs=lf, start=True, stop=True)],
                     reads=[r_cs, r_lf], writes=[psr[bk]])
                if t == 0:
                    P.op("dve", lambda e: e.tensor_copy(out=TcTab[:, 0, :], in_=bank(bk)[:, NH:2 * NH]), reads=[psr[bk]], writes=[r_TcTab])
                else:
                    P.op("dve", lambda e: e.tensor_tensor(out=TcTab[:, t, :], in0=bank(bk)[:, NH:2 * NH], in1=TcTab[:, t - 1, :], op=ALU.add),
                         reads=[psr[bk], r_TcTab], writes=[r_TcTab])
                P.op("dve", lambda e: e.tensor_tensor(out=ETab[:, t, :], in0=bank(bk)[:, 0:NH], in1=TcTab[:, t, :], op=ALU.subtract),
                     reads=[psr[bk], r_TcTab], writes=[r_ETab])

            run_stages([sL, s0a, s0b, s1, s2], TILES)

        def phase_A(layer):
            P.new_epoch()
            A.reset(PH0)
            QTh = [A.alloc(f"QTh{i}", [SEQ], BF16) for i in range(2)]
            KTh = [A.alloc(f"KTh{i}", [SEQ], BF16) for i in range(2)]
            Vh = [A.alloc(f"Vh{i}", [32, 132], BF16) for i in range(2)]
            SZh = [A.alloc(f"SZh{i}", [32, 128], F32) for i in range(2)]
            Yh = [A.alloc(f"Yh{i}", [32, 128], BF16) for i in range(2)]
            bT = [A.alloc(f"bT{i}", [32, 8], F32) for i in range(2)]
            Pb = [A.alloc(f"Pb{i}", [512], BF16) for i in range(3)]
            rinv = [A.alloc(f"rinv{i}", [1], F32) for i in range(4)]
            for i in range(2):
                P.op("dve", lambda e, i=i: e.memset(Vh[i][0][:, :, 128:129], 1.0), writes=[Vh[i][1]])

            def load_head(h):
                s = h % 2
                load([lambda e: e.dma_start(out=QTh[s][0], in_=QTs[h, :, 0:SEQ]),
                      lambda e: e.dma_start(out=KTh[s][0], in_=KTs[h, :, 0:SEQ])],
                     QTr[0:32] + KTr[0:32], [QTh[s][1], KTh[s][1]])
                load([lambda e: e.dma_start(out=Vh[s][0][:, :, 0:128], in_=Vbs[h, 0:SEQ, :].rearrange("(t p) e -> p t e", p=128)),
                      lambda e: e.dma_start(out=SZh[s][0], in_=SZs[h, 0:SEQ, :].rearrange("(t p) e -> p t e", p=128))],
                     VBr[0:32] + SZr[0:32], [Vh[s][1], SZh[s][1]])
                for qg in range(8):
                    P.op("dve", lambda e, qg=qg: e.tensor_scalar(out=bT[s][0][:, :, qg], in0=ETab[:, :, h],
                                                                  scalar1=TcTab[:, 4 * qg + 3, h:h + 1], scalar2=None, op0=ALU.add),
                         reads=[r_ETab, r_TcTab], writes=[bT[s][1]])

            its = []
            for h in range(NH):
                for qg in range(8):
                    for kb in range(4 * qg + 4):
                        its.append((h, qg, kb))

            def psO(qg, qt):
                b = (qg % 2) * 2 + qt // 2
                return bank(b)[:, (qt % 2) * 256:(qt % 2) * 256 + 129], b

            def do_S(i):
                h, qg, kb = its[i]
                s = h % 2
                d = max(0, kb - 4 * qg)
                n = (4 - d) * 128
                q0 = (4 * qg + d) * 128
                bk = 4 + i % 3
                P.op("pe", lambda e: e.matmul(bank(bk)[:, 0:n], lhsT=KTh[s][0][:, kb * 128:(kb + 1) * 128],
                                              rhs=QTh[s][0][:, q0:q0 + n], start=True, stop=True),
                     reads=[KTh[s][1], QTh[s][1]], writes=[psr[bk]])

            def do_rest(i):
                h, qg, kb = its[i]
                s = h % 2
                d = max(0, kb - 4 * qg)
                n = (4 - d) * 128
                bk = 4 + i % 3
                pb_, r_pb = Pb[i % 3]
                P.op("act", lambda e: e.activation(out=pb_[:, 0:n], in_=bank(bk)[:, 0:n], func=AF.Exp,
                                                   bias=bT[s][0][:, kb, qg:qg + 1], scale=SCALE),
                     reads=[psr[bk], bT[s][1]], writes=[r_pb])
                if kb >= 4 * qg:
                    P.op("pool", lambda e: e.tensor_tensor(out=pb_[:, 0:128], in0=pb_[:, 0:128], in1=mTb, op=ALU.mult),
                         reads=[r_pb, r_mTb], writes=[r_pb])
                fns = []
                banks = set()
                for qt in range(d, 4):
                    o_ap, b = psO(qg, qt)
                    banks.add(b)
                    fns.append(lambda e, qt=qt, o_ap=o_ap: e.matmul(o_ap, lhsT=pb_[:, (qt - d) * 128:(qt - d + 1) * 128],
                                                                     rhs=Vh[s][0][:, kb, 0:129], start=(kb == 0 and qt % 2 == 0), stop=(kb == 4 * qg + qt),
                                                                     skip_group_check=True))
                P.op("pe", fns, reads=[r_pb, Vh[s][1]], writes=[psr[b] for b in sorted(banks)])
                if kb == 4 * qg + 3:
                    for qt in range(4):
                        o_ap, b = psO(qg, qt)
                        ri, r_ri = rinv[qt]
                        P.op("dve", lambda e, o_ap=o_ap, ri=ri: e.reciprocal(out=ri, in_=o_ap[:, 128:129]), reads=[psr[b]], writes=[r_ri])
                        P.op("dve", lambda e, o_ap=o_ap, ri=ri, qt=qt: e.scalar_tensor_tensor(
                            out=Yh[s][0][:, 4 * qg + qt, :], in0=o_ap[:, 0:128], scalar=ri, in1=SZh[s][0][:, 4 * qg + qt, :],
                            op0=ALU.mult, op1=ALU.mult), reads=[psr[b], r_ri, SZh[s][1]], writes=[Yh[s][1]])
                    if qg == 7:
                        store([lambda e: e.dma_start(out=Ysc[0:SEQ, h * 128:(h + 1) * 128].rearrange("(t p) e -> p t e", p=128), in_=Yh[s][0])],
                              [Yh[s][1]], YS[0:32])

            load_head(0)
            n_it = len(its)
            do_S(0)
            for i in range(n_it):
                h, qg, kb = its[i]
                if qg == 0 and kb == 0 and h + 1 < NH:
                    load_head(h + 1)
                if i + 1 < n_it:
                    do_S(i + 1)
                do_rest(i)

        def phase_S(layer):
            jl = layer // 2
            P.new_epoch()
            A.reset(PH0)
            qTs, r_qTs = A.alloc("qTs", [16, 16], BF16)
            kTs, r_kTs = A.alloc("kTs", [16, 16], BF16)
            vs, r_vs = A.alloc("vs", [16, 132], BF16)
            szs, r_szs = A.alloc("szs", [16, 128], F32)
            lfs, r_lfs = A.alloc("lfs", [NH], F32)
            lfc, r_lfc = A.alloc("lfc", [16, 16], F32)
            sfxc, r_sfxc = A.alloc("sfxc", [16, 16], F32)
            totc, r_totc = A.alloc("totc", [16, 16], F32)
            biasS, r_biasS = A.alloc("biasS", [16, 16], F32)
            biasN, r_biasN = A.alloc("biasN", [NH], F32)
            Rs = [A.alloc(f"R{i}", [NH], F32) for i in range(17)]
            kc = [A.alloc(f"kc{i}", [1024], F32) for i in range(2)]
            vc = [A.alloc(f"vc{i}", [1024], F32) for i in range(2)]
            kcT = [A.alloc(f"kcT{i}", [8, 128], BF16) for i in range(2)]
            Vc = [A.alloc(f"Vc{i}", [8, 132], BF16) for i in range(2)]
            tmpS, r_tmpS = A.alloc("tmpS", [128], F32)
            Ps = [A.alloc(f"Ps{i}", [128], BF16) for i in range(2)]
            rinv = [A.alloc(f"rinvs{i}", [1], F32) for i in range(2)]
            Ysb, r_Ysb = A.alloc("Ysb", [E], BF16)

            load([lambda e: e.dma_start(out=qTs, in_=QTs[:, :, SEQ:SEQ + TS].rearrange("h d r -> d h r")),
                  lambda e: e.dma_start(out=kTs, in_=KTs[:, :, SEQ:SEQ + TS].rearrange("h d r -> d h r")),
                  lambda e: e.dma_start(out=vs[0:TS, :, 0:128], in_=Vbs[:, SEQ:SEQ + TS, :].rearrange("h r e -> r h e")),
                  lambda e: e.dma_start(out=szs[0:TS], in_=SZs[:, SEQ:SEQ + TS, :].rearrange("h r e -> r h e")),
                  lambda e: e.dma_start(out=lfs[0:TS], in_=flf_s[jl, :, :]),
                  lambda e: e.dma_start(out=lfc, in_=clf[jl].rearrange("(t p) h -> p t h", p=128))],
                 [QTr[32], KTr[32], VBr[32], SZr[32], LFSr], [r_qTs, r_kTs, r_vs, r_szs, r_lfs, r_lfc])
            P.op("dve", lambda e: e.memset(vs[0:TS, :, 128:129], 1.0), writes=[r_vs])
            for i in range(2):
                P.op("dve", lambda e, i=i: e.memset(Vc[i][0][:, :, 128:129], 1.0), writes=[Vc[i][1]])
            lfc2 = lfc.rearrange("p a b -> p (a b)")
            P.op("pe", [lambda e: e.matmul(bank(6)[:, 0:256], lhsT=tstr, rhs=lfc2, start=True, stop=True),
                        lambda e: e.matmul(bank(6)[:, 256:512], lhsT=ones, rhs=lfc2, start=True, stop=True),
                        lambda e: e.matmul(bank(7)[0:TS, 0:NH], lhsT=tstr[0:TS, 0:TS], rhs=lfs[0:TS], start=True, stop=True),
                        lambda e: e.matmul(bank(7)[:, NH:2 * NH], lhsT=ones[0:TS, :], rhs=lfs[0:TS], start=True, stop=True)],
                 reads=[r_cs, r_lfc, r_lfs], writes=[psr[6], psr[7]])
            P.op("dve", lambda e: e.tensor_copy(out=sfxc.rearrange("p a b -> p (a b)"), in_=bank(6)[:, 0:256]), reads=[psr[6]], writes=[r_sfxc])
            P.op("dve", lambda e: e.tensor_copy(out=totc.rearrange("p a b -> p (a b)"), in_=bank(6)[:, 256:512]), reads=[psr[6]], writes=[r_totc])
            P.op("dve", lambda e: e.tensor_copy(out=biasN[0:TS], in_=bank(7)[0:TS, 0:NH]), reads=[psr[7]], writes=[r_biasN])
            P.op("dve", lambda e: e.tensor_copy(out=Rs[16][0], in_=bank(7)[:, NH:2 * NH]), reads=[psr[7]], writes=[Rs[16][1]])
            for kt in range(15, -1, -1):
                P.op("dve", lambda e, kt=kt: e.tensor_tensor(out=biasS[:, kt, :], in0=sfxc[:, kt, :], in1=Rs[kt + 1][0], op=ALU.add),
                     reads=[r_sfxc, Rs[kt + 1][1]], writes=[r_biasS])
                P.op("dve", lambda e, kt=kt: e.tensor_tensor(out=Rs[kt][0], in0=Rs[kt + 1][0], in1=totc[:, kt, :], op=ALU.add),
                     reads=[Rs[kt + 1][1], r_totc], writes=[Rs[kt][1]])

            def psOs(hl):
                b = hl // 3
                return bank(b)[0:TS, (hl % 3) * 160:(hl % 3) * 160 + 129], b

            def s_iter(hh, kt):
                if True:
                    s = kt % 2
                    if kt < 16:
                        load([lambda e, kt=kt, s=s: e.dma_start(out=kc[s][0], in_=ck[jl, kt * 128:(kt + 1) * 128, hh * 1024:(hh + 1) * 1024]),
                              lambda e, kt=kt, s=s: e.dma_start(out=vc[s][0], in_=cv[jl, kt * 128:(kt + 1) * 128, hh * 1024:(hh + 1) * 1024])],
                             [], [kc[s][1], vc[s][1]])
                        pk = pair(2)
                        P.op("pe", [(lambda e, hl=hl, s=s: e.transpose(out=pk[:, hl * 128:(hl + 1) * 128], in_=kc[s][0][:, hl * 128:(hl + 1) * 128], identity=ident))
                                    for hl in range(8)], reads=[kc[s][1], r_cs], writes=[psr[4], psr[5]])
                        P.op("act", lambda e, s=s: e.activation(out=kcT[s][0].rearrange("p a b -> p (a b)"), in_=pk, func=AF.Copy),
                             reads=[psr[4], psr[5]], writes=[kcT[s][1]])
                        P.op("pool", lambda e, s=s: e.tensor_copy(out=Vc[s][0][:, :, 0:128], in_=vc[s][0].rearrange("p (h e) -> p h e", e=128)),
                             reads=[vc[s][1]], writes=[Vc[s][1]])
                        P.op("pe", [(lambda e, hl=hl, s=s: e.matmul(bank(6)[:, hl * 16:(hl + 1) * 16], lhsT=kcT[s][0][:, hl, :],
                                                                    rhs=qTs[:, hh * 8 + hl, :], start=True, stop=True)) for hl in range(8)],
                             reads=[kcT[s][1], r_qTs], writes=[psr[6]])
                        P.op("dve", lambda e, kt=kt: e.scalar_tensor_tensor(
                            out=tmpS.rearrange("p (h q) -> p h q", q=16), in0=bank(6)[:, 0:128].rearrange("p (h q) -> p h q", q=16), scalar=SCALE,
                            in1=biasS[:, kt, hh * 8:(hh + 1) * 8].unsqueeze(2).to_broadcast([128, 8, 16]), op0=ALU.mult, op1=ALU.add),
                            reads=[psr[6], r_biasS], writes=[r_tmpS])
                        P.op("act", lambda e, s=s: e.activation(out=Ps[s][0], in_=tmpS, func=AF.Exp), reads=[r_tmpS], writes=[Ps[s][1]])
                        fns = []
                        banks = set()
                        for hl in range(8):
                            o_ap, b = psOs(hl)
                            banks.add(b)
                            fns.append(lambda e, hl=hl, o_ap=o_ap, s=s, kt=kt: e.matmul(o_ap, lhsT=Ps[s][0][:, hl * 16:(hl + 1) * 16],
                                                                                       rhs=Vc[s][0][:, hl, 0:129], start=(kt == 0 and hl % 3 == 0), stop=False,
                                                                                       skip_group_check=True))
                        P.op("pe", fns, reads=[Ps[s][1], Vc[s][1]], writes=[psr[b] for b in sorted(banks)])
                    else:
                        P.op("pe", [(lambda e, hl=hl: e.matmul(bank(6)[0:TS, hl * 16:(hl + 1) * 16], lhsT=kTs[:, hh * 8 + hl, :],
                                                               rhs=qTs[:, hh * 8 + hl, :], start=True, stop=True)) for hl in range(8)],
                             reads=[r_kTs, r_qTs], writes=[psr[6]])
                        P.op("dve", lambda e: e.scalar_tensor_tensor(
                            out=tmpS[0:TS].rearrange("p (h q) -> p h q", q=16), in0=bank(6)[0:TS, 0:128].rearrange("p (h q) -> p h q", q=16), scalar=SCALE,
                            in1=biasN[0:TS, hh * 8:(hh + 1) * 8].unsqueeze(2).to_broadcast([TS, 8, 16]), op0=ALU.mult, op1=ALU.add),
                            reads=[psr[6], r_biasN], writes=[r_tmpS])
                        P.op("act", lambda e, s=s: e.activation(out=Ps[s][0][0:TS], in_=tmpS[0:TS], func=AF.Exp), reads=[r_tmpS], writes=[Ps[s][1]])
                        P.op("pool", lambda e, s=s: e.tensor_tensor(
                            out=Ps[s][0][0:TS].rearrange("p (h q) -> p h q", q=16), in0=Ps[s][0][0:TS].rearrange("p (h q) -> p h q", q=16),
                            in1=mTb[0:TS, 0:TS].unsqueeze(1).to_broadcast([TS, 8, 16]), op=ALU.mult),
                            reads=[Ps[s][1], r_mTb], writes=[Ps[s][1]])
                        fns = []
                        banks = set()
                        for hl in range(8):
                            o_ap, b = psOs(hl)
                            banks.add(b)
                            fns.append(lambda e, hl=hl, o_ap=o_ap, s=s: e.matmul(o_ap, lhsT=Ps[s][0][0:TS, hl * 16:(hl + 1) * 16],
                                                                                rhs=vs[0:TS, hh * 8 + hl, 0:129], start=False, stop=True, skip_group_check=True))
                        P.op("pe", fns, reads=[Ps[s][1], r_vs], writes=[psr[b] for b in sorted(banks)])

            for hh in range(2):
                for kt in range(17):
                    s_iter(hh, kt)
                for hl in range(8):
                    o_ap, b = psOs(hl)
                    ri, r_ri = rinv[hl % 2]
                    hg = hh * 8 + hl
                    P.op("dve", lambda e, o_ap=o_ap, ri=ri: e.reciprocal(out=ri[0:TS], in_=o_ap[:, 128:129]), reads=[psr[b]], writes=[r_ri])
                    P.op("dve", lambda e, o_ap=o_ap, ri=ri, hg=hg: e.scalar_tensor_tensor(
                        out=Ysb[0:TS, hg * 128:(hg + 1) * 128], in0=o_ap[:, 0:128], scalar=ri[0:TS], in1=szs[0:TS, hg, :],
                        op0=ALU.mult, op1=ALU.mult), reads=[psr[b], r_ri, r_szs], writes=[r_Ysb])
            store([lambda e: e.dma_start(out=Ysc[SEQ:SEQ + TS, :], in_=Ysb[0:TS])], [r_Ysb], [YS[32]])

        for layer in range(n_layers):
            jl = layer // 2
            if layer % 2 == 0:
                if 'G' in PHASES:
                    phase_G(layer)
                if 'T' in PHASES:
                    phase_T(layer, g_w_out[jl])
            else:
                if 'F' in PHASES:
                    phase_F(layer)
                if 'A' in PHASES:
                    phase_A(layer)
                if 'S' in PHASES:
                    phase_S(layer)
                if 'T' in PHASES:
                    phase_T(layer, f_w_out[jl])

        P.emit(final_ops=stores)
    return nc


def _consts():
    c = np.zeros((128, 640), np.float32)
    i = np.arange(128)
    c[:, 0:128] = np.eye(128, dtype=np.float32)
    c[:, 128:256] = (i[:, None] > i[None, :])
    c[:, 256:384] = 1.0
    c[:, 384:512] = (i[:, None] <= i[None, :])
    c[:, 512:640] = ((i[:, None] // 64) <= (i[None, :] // 64))
    return c


_NC_CACHE = {}


def kernel(x_prompt, x_sample, cache_fox_k, cache_fox_v, cache_fox_logf, p_prompt, p_sample,
           norm_pre, norm_post, gmlp_w_in, gmlp_ln_g, gmlp_ln_b, gmlp_w_s, gmlp_b_s, gmlp_w_out,
           fox_w_in, fox_b_f, fox_w_out, ple_w_proj, ple_w_gate, _n_layers=4, _cores=8):
    f = lambda a: np.ascontiguousarray(np.asarray(a), dtype=np.float32)
    n = _cores
    if _n_layers not in _NC_CACHE:
        _NC_CACHE[_n_layers] = build(_n_layers)
    nc = _NC_CACHE[_n_layers]
    shared = {
        "norm_pre": f(norm_pre), "norm_post": f(norm_post),
        "g_w_in": f(gmlp_w_in), "g_ln_g": f(gmlp_ln_g), "g_ln_b": f(gmlp_ln_b), "g_w_s": f(gmlp_w_s),
        "g_b_s": f(gmlp_b_s), "g_w_out": f(gmlp_w_out), "f_w_in": f(fox_w_in), "f_b_f": f(fox_b_f),
        "f_w_out": f(fox_w_out), "ple_proj": f(ple_w_proj), "ple_gate": f(ple_w_gate), "consts": _consts(),
    }
    xp = np.asarray(x_prompt); xs = np.asarray(x_sample)
    ckk = np.asarray(cache_fox_k); cvv = np.asarray(cache_fox_v); clff = np.asarray(cache_fox_logf)
    pp = np.asarray(p_prompt); psm = np.asarray(p_sample)
    in_maps = []
    for b in range(n):
        m = dict(shared)
        m["x_p"] = f(xp[b]); m["x_s"] = f(xs[b])
        m["ck"] = f(ckk[:, b].reshape(2, PAST, E)); m["cv"] = f(cvv[:, b].reshape(2, PAST, E)); m["clf"] = f(clff[:, b])
        m["p_p"] = f(pp[:, b]); m["p_s"] = f(psm[:, b])
        in_maps.append(m)
    res = run_bass_kernel_spmd(nc, in_maps, core_ids=list(range(n)))
    R = res.results
    st = lambda k, ax: np.stack([np.asarray(R[b][k], dtype=np.float32) for b in range(n)], axis=ax)
    y_prompt = st("y_p", 0)
    y_sample = st("y_s", 0)
    gv = st("gv_s", 1)
    fk_p = st("fk_p", 1).reshape(2, n, SEQ, NH, DH)
    fv_p = st("fv_p", 1).reshape(2, n, SEQ, NH, DH)
    flf_p = st("flf_p", 1)
    fk_s = st("fk_s", 1).reshape(2, n, TS, NH, DH)
    fv_s = st("fv_s", 1).reshape(2, n, TS, NH, DH)
    flf_s = st("flf_s", 1)
    return (y_prompt, y_sample, gv, fk_p, fv_p, flf_p, fk_s, fv_s, flf_s)
```

```python
import contextlib
import numpy as np
import concourse.bass as bass
import concourse.mybir as mybir
from concourse.bass_utils import run_bass_kernel_spmd

F32 = mybir.dt.float32
BF16 = mybir.dt.bfloat16
AF = mybir.ActivationFunctionType
ALU = mybir.AluOpType

D = 1024
E = 2048
SEQ = 4096
TS = 16
PAST = 2048
NH = 16
DH = 128
PLE = 256
NT = 33
TOK = 4224
SCALE = float(DH) ** -0.5
RMS_EPS = 1e-6
LN_EPS = 1e-5
FW = 4 * E + NH


class Sem:
    def __init__(self, h):
        self.h = h
        self.count = 0
        self.last_op = None


class Op:
    def __init__(self, eng, fns, is_dma):
        self.eng = eng
        self.fns = fns
        self.deps = []
        self.needs_inc = False
        self.sem = None
        self.ticket = None
        self.is_dma = is_dma


class Res:
    def __init__(self, name, arena=None, lo=0, hi=0):
        self.name = name
        self.arena = arena
        self.lo = lo
        self.hi = hi
        self.writers = []
        self.readers = []
        self.overlaps = []


class Prog:
    ENGS = ("sync", "act", "dve", "pool", "pe")
    BLK = {"sync": "sync", "act": "scalar", "dve": "vector", "pool": "gpsimd", "pe": "tensor"}

    def __init__(self, nc, stack):
        self.nc = nc
        self.stack = stack
        self.ops = {e: [] for e in self.ENGS}
        self.eng_sem = {}
        self.pools = {}
        self.pool_idx = {}
        self.arena_res = {}
        self.nsem = 0
        self.new_epoch()

    def new_sem(self, name):
        self.nsem += 1
        return Sem(self.stack.enter_context(self.nc.semaphore(f"{name}_{self.nsem}")))

    def new_epoch(self):
        for e in ("act", "dve", "pool", "pe"):
            self.eng_sem[e] = self.new_sem("e_" + e)

    def dma_pool(self, name, n):
        self.pools[name] = [self.new_sem("d_" + name) for _ in range(n)]
        self.pool_idx[name] = 0

    def res(self, name, arena=None, lo=0, hi=0):
        r = Res(name, arena, lo, hi)
        if arena is not None:
            lst = self.arena_res.setdefault(arena, [])
            for o in lst:
                if o.lo < hi and lo < o.hi:
                    r.overlaps.append(o)
                    o.overlaps.append(r)
            lst.append(r)
        return r

    def _dep(self, op, prod, raw):
        if prod is op:
            return
        if (not prod.is_dma) and (not op.is_dma) and prod.eng == op.eng:
            if not raw:
                return
            if op.eng == "pe":
                return
        prod.needs_inc = True
        op.deps.append(prod)

    def op(self, eng, fns, reads=(), writes=(), pool=None):
        if not isinstance(fns, (list, tuple)):
            fns = [fns]
        is_dma = pool is not None
        o = Op(eng, list(fns), is_dma)
        if is_dma:
            ps = self.pools[pool]
            i = self.pool_idx[pool]
            self.pool_idx[pool] = (i + 1) % len(ps)
            o.sem = ps[i]
            if o.sem.last_op is not None:
                o.sem.last_op.needs_inc = True
                o.deps.append(o.sem.last_op)
            o.sem.last_op = o
        else:
            o.sem = self.eng_sem[eng]
        for r in reads:
            for rr in [r] + r.overlaps:
                for w in rr.writers:
                    self._dep(o, w, True)
        for r in writes:
            for rr in [r] + r.overlaps:
                for w in rr.writers:
                    self._dep(o, w, False)
                for w in rr.readers:
                    self._dep(o, w, False)
        for r in reads:
            r.readers.append(o)
        for r in writes:
            r.writers = [o]
            r.readers = []
        self.ops[eng].append(o)
        return o

    def emit(self, final_ops=()):
        nc = self.nc
        for o in final_ops:
            o.needs_inc = True
        for e in self.ENGS:
            for o in self.ops[e]:
                if o.is_dma:
                    o.sem.count += 16 * len(o.fns)
                    o.ticket = o.sem.count
                elif o.needs_inc:
                    o.sem.count += 1
                    o.ticket = o.sem.count

        import os as _os
        if _os.environ.get("KCHECK"):
            self.check()

        def need_of(deps):
            need = {}
            for d in deps:
                k = id(d.sem)
                if d.ticket > need.get(k, (None, 0))[1]:
                    need[k] = (d.sem, d.ticket)
            return need

        with nc.Block() as block:
            for e in self.ENGS:
                ops = self.ops[e]
                if not ops and e != "sync":
                    continue

                def body(eng, ops=ops, e=e):
                    waited = {}
                    for o in ops:
                        for k, (s, v) in need_of(o.deps).items():
                            if waited.get(k, 0) < v:
                                eng.wait_ge(s.h, v)
                                waited[k] = v
                        n = len(o.fns)
                        for i, f in enumerate(o.fns):
                            ins = f(eng)
                            if o.is_dma:
                                ins.then_inc(o.sem.h, 16)
                            elif o.needs_inc and i == n - 1:
                                ins.then_inc(o.sem.h, 1)
                    if e == "sync":
                        for k, (s, v) in need_of(final_ops).items():
                            if waited.get(k, 0) < v:
                                eng.wait_ge(s.h, v)
                                waited[k] = v

                getattr(block, self.BLK[e])(body)


def _prog_check(self):
    pos = {e: 0 for e in self.ENGS}
    done = set()
    semval = {}
    total = sum(len(v) for v in self.ops.values())
    ndone = 0
    progress = True
    while progress:
        progress = False
        for e in self.ENGS:
            while pos[e] < len(self.ops[e]):
                o = self.ops[e][pos[e]]
                ok = True
                for d in o.deps:
                    if d.ticket is None:
                        raise RuntimeError("dep without ticket")
                    if semval.get(id(d.sem), 0) < d.ticket:
                        ok = False
                        break
                if not ok:
                    break
                if o.ticket is not None:
                    prev = semval.get(id(o.sem), 0)
                    exp = o.ticket - (16 * len(o.fns) if o.is_dma else 1)
                    if prev != exp:
                        raise RuntimeError(f"ticket order violation on {e}: prev={prev} exp={exp}")
                    semval[id(o.sem)] = o.ticket
                pos[e] += 1
                ndone += 1
                progress = True
    print("CHECK: done", ndone, "of", total, {e: (pos[e], len(self.ops[e])) for e in self.ENGS})
    if ndone != total:
        raise RuntimeError("DEADLOCK in abstract simulation")


Prog.check = _prog_check


class Arena:
    def __init__(self, P, name, ap2d_bf16, nbytes):
        self.P = P
        self.name = name
        self.ap = ap2d_bf16
        self.nbytes = nbytes
        self.off = 0

    def reset(self, off):
        self.off = off

    def alloc(self, name, free_shape, dtype):
        esz = 4 if dtype == F32 else 2
        n = int(np.prod(free_shape))
        nb = n * esz
        lo = (self.off + 63) // 64 * 64
        hi = lo + nb
        assert hi <= self.nbytes, f"arena overflow {name}: {hi} > {self.nbytes}"
        self.off = hi
        v = self.ap[:, lo // 2:hi // 2]
        if dtype == F32:
            v = v.bitcast(F32)
        if len(free_shape) == 2:
            v = v.rearrange("p (a b) -> p a b", b=free_shape[1])
        elif len(free_shape) == 3:
            v = v.rearrange("p (a b c) -> p a b c", b=free_shape[1], c=free_shape[2])
        r = self.P.res(name, self.name, lo, hi)
        return v, r


def run_stages(stages, tiles):
    K = len(stages)
    n = len(tiles)
    for step in range(n + K - 1):
        for k in range(K - 1, -1, -1):
            i = step - k
            if 0 <= i < n:
                stages[k](tiles[i])


def nrows(t):
    return 128 if t < 32 else TS


ARENA_BYTES = 207 * 1024


def build(n_layers=4, dbg=None):
    dbg = dbg or {}
    TILES = dbg.get('tiles', list(range(NT)))
    PHASES = dbg.get('phases', 'GTFAS')
    nc = bass.Bass("TRN2", target_bir_lowering=False)

    def din(name, shape, dt=F32):
        return nc.dram_tensor(name, shape, dt, kind="ExternalInput").ap()

    def dout(name, shape, dt=F32):
        return nc.dram_tensor(name, shape, dt, kind="ExternalOutput").ap()

    def dscr(name, shape, dt):
        return nc.dram_tensor(name, shape, dt, kind="Internal").ap()

    x_p = din("x_p", [SEQ, D]); x_s = din("x_s", [TS, D])
    ck = din("ck", [2, PAST, E]); cv = din("cv", [2, PAST, E]); clf = din("clf", [2, PAST, NH])
    p_p = din("p_p", [4, SEQ, PLE]); p_s = din("p_s", [4, TS, PLE])
    norm_pre = din("norm_pre", [4, D]); norm_post = din("norm_post", [4, D])
    g_w_in = din("g_w_in", [2, D, 3 * E]); g_ln_g = din("g_ln_g", [2, E]); g_ln_b = din("g_ln_b", [2, E])
    g_w_s = din("g_w_s", [2, 16, 128, 128]); g_b_s = din("g_b_s", [2, 16, 128]); g_w_out = din("g_w_out", [2, E, D])
    f_w_in = din("f_w_in", [2, D, FW]); f_b_f = din("f_b_f", [2, NH]); f_w_out = din("f_w_out", [2, E, D])
    ple_proj = din("ple_proj", [4, PLE, D]); ple_gate = din("ple_gate", [4, D, D])
    consts = din("consts", [128, 640])

    y_p = dout("y_p", [SEQ, D]); y_s = dout("y_s", [TS, D]); gv_s = dout("gv_s", [2, TS, E])
    fk_p = dout("fk_p", [2, SEQ, E]); fv_p = dout("fv_p", [2, SEQ, E]); flf_p = dout("flf_p", [2, SEQ, NH])
    fk_s = dout("fk_s", [2, TS, E]); fv_s = dout("fv_s", [2, TS, E]); flf_s = dout("flf_s", [2, TS, NH])

    Ysc = dscr("Ysc", [TOK, E], BF16)
    QTs = dscr("QTs", [NH, DH, TOK], BF16)
    KTs = dscr("KTs", [NH, DH, TOK], BF16)
    Vbs = dscr("Vbs", [NH, TOK, DH], BF16)
    SZs = dscr("SZs", [NH, TOK, DH], F32)

    with contextlib.ExitStack() as st:
        P = Prog(nc, st)
        P.dma_pool("ld", 6)
        P.dma_pool("st", 8)
        P.dma_pool("w", 4)
        ar_t = st.enter_context(nc.sbuf_tensor("arena", [128, ARENA_BYTES // 2], BF16))
        A = Arena(P, "sb", ar_t[:], ARENA_BYTES)
        ps_t = [st.enter_context(nc.psum_tensor(f"ps{i}", [128, 1024], F32)) for i in range(4)]
        psr = [P.res(f"bank{i}", "psum", i, i + 1) for i in range(8)]

        def bank(i):
            return ps_t[i // 2][:, (i % 2) * 512:(i % 2 + 1) * 512]

        def bank_bf(i):
            return ps_t[i // 2][:].bitcast(BF16)[:, (i % 2) * 1024:(i % 2 + 1) * 1024]

        def pair(i):
            return ps_t[i][:]

        def pair_bf(i):
            return ps_t[i][:].bitcast(BF16)

        XR = [P.res(f"xr{t}") for t in range(NT)]
        YS = [P.res(f"ysc{t}") for t in range(NT)]
        QTr = [P.res(f"qts{t}") for t in range(NT)]
        KTr = [P.res(f"kts{t}") for t in range(NT)]
        VBr = [P.res(f"vbs{t}") for t in range(NT)]
        SZr = [P.res(f"szs{t}") for t in range(NT)]
        LFSr = P.res("flf_s")
        OUTr = P.res("outs")
        stores = []

        def store(fns, reads, writes):
            o = P.op("sync", fns, reads=reads, writes=writes, pool="st")
            stores.append(o)
            return o

        def load(fns, reads, writes):
            return P.op("sync", fns, reads=reads, writes=writes, pool="ld")

        def wload(fns, writes):
            return P.op("pool", fns, reads=[], writes=writes, pool="w")

        def xsrc(layer, t):
            nr = nrows(t)
            if layer == 0:
                return x_p[t * 128:(t + 1) * 128, :] if t < 32 else x_s[0:nr, :]
            return y_p[t * 128:(t + 1) * 128, :] if t < 32 else y_s[0:nr, :]

        def xdst(t):
            return y_p[t * 128:(t + 1) * 128, :] if t < 32 else y_s[0:TS, :]

        def psrc(layer, t):
            return p_p[layer, t * 128:(t + 1) * 128, :] if t < 32 else p_s[layer, 0:TS, :]

        cs, r_cs = A.alloc("cs", [640], F32)
        idb, r_idb = A.alloc("idb", [128], BF16)
        mTb, r_mTb = A.alloc("mTb", [128], BF16)
        nh, r_nh = A.alloc("nh", [1], F32)
        ETab, r_ETab = A.alloc("ETab", [32, 16], F32)
        TcTab, r_TcTab = A.alloc("TcTab", [32, 16], F32)
        ident = cs[:, 0:128]
        tstr = cs[:, 128:256]
        ones = cs[:, 256:384]
        maskT = cs[:, 384:512]
        maskC = cs[:, 512:640]
        PH0 = A.off

        load([lambda e: e.dma_start(out=cs, in_=consts[:, :])], [], [r_cs])
        P.op("dve", lambda e: e.tensor_copy(out=idb, in_=ident), reads=[r_cs], writes=[r_idb])
        P.op("dve", lambda e: e.tensor_copy(out=mTb, in_=maskT), reads=[r_cs], writes=[r_mTb])
        P.op("pool", lambda e: e.memset(nh, -0.5), writes=[r_nh])

        def rstd_ops(ss, r_ss, v1, r_v1, rstd, r_rstd, nr, mul, eps):
            P.op("pool", lambda e: e.tensor_scalar(out=v1[0:nr], in0=ss[0:nr], scalar1=mul, scalar2=eps,
                                                    op0=ALU.mult, op1=ALU.add), reads=[r_ss], writes=[r_v1])
            P.op("pool", lambda e: e.tensor_tensor(out=rstd[0:nr], in0=v1[0:nr], in1=nh[0:nr], op=ALU.pow),
                 reads=[r_v1, r_nh], writes=[r_rstd])

        def head_a(B, t, xt, r_xt):
            nr = nrows(t)
            s = t % 2
            junk, r_junk = B["junk"]
            ss, r_ss = B["ss"][s]
            v1, r_v1 = B["v1"][s]
            rstd, r_rstd = B["rstd"][s]
            hb, r_hb = B["hb"]
            hT, r_hT = B["hT"][s]
            gpre, r_gpre = B["gpre"]
            P.op("act", lambda e: e.activation(out=junk[0:nr], in_=xt[0:nr], func=AF.Square, accum_out=ss[0:nr]),
                 reads=[r_xt], writes=[r_junk, r_ss])
            rstd_ops(ss, r_ss, v1, r_v1, rstd, r_rstd, nr, 1.0 / D, RMS_EPS)
            P.op("dve", lambda e: e.scalar_tensor_tensor(out=hb[0:nr], in0=xt[0:nr], scalar=rstd[0:nr], in1=gpre[0:nr],
                                                          op0=ALU.mult, op1=ALU.mult),
                 reads=[r_xt, r_rstd, r_gpre], writes=[r_hb])

        def head_b(B, t, tb):
            nr = nrows(t)
            s = t % 2
            hb, r_hb = B["hb"]
            hT, r_hT = B["hT"][s]
            pT = bank_bf(tb)
            P.op("pe", [(lambda e, c=c: e.transpose(out=pT[:, c * 128:c * 128 + nr], in_=hb[0:nr, c * 128:(c + 1) * 128],
                                                    identity=idb[0:nr, 0:nr])) for c in range(8)],
                 reads=[r_hb, r_idb], writes=[psr[tb]])
            P.op("act", lambda e: e.activation(out=hT[:, :, 0:nr], in_=pT.rearrange("p (a b) -> p a b", b=128)[:, :, 0:nr],
                                               func=AF.Copy), reads=[psr[tb]], writes=[r_hT])
            return hT, r_hT

        def common_small(B):
            B["junk"] = A.alloc("junk", [1024], BF16)
            for k in ("ss", "v1", "rstd"):
                B[k] = [A.alloc(f"{k}{i}", [1], F32) for i in range(2)]

        def phase_G(layer):
            jl = layer // 2
            P.new_epoch()
            A.reset(PH0)
            B = {}
            Win, r_Win = A.alloc("Win", [8, 3 * E], BF16)
            wsT, r_wsT = A.alloc("wsT", [16, 128], BF16)
            lng, r_lng = A.alloc("lng", [E], F32)
            lnb, r_lnb = A.alloc("lnb", [E], F32)
            bsT, r_bsT = A.alloc("bsT", [16], F32)
            B["gpre"] = A.alloc("gpre", [D], F32)
            gpre, r_gpre = B["gpre"]
            common_small(B)
            xts = [A.alloc(f"xt{i}", [D], F32) for i in range(2)]
            B["hb"] = A.alloc("hb", [D], BF16)
            B["hT"] = [A.alloc(f"hT{i}", [8, 128], BF16) for i in range(2)]
            u, r_u = A.alloc("u", [E], F32)
            vg, r_vg = A.alloc("vg", [E], F32)
            sz, r_sz = A.alloc("sz", [E], F32)
            vb, r_vb = A.alloc("vb", [E], BF16)
            ys = [A.alloc(f"y{i}", [E], BF16) for i in range(2)]
            stats, r_stats = A.alloc("stats", [4, 6], F32)
            mv, r_mv = A.alloc("mv", [2], F32)
            v2, r_v2 = A.alloc("v2", [1], F32)
            rs2, r_rs2 = A.alloc("rs2", [1], F32)

            wload([(lambda e, c=c: e.dma_start(out=Win[:, c, :], in_=g_w_in[jl, c * 128:(c + 1) * 128, :])) for c in range(8)],
                  [r_Win])
            load([lambda e: e.dma_start(out=lng, in_=g_ln_g[jl:jl + 1, :].to_broadcast([128, E])),
                  lambda e: e.dma_start(out=lnb, in_=g_ln_b[jl:jl + 1, :].to_broadcast([128, E])),
                  lambda e: e.dma_start(out=gpre, in_=norm_pre[layer:layer + 1, :].to_broadcast([128, D])),
                  lambda e: e.dma_start(out=bsT, in_=g_b_s[jl].rearrange("g i -> i g"), allow_slow_non_contiguous=True)],
                 [], [r_lng, r_lnb, r_gpre, r_bsT])
            wst = u.rearrange("p (g j) -> p g j", j=128)
            load([lambda e: e.dma_start(out=wst, in_=g_w_s[jl].rearrange("g i j -> i g j"))], [], [r_u])
            for hf in range(2):
                pp_ = pair(hf)
                P.op("pe", [(lambda e, g=g, pp_=pp_: e.transpose(out=pp_[:, (g % 8) * 128:(g % 8 + 1) * 128], in_=wst[:, g, :], identity=ident))
                            for g in range(hf * 8, hf * 8 + 8)], reads=[r_u, r_cs], writes=[psr[2 * hf], psr[2 * hf + 1]])
                P.op("dve", lambda e, hf=hf, pp_=pp_: e.tensor_tensor(
                    out=wsT[:, hf * 8:(hf + 1) * 8, :], in0=pp_.rearrange("p (g i) -> p g i", i=128),
                    in1=maskC.unsqueeze(1).to_broadcast([128, 8, 128]), op=ALU.mult),
                    reads=[psr[2 * hf], psr[2 * hf + 1], r_cs], writes=[r_wsT])

            def sL(t):
                nr = nrows(t)
                xt, r_xt = xts[t % 2]
                load([lambda e: e.dma_start(out=xt[0:nr], in_=xsrc(layer, t))], [XR[t]], [r_xt])

            def s0a(t):
                xt, r_xt = xts[t % 2]
                head_a(B, t, xt, r_xt)

            def s0b(t):
                head_b(B, t, 0)

            def s1(t):
                nr = nrows(t)
                hT, r_hT = B["hT"][t % 2]
                for n in range(12):
                    bk = 1 + (n % 3)
                    P.op("pe", [(lambda e, c=c, n=n, bk=bk: e.matmul(bank(bk)[0:nr, :], lhsT=hT[:, c, 0:nr],
                                                                      rhs=Win[:, c, n * 512:(n + 1) * 512],
                                                                      start=(c == 0), stop=(c == 7))) for c in range(8)],
                         reads=[r_hT, r_Win], writes=[psr[bk]])
                    if n < 4:
                        dst, r_dst, fn = u, r_u, AF.Gelu_apprx_tanh
                    elif n < 8:
                        dst, r_dst, fn = vg, r_vg, AF.Gelu_apprx_tanh
                    else:
                        dst, r_dst, fn = sz, r_sz, AF.Silu
                    c0 = (n % 4) * 512
                    P.op("act", lambda e, dst=dst, fn=fn, c0=c0, bk=bk: e.activation(out=dst[0:nr, c0:c0 + 512], in_=bank(bk)[0:nr, :], func=fn),
                         reads=[psr[bk]], writes=[r_dst])
                    if n == 7:
                        P.op("dve", [(lambda e, q=q: e.bn_stats(out=stats[0:nr, q, :], in_=vg[0:nr, q * 512:(q + 1) * 512])) for q in range(4)],
                             reads=[r_vg], writes=[r_stats])
                        P.op("dve", lambda e: e.bn_aggr(out=mv[0:nr], in_=stats[0:nr].rearrange("p a b -> p (a b)")),
                             reads=[r_stats], writes=[r_mv])
                        P.op("pool", lambda e: e.tensor_scalar(out=v2[0:nr], in0=mv[0:nr, 1:2], scalar1=1.0, scalar2=LN_EPS,
                                                                op0=ALU.mult, op1=ALU.add), reads=[r_mv], writes=[r_v2])
                        P.op("pool", lambda e: e.tensor_tensor(out=rs2[0:nr], in0=v2[0:nr], in1=nh[0:nr], op=ALU.pow),
                             reads=[r_v2, r_nh], writes=[r_rs2])
                        P.op("dve", lambda e: e.tensor_scalar(out=vg[0:nr], in0=vg[0:nr], scalar1=mv[0:nr, 0:1], scalar2=rs2[0:nr],
                                                               op0=ALU.subtract, op1=ALU.mult), reads=[r_vg, r_mv, r_rs2], writes=[r_vg])
                        P.op("pool", lambda e: e.tensor_tensor(out=vg[0:nr], in0=vg[0:nr], in1=lng[0:nr], op=ALU.mult),
                             reads=[r_vg, r_lng], writes=[r_vg])
                        if t < 32:
                            P.op("dve", lambda e: e.tensor_tensor(out=vb[0:nr], in0=vg[0:nr], in1=lnb[0:nr], op=ALU.add),
                                 reads=[r_vg, r_lnb], writes=[r_vb])
                        else:
                            P.op("dve", lambda e: e.tensor_tensor(out=vg[0:nr], in0=vg[0:nr], in1=lnb[0:nr], op=ALU.add),
                                 reads=[r_vg, r_lnb], writes=[r_vg])
                            P.op("dve", lambda e: e.tensor_copy(out=vb[0:nr], in_=vg[0:nr]), reads=[r_vg], writes=[r_vb])
                            store([lambda e: e.dma_start(out=gv_s[jl, :, :], in_=vg[0:nr])], [r_vg], [])

            def s2(t):
                nr = nrows(t)
                y, r_y = ys[t % 2]
                P.op("pe", [(lambda e, g=g: e.matmul(bank(4 + g // 4)[0:nr, (g % 4) * 128:(g % 4 + 1) * 128],
                                                     lhsT=wsT[0:nr, g, 0:nr], rhs=vb[0:nr, g * 128:(g + 1) * 128],
                                                     start=True, stop=True)) for g in range(16)],
                     reads=[r_wsT, r_vb], writes=[psr[4], psr[5], psr[6], psr[7]])
                P.op("dve", [(lambda e, g=g: e.scalar_tensor_tensor(
                    out=u[0:nr, g * 128:(g + 1) * 128], in0=bank(4 + g // 4)[0:nr, (g % 4) * 128:(g % 4 + 1) * 128],
                    scalar=bsT[0:nr, g:g + 1], in1=u[0:nr, g * 128:(g + 1) * 128], op0=ALU.add, op1=ALU.mult)) for g in range(16)],
                    reads=[psr[4], psr[5], psr[6], psr[7], r_bsT, r_u], writes=[r_u])
                P.op("dve", lambda e: e.tensor_tensor(out=y[0:nr], in0=u[0:nr], in1=sz[0:nr], op=ALU.mult),
                     reads=[r_u, r_sz], writes=[r_y])
                store([lambda e: e.dma_start(out=Ysc[t * 128:t * 128 + nr, :], in_=y[0:nr])], [r_y], [YS[t]])

            run_stages([sL, s0a, s0b, s1, s2], TILES)

        def phase_T(layer, w_out_ap):
            P.new_epoch()
            A.reset(PH0)
            Wout, r_Wout = A.alloc("Wout", [16, D], BF16)
            Wg, r_Wg = A.alloc("Wg", [8, D], BF16)
            Wp, r_Wp = A.alloc("Wp", [2, D], BF16)
            gpost, r_gpost = A.alloc("gpost", [D], F32)
            junk, r_junk = A.alloc("junk", [D], BF16)
            NS = 4
            xts = [A.alloc(f"xt{i}", [D], F32) for i in range(NS)]
            yts = [A.alloc(f"yt{i}", [E], BF16) for i in range(NS)]
            pts = [A.alloc(f"pt{i}", [PLE], F32) for i in range(NS)]
            yTs = [A.alloc(f"yT{i}", [16, 128], BF16) for i in range(2)]
            sss = [A.alloc(f"ss{i}", [1], F32) for i in range(2)]
            v1s = [A.alloc(f"v1{i}", [1], F32) for i in range(2)]
            rss = [A.alloc(f"rstd{i}", [1], F32) for i in range(2)]
            tmp, r_tmp = A.alloc("tmp", [D], F32)
            x1s = [A.alloc(f"x1{i}", [D], F32) for i in range(2)]
            x1b, r_x1b = A.alloc("x1b", [D], BF16)
            x1T, r_x1T = A.alloc("x1T", [8, 128], BF16)
            sg, r_sg = A.alloc("sg", [D], F32)
            pb, r_pb = A.alloc("pb", [PLE], BF16)
            pT, r_pT = A.alloc("pT", [2, 128], BF16)
            x2s = [A.alloc(f"x2{i}", [D], F32) for i in range(2)]

            wload([(lambda e, q=q: e.dma_start(out=Wout[:, q * 4:(q + 1) * 4, :],
                                               in_=w_out_ap[q * 512:(q + 1) * 512, :].rearrange("(e p) n -> p e n", p=128)))
                   for q in range(4)], [r_Wout])
            wload([(lambda e, q=q: e.dma_start(out=Wg[:, q * 4:(q + 1) * 4, :],
                                               in_=ple_gate[layer, q * 512:(q + 1) * 512, :].rearrange("(e p) n -> p e n", p=128)))
                   for q in range(2)] +
                  [lambda e: e.dma_start(out=Wp, in_=ple_proj[layer].rearrange("(e p) n -> p e n", p=128))], [r_Wg, r_Wp])
            load([lambda e: e.dma_start(out=gpost, in_=norm_post[layer:layer + 1, :].to_broadcast([128, D]))], [], [r_gpost])

            o_sb, r_o_sb = A.alloc("o_sb", [D], F32)

            def L(t):
                nr = nrows(t)
                xt, r_xt = xts[t % NS]
                yt, r_yt = yts[t % NS]
                pt, r_pt = pts[t % NS]
                load([lambda e: e.dma_start(out=xt[0:nr], in_=xsrc(layer, t)),
                      lambda e: e.dma_start(out=yt[0:nr], in_=Ysc[t * 128:t * 128 + nr, :]),
                      lambda e: e.dma_start(out=pt[0:nr], in_=psrc(layer, t))],
                     [XR[t], YS[t]], [r_xt, r_yt, r_pt])

            def a_ytr(t):
                nr = nrows(t)
                yt, r_yt = yts[t % NS]
                yT, r_yT = yTs[t % 2]
                pTb = pair_bf(0)
                P.op("pe", [(lambda e, c=c: e.transpose(out=pTb[:, c * 128:c * 128 + nr], in_=yt[0:nr, c * 128:(c + 1) * 128],
                                                        identity=idb[0:nr, 0:nr])) for c in range(16)],
                     reads=[r_yt, r_idb], writes=[psr[0], psr[1]])
                P.op("act", lambda e: e.activation(out=yT[:, :, 0:nr], in_=pTb.rearrange("p (a b) -> p a b", b=128)[:, :, 0:nr], func=AF.Copy),
                     reads=[psr[0], psr[1]], writes=[r_yT])

            def a_o(t):
                nr = nrows(t)
                yT, r_yT = yTs[t % 2]
                po = pair(1)
                P.op("pe", [(lambda e, n=n, c=c: e.matmul(po[0:nr, n * 512:(n + 1) * 512], lhsT=yT[:, c, 0:nr],
                                                          rhs=Wout[:, c, n * 512:(n + 1) * 512], start=(c == 0), stop=(c == 15)))
                            for n in range(2) for c in range(16)],
                     reads=[r_yT, r_Wout], writes=[psr[2], psr[3]])
                P.op("act", lambda e: e.activation(out=o_sb[0:nr], in_=po[0:nr], func=AF.Copy), reads=[psr[2], psr[3]], writes=[r_o_sb])

            def b_chain(t):
                nr = nrows(t)
                s = t % 2
                xt, r_xt = xts[t % NS]
                pt, r_pt = pts[t % NS]
                ss, r_ss = sss[s]
                v1, r_v1 = v1s[s]
                rstd, r_rstd = rss[s]
                x1, r_x1 = x1s[s]
                P.op("act", lambda e: e.activation(out=junk[0:nr], in_=o_sb[0:nr], func=AF.Square, accum_out=ss[0:nr]),
                     reads=[r_o_sb], writes=[r_junk, r_ss])
                rstd_ops(ss, r_ss, v1, r_v1, rstd, r_rstd, nr, 1.0 / D, RMS_EPS)
                P.op("dve", lambda e: e.scalar_tensor_tensor(out=tmp[0:nr], in0=o_sb[0:nr], scalar=rstd[0:nr], in1=gpost[0:nr],
                                                              op0=ALU.mult, op1=ALU.mult),
                     reads=[r_o_sb, r_rstd, r_gpost], writes=[r_tmp])
                P.op("dve", lambda e: e.tensor_tensor(out=x1[0:nr], in0=tmp[0:nr], in1=xt[0:nr], op=ALU.add),
                     reads=[r_tmp, r_xt], writes=[r_x1])
                P.op("dve", lambda e: e.tensor_copy(out=x1b[0:nr], in_=x1[0:nr]), reads=[r_x1], writes=[r_x1b])
                P.op("dve", lambda e: e.tensor_copy(out=pb[0:nr], in_=pt[0:nr]), reads=[r_pt], writes=[r_pb])

            def c_tr(t):
                nr = nrows(t)
                pTb = bank_bf(4)
                P.op("pe", [(lambda e, c=c: e.transpose(out=pTb[:, c * 128:c * 128 + nr], in_=x1b[0:nr, c * 128:(c + 1) * 128],
                                                        identity=idb[0:nr, 0:nr])) for c in range(8)],
                     reads=[r_x1b, r_idb], writes=[psr[4]])
                P.op("act", lambda e: e.activation(out=x1T[:, :, 0:nr], in_=pTb.rearrange("p (a b) -> p a b", b=128)[:, :, 0:nr], func=AF.Copy),
                     reads=[psr[4]], writes=[r_x1T])
                pPb = bank_bf(5)
                P.op("pe", [(lambda e, c=c: e.transpose(out=pPb[:, c * 128:c * 128 + nr], in_=pb[0:nr, c * 128:(c + 1) * 128],
                                                        identity=idb[0:nr, 0:nr])) for c in range(2)],
                     reads=[r_pb, r_idb], writes=[psr[5]])
                P.op("act", lambda e: e.activation(out=pT[:, :, 0:nr], in_=pPb[:, 0:256].rearrange("p (a b) -> p a b", b=128)[:, :, 0:nr], func=AF.Copy),
                     reads=[psr[5]], writes=[r_pT])

            def c_gate(t):
                nr = nrows(t)
                pg = pair(3)
                P.op("pe", [(lambda e, n=n, c=c: e.matmul(pg[0:nr, n * 512:(n + 1) * 512], lhsT=x1T[:, c, 0:nr],
                                                          rhs=Wg[:, c, n * 512:(n + 1) * 512], start=(c == 0), stop=(c == 7)))
                            for n in range(2) for c in range(8)],
                     reads=[r_x1T, r_Wg], writes=[psr[6], psr[7]])

            def d_sig(t):
                nr = nrows(t)
                pg = pair(3)
                P.op("act", lambda e: e.activation(out=sg[0:nr], in_=pg[0:nr], func=AF.Sigmoid), reads=[psr[6], psr[7]], writes=[r_sg])

            def d_pp(t):
                nr = nrows(t)
                s = t % 2
                x1, r_x1 = x1s[s]
                x2, r_x2 = x2s[s]
                pg = pair(3)
                P.op("pe", [(lambda e, n=n, c=c: e.matmul(pg[0:nr, n * 512:(n + 1) * 512], lhsT=pT[:, c, 0:nr],
                                                          rhs=Wp[:, c, n * 512:(n + 1) * 512], start=(c == 0), stop=(c == 1)))
                            for n in range(2) for c in range(2)],
                     reads=[r_pT, r_Wp], writes=[psr[6], psr[7]])
                P.op("dve", lambda e: e.tensor_tensor(out=tmp[0:nr], in0=pg[0:nr], in1=sg[0:nr], op=ALU.mult),
                     reads=[r_sg, psr[6], psr[7]], writes=[r_tmp])
                P.op("dve", lambda e: e.tensor_tensor(out=x2[0:nr], in0=tmp[0:nr], in1=x1[0:nr], op=ALU.add),
                     reads=[r_tmp, r_x1], writes=[r_x2])
                store([lambda e: e.dma_start(out=xdst(t), in_=x2[0:nr])], [r_x2], [XR[t]])

            n_t = len(TILES)
            for step in range(n_t + 4):
                def tl(k):
                    i = step - k
                    return TILES[i] if 0 <= i < n_t else None
                tD, tC, tB, tA, tL = tl(4), tl(3), tl(2), tl(1), tl(0)
                if tD is not None:
                    d_sig(tD)
                if tA is not None:
                    a_ytr(tA)
                if tD is not None:
                    d_pp(tD)
                if tC is not None:
                    c_tr(tC)
                if tB is not None:
                    b_chain(tB)
                if tA is not None:
                    a_o(tA)
                if tC is not None:
                    c_gate(tC)
                if tL is not None:
                    L(tL)

        def phase_F(layer):
            jl = layer // 2
            P.new_epoch()
            A.reset(PH0)
            B = {}
            Wf, r_Wf = A.alloc("Wf", [8, FW], BF16)
            B["gpre"] = A.alloc("gpre", [D], F32)
            gpre, r_gpre = B["gpre"]
            bfb, r_bfb = A.alloc("bfb", [NH], F32)
            common_small(B)
            xts = [A.alloc(f"xt{i}", [D], F32) for i in range(2)]
            B["hb"] = A.alloc("hb", [D], BF16)
            B["hT"] = [A.alloc(f"hT{i}", [8, 128], BF16) for i in range(2)]
            qb, r_qb = A.alloc("qb", [E], BF16)
            kf, r_kf = A.alloc("kf", [E], F32)
            kb_, r_kb = A.alloc("kb", [E], BF16)
            vf, r_vf = A.alloc("vf", [E], F32)
            vb, r_vb = A.alloc("vb", [E], BF16)
            szt, r_szt = A.alloc("szt", [E], F32)
            qT, r_qT = A.alloc("qT", [16, 128], BF16)
            kT, r_kT = A.alloc("kT", [16, 128], BF16)
            t16, r_t16 = A.alloc("t16", [NH], F32)
            e16, r_e16 = A.alloc("e16", [NH], F32)
            l16, r_l16 = A.alloc("l16", [NH], F32)
            lfs_ = [A.alloc(f"lf{i}", [NH], F32) for i in range(2)]

            wload([(lambda e, c=c: e.dma_start(out=Wf[:, c, :], in_=f_w_in[jl, c * 128:(c + 1) * 128, :])) for c in range(8)], [r_Wf])
            load([lambda e: e.dma_start(out=gpre, in_=norm_pre[layer:layer + 1, :].to_broadcast([128, D])),
                  lambda e: e.dma_start(out=bfb, in_=f_b_f[jl:jl + 1, :].to_broadcast([128, NH]))], [], [r_gpre, r_bfb])

            def sL(t):
                nr = nrows(t)
                xt, r_xt = xts[t % 2]
                load([lambda e: e.dma_start(out=xt[0:nr], in_=xsrc(layer, t))], [XR[t]], [r_xt])

            def s0a(t):
                xt, r_xt = xts[t % 2]
                head_a(B, t, xt, r_xt)

            def s0b(t):
                head_b(B, t, 0)

            def s1(t):
                nr = nrows(t)
                r0 = t * 128
                hT, r_hT = B["hT"][t % 2]
                lf, r_lf = lfs_[t % 2]
                for n in range(17):
                    bk = 1 + (n % 3)
                    ncol = 512 if n < 16 else NH
                    P.op("pe", [(lambda e, c=c, n=n, bk=bk, ncol=ncol: e.matmul(bank(bk)[0:nr, 0:ncol], lhsT=hT[:, c, 0:nr],
                                                                                   rhs=Wf[:, c, n * 512:n * 512 + ncol],
                                                                                   start=(c == 0), stop=(c == 7))) for c in range(8)],
                         reads=[r_hT, r_Wf], writes=[psr[bk]])
                    c0 = (n % 4) * 512
                    if n < 4:
                        P.op("act", lambda e, c0=c0, bk=bk: e.activation(out=qb[0:nr, c0:c0 + 512], in_=bank(bk)[0:nr, :], func=AF.Copy),
                             reads=[psr[bk]], writes=[r_qb])
                    elif n < 8:
                        P.op("act", lambda e, c0=c0, bk=bk: e.activation(out=kf[0:nr, c0:c0 + 512], in_=bank(bk)[0:nr, :], func=AF.Copy),
                             reads=[psr[bk]], writes=[r_kf])
                    elif n < 12:
                        P.op("dve", lambda e, c0=c0, bk=bk: e.tensor_copy(out=vf[0:nr, c0:c0 + 512], in_=bank(bk)[0:nr, :]),
                             reads=[psr[bk]], writes=[r_vf])
                    elif n < 16:
                        P.op("act", lambda e, c0=c0, bk=bk: e.activation(out=szt[0:nr, c0:c0 + 512], in_=bank(bk)[0:nr, :], func=AF.Silu),
                             reads=[psr[bk]], writes=[r_szt])
                    else:
                        P.op("dve", lambda e, bk=bk: e.tensor_tensor(out=t16[0:nr], in0=bank(bk)[0:nr, 0:NH], in1=bfb[0:nr], op=ALU.add),
                             reads=[psr[bk], r_bfb], writes=[r_t16])
                    if n == 3:
                        pTb = pair_bf(2)
                        P.op("pe", [(lambda e, c=c: e.transpose(out=pTb[:, c * 128:c * 128 + nr], in_=qb[0:nr, c * 128:(c + 1) * 128],
                                                                identity=idb[0:nr, 0:nr])) for c in range(16)],
                             reads=[r_qb, r_idb], writes=[psr[4], psr[5]])
                        P.op("act", lambda e, pTb=pTb: e.activation(out=qT[:, :, 0:nr], in_=pTb.rearrange("p (a b) -> p a b", b=128)[:, :, 0:nr], func=AF.Copy),
                             reads=[psr[4], psr[5]], writes=[r_qT])
                        store([lambda e: e.dma_start(out=QTs[:, :, r0:r0 + nr].rearrange("h d r -> d h r"), in_=qT[:, :, 0:nr])],
                              [r_qT], [QTr[t]])
                    if n == 7:
                        if t < 32:
                            store([lambda e: e.dma_start(out=fk_p[jl, r0:r0 + nr, :], in_=kf[0:nr])], [r_kf], [])
                        else:
                            store([lambda e: e.dma_start(out=fk_s[jl, :, :], in_=kf[0:nr])], [r_kf], [])
                        P.op("pool", lambda e: e.tensor_copy(out=kb_[0:nr], in_=kf[0:nr]), reads=[r_kf], writes=[r_kb])
                        pTb2 = pair_bf(3)
                        P.op("pe", [(lambda e, c=c: e.transpose(out=pTb2[:, c * 128:c * 128 + nr], in_=kb_[0:nr, c * 128:(c + 1) * 128],
                                                                identity=idb[0:nr, 0:nr])) for c in range(16)],
                             reads=[r_kb, r_idb], writes=[psr[6], psr[7]])
                        P.op("act", lambda e, pTb2=pTb2: e.activation(out=kT[:, :, 0:nr], in_=pTb2.rearrange("p (a b) -> p a b", b=128)[:, :, 0:nr], func=AF.Copy),
                             reads=[psr[6], psr[7]], writes=[r_kT])
                        store([lambda e: e.dma_start(out=KTs[:, :, r0:r0 + nr].rearrange("h d r -> d h r"), in_=kT[:, :, 0:nr])],
                              [r_kT], [KTr[t]])
                    if n == 11:
                        if t < 32:
                            store([lambda e: e.dma_start(out=fv_p[jl, r0:r0 + nr, :], in_=vf[0:nr])], [r_vf], [])
                        else:
                            store([lambda e: e.dma_start(out=fv_s[jl, :, :], in_=vf[0:nr])], [r_vf], [])
                        P.op("pool", lambda e: e.tensor_copy(out=vb[0:nr], in_=vf[0:nr]), reads=[r_vf], writes=[r_vb])
                        store([lambda e: e.dma_start(out=Vbs[:, r0:r0 + nr, :].rearrange("h r e -> r h e"),
                                                     in_=vb[0:nr].rearrange("p (h e) -> p h e", e=DH))], [r_vb], [VBr[t]])
                    if n == 15:
                        store([lambda e: e.dma_start(out=SZs[:, r0:r0 + nr, :].rearrange("h r e -> r h e"),
                                                     in_=szt[0:nr].rearrange("p (h e) -> p h e", e=DH))], [r_szt], [SZr[t]])
                    if n == 16:
                        P.op("act", lambda e: e.activation(out=e16[0:nr], in_=t16[0:nr], func=AF.Exp, scale=-1.0), reads=[r_t16], writes=[r_e16])
                        P.op("act", lambda e: e.activation(out=l16[0:nr], in_=e16[0:nr], func=AF.Ln, bias=1.0, scale=1.0), reads=[r_e16], writes=[r_l16])
                        P.op("dve", lambda e: e.tensor_scalar(out=lf[0:nr], in0=l16[0:nr], scalar1=-1.0, scalar2=None, op0=ALU.mult),
                             reads=[r_l16], writes=[r_lf])
                        if t < 32:
                            store([lambda e: e.dma_start(out=flf_p[jl, r0:r0 + nr, :], in_=lf[0:nr])], [r_lf], [])
                        else:
                            store([lambda e: e.dma_start(out=flf_s[jl, :, :], in_=lf[0:nr])], [r_lf], [LFSr])

            def s2(t):
                if t >= 32:
                    return
                lf, r_lf = lfs_[t % 2]
                bk = 1 + (t % 3)
                P.op("pe", [lambda e: e.matmul(bank(bk)[:, 0:NH], lhsT=tstr, rhs=lf, start=True, stop=True),
                            lambda e: e.matmul(bank(bk)[:, NH:2 * NH], lhsT=ones, rhs=lf, start=True, stop=True)],
                     reads=[r_cs, r_lf], writes=[psr[bk]])
                if t == 0:
                    P.op("dve", lambda e: e.tensor_copy(out=TcTab[:, 0, :], in_=bank(bk)[:, NH:2 * NH]), reads=[psr[bk]], writes=[r_TcTab])
                else:
                    P.op("dve", lambda e: e.tensor_tensor(out=TcTab[:, t, :], in0=bank(bk)[:, NH:2 * NH], in1=TcTab[:, t - 1, :], op=ALU.add),
                         reads=[psr[bk], r_TcTab], writes=[r_TcTab])
                P.op("dve", lambda e: e.tensor_tensor(out=ETab[:, t, :], in0=bank(bk)[:, 0:NH], in1=TcTab[:, t, :], op=ALU.subtract),
                     reads=[psr[bk], r_TcTab], writes=[r_ETab])

            run_stages([sL, s0a, s0b, s1, s2], TILES)

        def phase_A(layer):
            P.new_epoch()
            A.reset(PH0)
            QTh = [A.alloc(f"QTh{i}", [SEQ], BF16) for i in range(2)]
            KTh = [A.alloc(f"KTh{i}", [SEQ], BF16) for i in range(2)]
            Vh = [A.alloc(f"Vh{i}", [32, 132], BF16) for i in range(2)]
            SZh = [A.alloc(f"SZh{i}", [32, 128], F32) for i in range(2)]
            Yh = [A.alloc(f"Yh{i}", [32, 128], BF16) for i in range(2)]
            bT = [A.alloc(f"bT{i}", [32, 8], F32) for i in range(2)]
            Pb = [A.alloc(f"Pb{i}", [512], BF16) for i in range(3)]
            rinv = [A.alloc(f"rinv{i}", [1], F32) for i in range(4)]
            for i in range(2):
                P.op("dve", lambda e, i=i: e.memset(Vh[i][0][:, :, 128:129], 1.0), writes=[Vh[i][1]])

            def load_head(h):
                s = h % 2
                load([lambda e: e.dma_start(out=QTh[s][0], in_=QTs[h, :, 0:SEQ]),
                      lambda e: e.dma_start(out=KTh[s][0], in_=KTs[h, :, 0:SEQ])],
                     QTr[0:32] + KTr[0:32], [QTh[s][1], KTh[s][1]])
                load([lambda e: e.dma_start(out=Vh[s][0][:, :, 0:128], in_=Vbs[h, 0:SEQ, :].rearrange("(t p) e -> p t e", p=128)),
                      lambda e: e.dma_start(out=SZh[s][0], in_=SZs[h, 0:SEQ, :].rearrange("(t p) e -> p t e", p=128))],
                     VBr[0:32] + SZr[0:32], [Vh[s][1], SZh[s][1]])
                for qg in range(8):
                    P.op("dve", lambda e, qg=qg: e.tensor_scalar(out=bT[s][0][:, :, qg], in0=ETab[:, :, h],
                                                                  scalar1=TcTab[:, 4 * qg + 3, h:h + 1], scalar2=None, op0=ALU.add),
                         reads=[r_ETab, r_TcTab], writes=[bT[s][1]])

            its = []
            for h in range(NH):
                for qg in range(8):
                    for kb in range(4 * qg + 4):
                        its.append((h, qg, kb))

            def psO(qg, qt):
                b = (qg % 2) * 2 + qt // 2
                return bank(b)[:, (qt % 2) * 256:(qt % 2) * 256 + 129], b

            def do_S(i):
                h, qg, kb = its[i]
                s = h % 2
                d = max(0, kb - 4 * qg)
                n = (4 - d) * 128
                q0 = (4 * qg + d) * 128
                bk = 4 + i % 3
                P.op("pe", lambda e: e.matmul(bank(bk)[:, 0:n], lhsT=KTh[s][0][:, kb * 128:(kb + 1) * 128],
                                              rhs=QTh[s][0][:, q0:q0 + n], start=True, stop=True),
                     reads=[KTh[s][1], QTh[s][1]], writes=[psr[bk]])

            def do_rest(i):
                h, qg, kb = its[i]
                s = h % 2
                d = max(0, kb - 4 * qg)
                n = (4 - d) * 128
                bk = 4 + i % 3
                pb_, r_pb = Pb[i % 3]
                P.op("act", lambda e: e.activation(out=pb_[:, 0:n], in_=bank(bk)[:, 0:n], func=AF.Exp,
                                                   bias=bT[s][0][:, kb, qg:qg + 1], scale=SCALE),
                     reads=[psr[bk], bT[s][1]], writes=[r_pb])
                if kb >= 4 * qg:
                    P.op("pool", lambda e: e.tensor_tensor(out=pb_[:, 0:128], in0=pb_[:, 0:128], in1=mTb, op=ALU.mult),
                         reads=[r_pb, r_mTb], writes=[r_pb])
                fns = []
                banks = set()
                for qt in range(d, 4):
                    o_ap, b = psO(qg, qt)
                    banks.add(b)
                    fns.append(lambda e, qt=qt, o_ap=o_ap: e.matmul(o_ap, lhsT=pb_[:, (qt - d) * 128:(qt - d + 1) * 128],
                                                                     rhs=Vh[s][0][:, kb, 0:129], start=(kb == 0 and qt % 2 == 0), stop=(kb == 4 * qg + qt),
                                                                     skip_group_check=True))
                P.op("pe", fns, reads=[r_pb, Vh[s][1]], writes=[psr[b] for b in sorted(banks)])
                if kb == 4 * qg + 3:
                    for qt in range(4):
                        o_ap, b = psO(qg, qt)
                        ri, r_ri = rinv[qt]
                        P.op("dve", lambda e, o_ap=o_ap, ri=ri: e.reciprocal(out=ri, in_=o_ap[:, 128:129]), reads=[psr[b]], writes=[r_ri])
                        P.op("dve", lambda e, o_ap=o_ap, ri=ri, qt=qt: e.scalar_tensor_tensor(
                            out=Yh[s][0][:, 4 * qg + qt, :], in0=o_ap[:, 0:128], scalar=ri, in1=SZh[s][0][:, 4 * qg + qt, :],
                            op0=ALU.mult, op1=ALU.mult), reads=[psr[b], r_ri, SZh[s][1]], writes=[Yh[s][1]])
                    if qg == 7:
                        store([lambda e: e.dma_start(out=Ysc[0:SEQ, h * 128:(h + 1) * 128].rearrange("(t p) e -> p t e", p=128), in_=Yh[s][0])],
                              [Yh[s][1]], YS[0:32])

            load_head(0)
            n_it = len(its)
            do_S(0)
            for i in range(n_it):
                h, qg, kb = its[i]
                if qg == 0 and kb == 0 and h + 1 < NH:
                    load_head(h + 1)
                if i + 1 < n_it:
                    do_S(i + 1)
                do_rest(i)

        def phase_S(layer):
            jl = layer // 2
            P.new_epoch()
            A.reset(PH0)
            qTs, r_qTs = A.alloc("qTs", [16, 16], BF16)
            kTs, r_kTs = A.alloc("kTs", [16, 16], BF16)
            vs, r_vs = A.alloc("vs", [16, 132], BF16)
            szs, r_szs = A.alloc("szs", [16, 128], F32)
            lfs, r_lfs = A.alloc("lfs", [NH], F32)
            lfc, r_lfc = A.alloc("lfc", [16, 16], F32)
            sfxc, r_sfxc = A.alloc("sfxc", [16, 16], F32)
            totc, r_totc = A.alloc("totc", [16, 16], F32)
            biasS, r_biasS = A.alloc("biasS", [16, 16], F32)
            biasN, r_biasN = A.alloc("biasN", [NH], F32)
            Rs = [A.alloc(f"R{i}", [NH], F32) for i in range(17)]
            kc = [A.alloc(f"kc{i}", [1024], F32) for i in range(2)]
            vc = [A.alloc(f"vc{i}", [1024], F32) for i in range(2)]
            kcT = [A.alloc(f"kcT{i}", [8, 128], BF16) for i in range(2)]
            Vc = [A.alloc(f"Vc{i}", [8, 132], BF16) for i in range(2)]
            tmpS, r_tmpS = A.alloc("tmpS", [128], F32)
            Ps = [A.alloc(f"Ps{i}", [128], BF16) for i in range(2)]
            rinv = [A.alloc(f"rinvs{i}", [1], F32) for i in range(2)]
            Ysb, r_Ysb = A.alloc("Ysb", [E], BF16)

            load([lambda e: e.dma_start(out=qTs, in_=QTs[:, :, SEQ:SEQ + TS].rearrange("h d r -> d h r")),
                  lambda e: e.dma_start(out=kTs, in_=KTs[:, :, SEQ:SEQ + TS].rearrange("h d r -> d h r")),
                  lambda e: e.dma_start(out=vs[0:TS, :, 0:128], in_=Vbs[:, SEQ:SEQ + TS, :].rearrange("h r e -> r h e")),
                  lambda e: e.dma_start(out=szs[0:TS], in_=SZs[:, SEQ:SEQ + TS, :].rearrange("h r e -> r h e")),
                  lambda e: e.dma_start(out=lfs[0:TS], in_=flf_s[jl, :, :]),
                  lambda e: e.dma_start(out=lfc, in_=clf[jl].rearrange("(t p) h -> p t h", p=128))],
                 [QTr[32], KTr[32], VBr[32], SZr[32], LFSr], [r_qTs, r_kTs, r_vs, r_szs, r_lfs, r_lfc])
            P.op("dve", lambda e: e.memset(vs[0:TS, :, 128:129], 1.0), writes=[r_vs])
            for i in range(2):
                P.op("dve", lambda e, i=i: e.memset(Vc[i][0][:, :, 128:129], 1.0), writes=[Vc[i][1]])
            lfc2 = lfc.rearrange("p a b -> p (a b)")
            P.op("pe", [lambda e: e.matmul(bank(6)[:, 0:256], lhsT=tstr, rhs=lfc2, start=True, stop=True),
                        lambda e: e.matmul(bank(6)[:, 256:512], lhsT=ones, rhs=lfc2, start=True, stop=True),
                        lambda e: e.matmul(bank(7)[0:TS, 0:NH], lhsT=tstr[0:TS, 0:TS], rhs=lfs[0:TS], start=True, stop=True),
                        lambda e: e.matmul(bank(7)[:, NH:2 * NH], lhsT=ones[0:TS, :], rhs=lfs[0:TS], start=True, stop=True)],
                 reads=[r_cs, r_lfc, r_lfs], writes=[psr[6], psr[7]])
            P.op("dve", lambda e: e.tensor_copy(out=sfxc.rearrange("p a b -> p (a b)"), in_=bank(6)[:, 0:256]), reads=[psr[6]], writes=[r_sfxc])
            P.op("dve", lambda e: e.tensor_copy(out=totc.rearrange("p a b -> p (a b)"), in_=bank(6)[:, 256:512]), reads=[psr[6]], writes=[r_totc])
            P.op("dve", lambda e: e.tensor_copy(out=biasN[0:TS], in_=bank(7)[0:TS, 0:NH]), reads=[psr[7]], writes=[r_biasN])
            P.op("dve", lambda e: e.tensor_copy(out=Rs[16][0], in_=bank(7)[:, NH:2 * NH]), reads=[psr[7]], writes=[Rs[16][1]])
            for kt in range(15, -1, -1):
                P.op("dve", lambda e, kt=kt: e.tensor_tensor(out=biasS[:, kt, :], in0=sfxc[:, kt, :], in1=Rs[kt + 1][0], op=ALU.add),
                     reads=[r_sfxc, Rs[kt + 1][1]], writes=[r_biasS])
                P.op("dve", lambda e, kt=kt: e.tensor_tensor(out=Rs[kt][0], in0=Rs[kt + 1][0], in1=totc[:, kt, :], op=ALU.add),
                     reads=[Rs[kt + 1][1], r_totc], writes=[Rs[kt][1]])

            def psOs(hl):
                b = hl // 3
                return bank(b)[0:TS, (hl % 3) * 160:(hl % 3) * 160 + 129], b

            def s_iter(hh, kt):
                if True:
                    s = kt % 2
                    if kt < 16:
                        load([lambda e, kt=kt, s=s: e.dma_start(out=kc[s][0], in_=ck[jl, kt * 128:(kt + 1) * 128, hh * 1024:(hh + 1) * 1024]),
                              lambda e, kt=kt, s=s: e.dma_start(out=vc[s][0], in_=cv[jl, kt * 128:(kt + 1) * 128, hh * 1024:(hh + 1) * 1024])],
                             [], [kc[s][1], vc[s][1]])
                        pk = pair(2)
                        P.op("pe", [(lambda e, hl=hl, s=s: e.transpose(out=pk[:, hl * 128:(hl + 1) * 128], in_=kc[s][0][:, hl * 128:(hl + 1) * 128], identity=ident))
                                    for hl in range(8)], reads=[kc[s][1], r_cs], writes=[psr[4], psr[5]])
                        P.op("act", lambda e, s=s: e.activation(out=kcT[s][0].rearrange("p a b -> p (a b)"), in_=pk, func=AF.Copy),
                             reads=[psr[4], psr[5]], writes=[kcT[s][1]])
                        P.op("pool", lambda e, s=s: e.tensor_copy(out=Vc[s][0][:, :, 0:128], in_=vc[s][0].rearrange("p (h e) -> p h e", e=128)),
                             reads=[vc[s][1]], writes=[Vc[s][1]])
                        P.op("pe", [(lambda e, hl=hl, s=s: e.matmul(bank(6)[:, hl * 16:(hl + 1) * 16], lhsT=kcT[s][0][:, hl, :],
                                                                    rhs=qTs[:, hh * 8 + hl, :], start=True, stop=True)) for hl in range(8)],
                             reads=[kcT[s][1], r_qTs], writes=[psr[6]])
                        P.op("dve", lambda e, kt=kt: e.scalar_tensor_tensor(
                            out=tmpS.rearrange("p (h q) -> p h q", q=16), in0=bank(6)[:, 0:128].rearrange("p (h q) -> p h q", q=16), scalar=SCALE,
                            in1=biasS[:, kt, hh * 8:(hh + 1) * 8].unsqueeze(2).to_broadcast([128, 8, 16]), op0=ALU.mult, op1=ALU.add),
                            reads=[psr[6], r_biasS], writes=[r_tmpS])
                        P.op("act", lambda e, s=s: e.activation(out=Ps[s][0], in_=tmpS, func=AF.Exp), reads=[r_tmpS], writes=[Ps[s][1]])
                        fns = []
                        banks = set()
                        for hl in range(8):
                            o_ap, b = psOs(hl)
                            banks.add(b)
                            fns.append(lambda e, hl=hl, o_ap=o_ap, s=s, kt=kt: e.matmul(o_ap, lhsT=Ps[s][0][:, hl * 16:(hl + 1) * 16],
                                                                                       rhs=Vc[s][0][:, hl, 0:129], start=(kt == 0 and hl % 3 == 0), stop=False,
                                                                                       skip_group_check=True))
                        P.op("pe", fns, reads=[Ps[s][1], Vc[s][1]], writes=[psr[b] for b in sorted(banks)])
                    else:
                        P.op("pe", [(lambda e, hl=hl: e.matmul(bank(6)[0:TS, hl * 16:(hl + 1) * 16], lhsT=kTs[:, hh * 8 + hl, :],
                                                               rhs=qTs[:, hh * 8 + hl, :], start=True, stop=True)) for hl in range(8)],
                             reads=[r_kTs, r_qTs], writes=[psr[6]])
                        P.op("dve", lambda e: e.scalar_tensor_tensor(
                            out=tmpS[0:TS].rearrange("p (h q) -> p h q", q=16), in0=bank(6)[0:TS, 0:128].rearrange("p (h q) -> p h q", q=16), scalar=SCALE,
                            in1=biasN[0:TS, hh * 8:(hh + 1) * 8].unsqueeze(2).to_broadcast([TS, 8, 16]), op0=ALU.mult, op1=ALU.add),
                            reads=[psr[6], r_biasN], writes=[r_tmpS])
                        P.op("act", lambda e, s=s: e.activation(out=Ps[s][0][0:TS], in_=tmpS[0:TS], func=AF.Exp), reads=[r_tmpS], writes=[Ps[s][1]])
                        P.op("pool", lambda e, s=s: e.tensor_tensor(
                            out=Ps[s][0][0:TS].rearrange("p (h q) -> p h q", q=16), in0=Ps[s][0][0:TS].rearrange("p (h q) -> p h q", q=16),
                            in1=mTb[0:TS, 0:TS].unsqueeze(1).to_broadcast([TS, 8, 16]), op=ALU.mult),
                            reads=[Ps[s][1], r_mTb], writes=[Ps[s][1]])
                        fns = []
                        banks = set()
                        for hl in range(8):
                            o_ap, b = psOs(hl)
                            banks.add(b)
                            fns.append(lambda e, hl=hl, o_ap=o_ap, s=s: e.matmul(o_ap, lhsT=Ps[s][0][0:TS, hl * 16:(hl + 1) * 16],
                                                                                rhs=vs[0:TS, hh * 8 + hl, 0:129], start=False, stop=True, skip_group_check=True))
                        P.op("pe", fns, reads=[Ps[s][1], r_vs], writes=[psr[b] for b in sorted(banks)])

            for hh in range(2):
                for kt in range(17):
                    s_iter(hh, kt)
                for hl in range(8):
                    o_ap, b = psOs(hl)
                    ri, r_ri = rinv[hl % 2]
                    hg = hh * 8 + hl
                    P.op("dve", lambda e, o_ap=o_ap, ri=ri: e.reciprocal(out=ri[0:TS], in_=o_ap[:, 128:129]), reads=[psr[b]], writes=[r_ri])
                    P.op("dve", lambda e, o_ap=o_ap, ri=ri, hg=hg: e.scalar_tensor_tensor(
                        out=Ysb[0:TS, hg * 128:(hg + 1) * 128], in0=o_ap[:, 0:128], scalar=ri[0:TS], in1=szs[0:TS, hg, :],
                        op0=ALU.mult, op1=ALU.mult), reads=[psr[b], r_ri, r_szs], writes=[r_Ysb])
            store([lambda e: e.dma_start(out=Ysc[SEQ:SEQ + TS, :], in_=Ysb[0:TS])], [r_Ysb], [YS[32]])

        for layer in range(n_layers):
            jl = layer // 2
            if layer % 2 == 0:
                if 'G' in PHASES:
                    phase_G(layer)
                if 'T' in PHASES:
                    phase_T(layer, g_w_out[jl])
            else:
                if 'F' in PHASES:
                    phase_F(layer)
                if 'A' in PHASES:
                    phase_A(layer)
                if 'S' in PHASES:
                    phase_S(layer)
                if 'T' in PHASES:
                    phase_T(layer, f_w_out[jl])

        P.emit(final_ops=stores)
    return nc


def _consts():
    c = np.zeros((128, 640), np.float32)
    i = np.arange(128)
    c[:, 0:128] = np.eye(128, dtype=np.float32)
    c[:, 128:256] = (i[:, None] > i[None, :])
    c[:, 256:384] = 1.0
    c[:, 384:512] = (i[:, None] <= i[None, :])
    c[:, 512:640] = ((i[:, None] // 64) <= (i[None, :] // 64))
    return c


_NC_CACHE = {}


def kernel(x_prompt, x_sample, cache_fox_k, cache_fox_v, cache_fox_logf, p_prompt, p_sample,
           norm_pre, norm_post, gmlp_w_in, gmlp_ln_g, gmlp_ln_b, gmlp_w_s, gmlp_b_s, gmlp_w_out,
           fox_w_in, fox_b_f, fox_w_out, ple_w_proj, ple_w_gate, _n_layers=4, _cores=8):
    f = lambda a: np.ascontiguousarray(np.asarray(a), dtype=np.float32)
    n = _cores
    if _n_layers not in _NC_CACHE:
        _NC_CACHE[_n_layers] = build(_n_layers)
    nc = _NC_CACHE[_n_layers]
    shared = {
        "norm_pre": f(norm_pre), "norm_post": f(norm_post),
        "g_w_in": f(gmlp_w_in), "g_ln_g": f(gmlp_ln_g), "g_ln_b": f(gmlp_ln_b), "g_w_s": f(gmlp_w_s),
        "g_b_s": f(gmlp_b_s), "g_w_out": f(gmlp_w_out), "f_w_in": f(fox_w_in), "f_b_f": f(fox_b_f),
        "f_w_out": f(fox_w_out), "ple_proj": f(ple_w_proj), "ple_gate": f(ple_w_gate), "consts": _consts(),
    }
    xp = np.asarray(x_prompt); xs = np.asarray(x_sample)
    ckk = np.asarray(cache_fox_k); cvv = np.asarray(cache_fox_v); clff = np.asarray(cache_fox_logf)
    pp = np.asarray(p_prompt); psm = np.asarray(p_sample)
    in_maps = []
    for b in range(n):
        m = dict(shared)
        m["x_p"] = f(xp[b]); m["x_s"] = f(xs[b])
        m["ck"] = f(ckk[:, b].reshape(2, PAST, E)); m["cv"] = f(cvv[:, b].reshape(2, PAST, E)); m["clf"] = f(clff[:, b])
        m["p_p"] = f(pp[:, b]); m["p_s"] = f(psm[:, b])
        in_maps.append(m)
    res = run_bass_kernel_spmd(nc, in_maps, core_ids=list(range(n)))
    R = res.results
    st = lambda k, ax: np.stack([np.asarray(R[b][k], dtype=np.float32) for b in range(n)], axis=ax)
    y_prompt = st("y_p", 0)
    y_sample = st("y_s", 0)
    gv = st("gv_s", 1)
    fk_p = st("fk_p", 1).reshape(2, n, SEQ, NH, DH)
    fv_p = st("fv_p", 1).reshape(2, n, SEQ, NH, DH)
    flf_p = st("flf_p", 1)
    fk_s = st("fk_s", 1).reshape(2, n, TS, NH, DH)
    fv_s = st("fv_s", 1).reshape(2, n, TS, NH, DH)
    flf_s = st("flf_s", 1)
    return (y_prompt, y_sample, gv, fk_p, fv_p, flf_p, fk_s, fv_s, flf_s)
```

```python
import contextlib
import numpy as np
import concourse.bass as bass
import concourse.mybir as mybir
from concourse.bass_utils import run_bass_kernel_spmd

F32 = mybir.dt.float32
BF16 = mybir.dt.bfloat16
AF = mybir.ActivationFunctionType
ALU = mybir.AluOpType

D = 1024
E = 2048
SEQ = 4096
TS = 16
PAST = 2048
NH = 16
DH = 128
PLE = 256
NT = 33
TOK = 4224
SCALE = float(DH) ** -0.5
RMS_EPS = 1e-6
LN_EPS = 1e-5
FW = 4 * E + NH


class Sem:
    def __init__(self, h):
        self.h = h
        self.count = 0
        self.last_op = None


class Op:
    def __init__(self, eng, fns, is_dma):
        self.eng = eng
        self.fns = fns
        self.deps = []
        self.needs_inc = False
        self.sem = None
        self.ticket = None
        self.is_dma = is_dma


class Res:
    def __init__(self, name, arena=None, lo=0, hi=0):
        self.name = name
        self.arena = arena
        self.lo = lo
        self.hi = hi
        self.writers = []
        self.readers = []
        self.overlaps = []


class Prog:
    ENGS = ("sync", "act", "dve", "pool", "pe")
    BLK = {"sync": "sync", "act": "scalar", "dve": "vector", "pool": "gpsimd", "pe": "tensor"}

    def __init__(self, nc, stack):
        self.nc = nc
        self.stack = stack
        self.ops = {e: [] for e in self.ENGS}
        self.eng_sem = {}
        self.pools = {}
        self.pool_idx = {}
        self.arena_res = {}
        self.nsem = 0
        self.new_epoch()

    def new_sem(self, name):
        self.nsem += 1
        return Sem(self.stack.enter_context(self.nc.semaphore(f"{name}_{self.nsem}")))

    def new_epoch(self):
        for e in ("act", "dve", "pool", "pe"):
            self.eng_sem[e] = self.new_sem("e_" + e)

    def dma_pool(self, name, n):
        self.pools[name] = [self.new_sem("d_" + name) for _ in range(n)]
        self.pool_idx[name] = 0

    def res(self, name, arena=None, lo=0, hi=0):
        r = Res(name, arena, lo, hi)
        if arena is not None:
            lst = self.arena_res.setdefault(arena, [])
            for o in lst:
                if o.lo < hi and lo < o.hi:
                    r.overlaps.append(o)
                    o.overlaps.append(r)
            lst.append(r)
        return r

    def _dep(self, op, prod, raw):
        if prod is op:
            return
        if (not prod.is_dma) and (not op.is_dma) and prod.eng == op.eng:
            if not raw:
                return
            if op.eng == "pe":
                return
        prod.needs_inc = True
        op.deps.append(prod)

    def op(self, eng, fns, reads=(), writes=(), pool=None):
        if not isinstance(fns, (list, tuple)):
            fns = [fns]
        is_dma = pool is not None
        o = Op(eng, list(fns), is_dma)
        if is_dma:
            ps = self.pools[pool]
            i = self.pool_idx[pool]
            self.pool_idx[pool] = (i + 1) % len(ps)
            o.sem = ps[i]
            if o.sem.last_op is not None:
                o.sem.last_op.needs_inc = True
                o.deps.append(o.sem.last_op)
            o.sem.last_op = o
        else:
            o.sem = self.eng_sem[eng]
        for r in reads:
            for rr in [r] + r.overlaps:
                for w in rr.writers:
                    self._dep(o, w, True)
        for r in writes:
            for rr in [r] + r.overlaps:
                for w in rr.writers:
                    self._dep(o, w, False)
                for w in rr.readers:
                    self._dep(o, w, False)
        for r in reads:
            r.readers.append(o)
        for r in writes:
            r.writers = [o]
            r.readers = []
        self.ops[eng].append(o)
        return o

    def emit(self, final_ops=()):
        nc = self.nc
        for o in final_ops:
            o.needs_inc = True
        for e in self.ENGS:
            for o in self.ops[e]:
                if o.is_dma:
                    o.sem.count += 16 * len(o.fns)
                    o.ticket = o.sem.count
                elif o.needs_inc:
                    o.sem.count += 1
                    o.ticket = o.sem.count

        import os as _os
        if _os.environ.get("KCHECK"):
            self.check()

        def need_of(deps):
            need = {}
            for d in deps:
                k = id(d.sem)
                if d.ticket > need.get(k, (None, 0))[1]:
                    need[k] = (d.sem, d.ticket)
            return need

        with nc.Block() as block:
            for e in self.ENGS:
                ops = self.ops[e]
                if not ops and e != "sync":
                    continue

                def body(eng, ops=ops, e=e):
                    waited = {}
                    for o in ops:
                        for k, (s, v) in need_of(o.deps).items():
                            if waited.get(k, 0) < v:
                                eng.wait_ge(s.h, v)
                                waited[k] = v
                        n = len(o.fns)
                        for i, f in enumerate(o.fns):
                            ins = f(eng)
                            if o.is_dma:
                                ins.then_inc(o.sem.h, 16)
                            elif o.needs_inc and i == n - 1:
                                ins.then_inc(o.sem.h, 1)
                    if e == "sync":
                        for k, (s, v) in need_of(final_ops).items():
                            if waited.get(k, 0) < v:
                                eng.wait_ge(s.h, v)
                                waited[k] = v

                getattr(block, self.BLK[e])(body)


def _prog_check(self):
    pos = {e: 0 for e in self.ENGS}
    done = set()
    semval = {}
    total = sum(len(v) for v in self.ops.values())
    ndone = 0
    progress = True
    while progress:
        progress = False
        for e in self.ENGS:
            while pos[e] < len(self.ops[e]):
                o = self.ops[e][pos[e]]
                ok = True
                for d in o.deps:
                    if d.ticket is None:
                        raise RuntimeError("dep without ticket")
                    if semval.get(id(d.sem), 0) < d.ticket:
                        ok = False
                        break
                if not ok:
                    break
                if o.ticket is not None:
                    prev = semval.get(id(o.sem), 0)
                    exp = o.ticket - (16 * len(o.fns) if o.is_dma else 1)
                    if prev != exp:
                        raise RuntimeError(f"ticket order violation on {e}: prev={prev} exp={exp}")
                    semval[id(o.sem)] = o.ticket
                pos[e] += 1
                ndone += 1
                progress = True
    print("CHECK: done", ndone, "of", total, {e: (pos[e], len(self.ops[e])) for e in self.ENGS})
    if ndone != total:
        raise RuntimeError("DEADLOCK in abstract simulation")


Prog.check = _prog_check


class Arena:
    def __init__(self, P, name, ap2d_bf16, nbytes):
        self.P = P
        self.name = name
        self.ap = ap2d_bf16
        self.nbytes = nbytes
        self.off = 0

    def reset(self, off):
        self.off = off

    def alloc(self, name, free_shape, dtype):
        esz = 4 if dtype == F32 else 2
        n = int(np.prod(free_shape))
        nb = n * esz
        lo = (self.off + 63) // 64 * 64
        hi = lo + nb
        assert hi <= self.nbytes, f"arena overflow {name}: {hi} > {self.nbytes}"
        self.off = hi
        v = self.ap[:, lo // 2:hi // 2]
        if dtype == F32:
            v = v.bitcast(F32)
        if len(free_shape) == 2:
            v = v.rearrange("p (a b) -> p a b", b=free_shape[1])
        elif len(free_shape) == 3:
            v = v.rearrange("p (a b c) -> p a b c", b=free_shape[1], c=free_shape[2])
        r = self.P.res(name, self.name, lo, hi)
        return v, r


def run_stages(stages, tiles):
    K = len(stages)
    n = len(tiles)
    for step in range(n + K - 1):
        for k in range(K - 1, -1, -1):
            i = step - k
            if 0 <= i < n:
                stages[k](tiles[i])


def nrows(t):
    return 128 if t < 32 else TS


ARENA_BYTES = 207 * 1024


def build(n_layers=4, dbg=None):
    dbg = dbg or {}
    TILES = dbg.get('tiles', list(range(NT)))
    PHASES = dbg.get('phases', 'GTFAS')
    nc = bass.Bass("TRN2", target_bir_lowering=False)

    def din(name, shape, dt=F32):
        return nc.dram_tensor(name, shape, dt, kind="ExternalInput").ap()

    def dout(name, shape, dt=F32):
        return nc.dram_tensor(name, shape, dt, kind="ExternalOutput").ap()

    def dscr(name, shape, dt):
        return nc.dram_tensor(name, shape, dt, kind="Internal").ap()

    x_p = din("x_p", [SEQ, D]); x_s = din("x_s", [TS, D])
    ck = din("ck", [2, PAST, E]); cv = din("cv", [2, PAST, E]); clf = din("clf", [2, PAST, NH])
    p_p = din("p_p", [4, SEQ, PLE]); p_s = din("p_s", [4, TS, PLE])
    norm_pre = din("norm_pre", [4, D]); norm_post = din("norm_post", [4, D])
    g_w_in = din("g_w_in", [2, D, 3 * E]); g_ln_g = din("g_ln_g", [2, E]); g_ln_b = din("g_ln_b", [2, E])
    g_w_s = din("g_w_s", [2, 16, 128, 128]); g_b_s = din("g_b_s", [2, 16, 128]); g_w_out = din("g_w_out", [2, E, D])
    f_w_in = din("f_w_in", [2, D, FW]); f_b_f = din("f_b_f", [2, NH]); f_w_out = din("f_w_out", [2, E, D])
    ple_proj = din("ple_proj", [4, PLE, D]); ple_gate = din("ple_gate", [4, D, D])
    consts = din("consts", [128, 640])

    y_p = dout("y_p", [SEQ, D]); y_s = dout("y_s", [TS, D]); gv_s = dout("gv_s", [2, TS, E])
    fk_p = dout("fk_p", [2, SEQ, E]); fv_p = dout("fv_p", [2, SEQ, E]); flf_p = dout("flf_p", [2, SEQ, NH])
    fk_s = dout("fk_s", [2, TS, E]); fv_s = dout("fv_s", [2, TS, E]); flf_s = dout("flf_s", [2, TS, NH])

    Ysc = dscr("Ysc", [TOK, E], BF16)
    QTs = dscr("QTs", [NH, DH, TOK], BF16)
    KTs = dscr("KTs", [NH, DH, TOK], BF16)
    Vbs = dscr("Vbs", [NH, TOK, DH], BF16)
    SZs = dscr("SZs", [NH, TOK, DH], F32)

    with contextlib.ExitStack() as st:
        P = Prog(nc, st)
        P.dma_pool("ld", 6)
        P.dma_pool("st", 8)
        P.dma_pool("w", 4)
        ar_t = st.enter_context(nc.sbuf_tensor("arena", [128, ARENA_BYTES // 2], BF16))
        A = Arena(P, "sb", ar_t[:], ARENA_BYTES)
        ps_t = [st.enter_context(nc.psum_tensor(f"ps{i}", [128, 1024], F32)) for i in range(4)]
        psr = [P.res(f"bank{i}", "psum", i, i + 1) for i in range(8)]

        def bank(i):
            return ps_t[i // 2][:, (i % 2) * 512:(i % 2 + 1) * 512]

        def bank_bf(i):
            return ps_t[i // 2][:].bitcast(BF16)[:, (i % 2) * 1024:(i % 2 + 1) * 1024]

        def pair(i):
            return ps_t[i][:]

        def pair_bf(i):
            return ps_t[i][:].bitcast(BF16)

        XR = [P.res(f"xr{t}") for t in range(NT)]
        YS = [P.res(f"ysc{t}") for t in range(NT)]
        QTr = [P.res(f"qts{t}") for t in range(NT)]
        KTr = [P.res(f"kts{t}") for t in range(NT)]
        VBr = [P.res(f"vbs{t}") for t in range(NT)]
        SZr = [P.res(f"szs{t}") for t in range(NT)]
        LFSr = P.res("flf_s")
        OUTr = P.res("outs")
        stores = []

        def store(fns, reads, writes):
            o = P.op("sync", fns, reads=reads, writes=writes, pool="st")
            stores.append(o)
            return o

        def load(fns, reads, writes):
            return P.op("sync", fns, reads=reads, writes=writes, pool="ld")

        def wload(fns, writes):
            return P.op("pool", fns, reads=[], writes=writes, pool="w")

        def xsrc(layer, t):
            nr = nrows(t)
            if layer == 0:
                return x_p[t * 128:(t + 1) * 128, :] if t < 32 else x_s[0:nr, :]
            return y_p[t * 128:(t + 1) * 128, :] if t < 32 else y_s[0:nr, :]

        def xdst(t):
            return y_p[t * 128:(t + 1) * 128, :] if t < 32 else y_s[0:TS, :]

        def psrc(layer, t):
            return p_p[layer, t * 128:(t + 1) * 128, :] if t < 32 else p_s[layer, 0:TS, :]

        cs, r_cs = A.alloc("cs", [640], F32)
        idb, r_idb = A.alloc("idb", [128], BF16)
        mTb, r_mTb = A.alloc("mTb", [128], BF16)
        nh, r_nh = A.alloc("nh", [1], F32)
        ETab, r_ETab = A.alloc("ETab", [32, 16], F32)
        TcTab, r_TcTab = A.alloc("TcTab", [32, 16], F32)
        ident = cs[:, 0:128]
        tstr = cs[:, 128:256]
        ones = cs[:, 256:384]
        maskT = cs[:, 384:512]
        maskC = cs[:, 512:640]
        PH0 = A.off

        load([lambda e: e.dma_start(out=cs, in_=consts[:, :])], [], [r_cs])
        P.op("dve", lambda e: e.tensor_copy(out=idb, in_=ident), reads=[r_cs], writes=[r_idb])
        P.op("dve", lambda e: e.tensor_copy(out=mTb, in_=maskT), reads=[r_cs], writes=[r_mTb])
        P.op("pool", lambda e: e.memset(nh, -0.5), writes=[r_nh])

        def rstd_ops(ss, r_ss, v1, r_v1, rstd, r_rstd, nr, mul, eps):
            P.op("pool", lambda e: e.tensor_scalar(out=v1[0:nr], in0=ss[0:nr], scalar1=mul, scalar2=eps,
                                                    op0=ALU.mult, op1=ALU.add), reads=[r_ss], writes=[r_v1])
            P.op("pool", lambda e: e.tensor_tensor(out=rstd[0:nr], in0=v1[0:nr], in1=nh[0:nr], op=ALU.pow),
                 reads=[r_v1, r_nh], writes=[r_rstd])

        def head_a(B, t, xt, r_xt):
            nr = nrows(t)
            s = t % 2
            junk, r_junk = B["junk"]
            ss, r_ss = B["ss"][s]
            v1, r_v1 = B["v1"][s]
            rstd, r_rstd = B["rstd"][s]
            hb, r_hb = B["hb"]
            hT, r_hT = B["hT"][s]
            gpre, r_gpre = B["gpre"]
            P.op("act", lambda e: e.activation(out=junk[0:nr], in_=xt[0:nr], func=AF.Square, accum_out=ss[0:nr]),
                 reads=[r_xt], writes=[r_junk, r_ss])
            rstd_ops(ss, r_ss, v1, r_v1, rstd, r_rstd, nr, 1.0 / D, RMS_EPS)
            P.op("dve", lambda e: e.scalar_tensor_tensor(out=hb[0:nr], in0=xt[0:nr], scalar=rstd[0:nr], in1=gpre[0:nr],
                                                          op0=ALU.mult, op1=ALU.mult),
                 reads=[r_xt, r_rstd, r_gpre], writes=[r_hb])

        def head_b(B, t, tb):
            nr = nrows(t)
            s = t % 2
            hb, r_hb = B["hb"]
            hT, r_hT = B["hT"][s]
            pT = bank_bf(tb)
            P.op("pe", [(lambda e, c=c: e.transpose(out=pT[:, c * 128:c * 128 + nr], in_=hb[0:nr, c * 128:(c + 1) * 128],
                                                    identity=idb[0:nr, 0:nr])) for c in range(8)],
                 reads=[r_hb, r_idb], writes=[psr[tb]])
            P.op("act", lambda e: e.activation(out=hT[:, :, 0:nr], in_=pT.rearrange("p (a b) -> p a b", b=128)[:, :, 0:nr],
                                               func=AF.Copy), reads=[psr[tb]], writes=[r_hT])
            return hT, r_hT

        def common_small(B):
            B["junk"] = A.alloc("junk", [1024], BF16)
            for k in ("ss", "v1", "rstd"):
                B[k] = [A.alloc(f"{k}{i}", [1], F32) for i in range(2)]

        def phase_G(layer):
            jl = layer // 2
            P.new_epoch()
            A.reset(PH0)
            B = {}
            Win, r_Win = A.alloc("Win", [8, 3 * E], BF16)
            wsT, r_wsT = A.alloc("wsT", [16, 128], BF16)
            lng, r_lng = A.alloc("lng", [E], F32)
            lnb, r_lnb = A.alloc("lnb", [E], F32)
            bsT, r_bsT = A.alloc("bsT", [16], F32)
            B["gpre"] = A.alloc("gpre", [D], F32)
            gpre, r_gpre = B["gpre"]
            common_small(B)
            xts = [A.alloc(f"xt{i}", [D], F32) for i in range(2)]
            B["hb"] = A.alloc("hb", [D], BF16)
            B["hT"] = [A.alloc(f"hT{i}", [8, 128], BF16) for i in range(2)]
            us = [A.alloc(f"u{i}", [E], F32) for i in range(2)]
            u, r_u = us[0]
            vg, r_vg = A.alloc("vg", [E], F32)
            szs_ = [A.alloc(f"sz{i}", [E], F32) for i in range(2)]
            vb, r_vb = A.alloc("vb", [E], BF16)
            ys = [A.alloc(f"y{i}", [E], BF16) for i in range(2)]
            stats, r_stats = A.alloc("stats", [4, 6], F32)
            mv, r_mv = A.alloc("mv", [2], F32)
            v2, r_v2 = A.alloc("v2", [1], F32)
            rs2, r_rs2 = A.alloc("rs2", [1], F32)

            wload([(lambda e, c=c: e.dma_start(out=Win[:, c, :], in_=g_w_in[jl, c * 128:(c + 1) * 128, :])) for c in range(8)],
                  [r_Win])
            load([lambda e: e.dma_start(out=lng, in_=g_ln_g[jl:jl + 1, :].to_broadcast([128, E])),
                  lambda e: e.dma_start(out=lnb, in_=g_ln_b[jl:jl + 1, :].to_broadcast([128, E])),
                  lambda e: e.dma_start(out=gpre, in_=norm_pre[layer:layer + 1, :].to_broadcast([128, D])),
                  lambda e: e.dma_start(out=bsT, in_=g_b_s[jl].rearrange("g i -> i g"), allow_slow_non_contiguous=True)],
                 [], [r_lng, r_lnb, r_gpre, r_bsT])
            wst = u.rearrange("p (g j) -> p g j", j=128)
            load([lambda e: e.dma_start(out=wst, in_=g_w_s[jl].rearrange("g i j -> i g j"))], [], [r_u])
            for hf in range(2):
                pp_ = pair(hf)
                P.op("pe", [(lambda e, g=g, pp_=pp_: e.transpose(out=pp_[:, (g % 8) * 128:(g % 8 + 1) * 128], in_=wst[:, g, :], identity=ident))
                            for g in range(hf * 8, hf * 8 + 8)], reads=[r_u, r_cs], writes=[psr[2 * hf], psr[2 * hf + 1]])
                P.op("dve", lambda e, hf=hf, pp_=pp_: e.tensor_tensor(
                    out=wsT[:, hf * 8:(hf + 1) * 8, :], in0=pp_.rearrange("p (g i) -> p g i", i=128),
                    in1=maskC.unsqueeze(1).to_broadcast([128, 8, 128]), op=ALU.mult),
                    reads=[psr[2 * hf], psr[2 * hf + 1], r_cs], writes=[r_wsT])

            def sL(t):
                nr = nrows(t)
                xt, r_xt = xts[t % 2]
                load([lambda e: e.dma_start(out=xt[0:nr], in_=xsrc(layer, t))], [XR[t]], [r_xt])

            def s0a(t):
                xt, r_xt = xts[t % 2]
                head_a(B, t, xt, r_xt)

            def s0b(t):
                head_b(B, t, 0)

            prev_tile = [None]

            def s1(t):
                nr = nrows(t)
                hT, r_hT = B["hT"][t % 2]
                u, r_u = us[t % 2]
                sz, r_sz = szs_[t % 2]
                for n in range(12):
                    if n == 6 and prev_tile[0] is not None:
                        s2(prev_tile[0])
                    bk = 1 + (n % 3)
                    P.op("pe", [(lambda e, c=c, n=n, bk=bk: e.matmul(bank(bk)[0:nr, :], lhsT=hT[:, c, 0:nr],
                                                                      rhs=Win[:, c, n * 512:(n + 1) * 512],
                                                                      start=(c == 0), stop=(c == 7))) for c in range(8)],
                         reads=[r_hT, r_Win], writes=[psr[bk]])
                    if n < 4:
                        dst, r_dst, fn = u, r_u, AF.Gelu_apprx_tanh
                    elif n < 8:
                        dst, r_dst, fn = vg, r_vg, AF.Gelu_apprx_tanh
                    else:
                        dst, r_dst, fn = sz, r_sz, AF.Silu
                    c0 = (n % 4) * 512
                    P.op("act", lambda e, dst=dst, fn=fn, c0=c0, bk=bk: e.activation(out=dst[0:nr, c0:c0 + 512], in_=bank(bk)[0:nr, :], func=fn),
                         reads=[psr[bk]], writes=[r_dst])
                    if n == 7:
                        P.op("dve", [(lambda e, q=q: e.bn_stats(out=stats[0:nr, q, :], in_=vg[0:nr, q * 512:(q + 1) * 512])) for q in range(4)],
                             reads=[r_vg], writes=[r_stats])
                        P.op("dve", lambda e: e.bn_aggr(out=mv[0:nr], in_=stats[0:nr].rearrange("p a b -> p (a b)")),
                             reads=[r_stats], writes=[r_mv])
                        P.op("pool", lambda e: e.tensor_scalar(out=v2[0:nr], in0=mv[0:nr, 1:2], scalar1=1.0, scalar2=LN_EPS,
                                                                op0=ALU.mult, op1=ALU.add), reads=[r_mv], writes=[r_v2])
                        P.op("pool", lambda e: e.tensor_tensor(out=rs2[0:nr], in0=v2[0:nr], in1=nh[0:nr], op=ALU.pow),
                             reads=[r_v2, r_nh], writes=[r_rs2])
                        P.op("dve", lambda e: e.tensor_scalar(out=vg[0:nr], in0=vg[0:nr], scalar1=mv[0:nr, 0:1], scalar2=rs2[0:nr],
                                                               op0=ALU.subtract, op1=ALU.mult), reads=[r_vg, r_mv, r_rs2], writes=[r_vg])
                        P.op("pool", lambda e: e.tensor_tensor(out=vg[0:nr], in0=vg[0:nr], in1=lng[0:nr], op=ALU.mult),
                             reads=[r_vg, r_lng], writes=[r_vg])
                        if t < 32:
                            P.op("dve", lambda e: e.tensor_tensor(out=vb[0:nr], in0=vg[0:nr], in1=lnb[0:nr], op=ALU.add),
                                 reads=[r_vg, r_lnb], writes=[r_vb])
                        else:
                            P.op("dve", lambda e: e.tensor_tensor(out=vg[0:nr], in0=vg[0:nr], in1=lnb[0:nr], op=ALU.add),
                                 reads=[r_vg, r_lnb], writes=[r_vg])
                            P.op("dve", lambda e: e.tensor_copy(out=vb[0:nr], in_=vg[0:nr]), reads=[r_vg], writes=[r_vb])
                            store([lambda e: e.dma_start(out=gv_s[jl, :, :], in_=vg[0:nr])], [r_vg], [])
                prev_tile[0] = t

            def s2(t):
                nr = nrows(t)
                y, r_y = ys[t % 2]
                u, r_u = us[t % 2]
                sz, r_sz = szs_[t % 2]
                P.op("pe", [(lambda e, g=g: e.matmul(bank(4 + g // 4)[0:nr, (g % 4) * 128:(g % 4 + 1) * 128],
                                                     lhsT=wsT[0:nr, g, 0:nr], rhs=vb[0:nr, g * 128:(g + 1) * 128],
                                                     start=True, stop=True)) for g in range(16)],
                     reads=[r_wsT, r_vb], writes=[psr[4], psr[5], psr[6], psr[7]])
                P.op("dve", [(lambda e, g=g: e.scalar_tensor_tensor(
                    out=u[0:nr, g * 128:(g + 1) * 128], in0=bank(4 + g // 4)[0:nr, (g % 4) * 128:(g % 4 + 1) * 128],
                    scalar=bsT[0:nr, g:g + 1], in1=u[0:nr, g * 128:(g + 1) * 128], op0=ALU.add, op1=ALU.mult)) for g in range(16)],
                    reads=[psr[4], psr[5], psr[6], psr[7], r_bsT, r_u], writes=[r_u])
                P.op("dve", lambda e: e.tensor_tensor(out=y[0:nr], in0=u[0:nr], in1=sz[0:nr], op=ALU.mult),
                     reads=[r_u, r_sz], writes=[r_y])
                store([lambda e: e.dma_start(out=Ysc[t * 128:t * 128 + nr, :], in_=y[0:nr])], [r_y], [YS[t]])

            run_stages([sL, s0a, s0b, s1], TILES)
            s2(prev_tile[0])

        def phase_T(layer, w_out_ap):
            P.new_epoch()
            A.reset(PH0)
            Wout, r_Wout = A.alloc("Wout", [16, D], BF16)
            Wg, r_Wg = A.alloc("Wg", [8, D], BF16)
            Wp, r_Wp = A.alloc("Wp", [2, D], BF16)
            gpost, r_gpost = A.alloc("gpost", [D], F32)
            junk, r_junk = A.alloc("junk", [D], BF16)
            NS = 4
            xts = [A.alloc(f"xt{i}", [D], F32) for i in range(NS)]
            yts = [A.alloc(f"yt{i}", [E], BF16) for i in range(NS)]
            pts = [A.alloc(f"pt{i}", [PLE], F32) for i in range(NS)]
            yTs = [A.alloc(f"yT{i}", [16, 128], BF16) for i in range(2)]
            sss = [A.alloc(f"ss{i}", [1], F32) for i in range(2)]
            v1s = [A.alloc(f"v1{i}", [1], F32) for i in range(2)]
            rss = [A.alloc(f"rstd{i}", [1], F32) for i in range(2)]
            tmp, r_tmp = A.alloc("tmp", [D], F32)
            x1s = [A.alloc(f"x1{i}", [D], F32) for i in range(2)]
            x1b, r_x1b = A.alloc("x1b", [D], BF16)
            x1T, r_x1T = A.alloc("x1T", [8, 128], BF16)
            sg, r_sg = A.alloc("sg", [D], F32)
            pb, r_pb = A.alloc("pb", [PLE], BF16)
            pT, r_pT = A.alloc("pT", [2, 128], BF16)
            x2s = [A.alloc(f"x2{i}", [D], F32) for i in range(2)]

            wload([(lambda e, q=q: e.dma_start(out=Wout[:, q * 4:(q + 1) * 4, :],
                                               in_=w_out_ap[q * 512:(q + 1) * 512, :].rearrange("(e p) n -> p e n", p=128)))
                   for q in range(4)], [r_Wout])
            wload([(lambda e, q=q: e.dma_start(out=Wg[:, q * 4:(q + 1) * 4, :],
                                               in_=ple_gate[layer, q * 512:(q + 1) * 512, :].rearrange("(e p) n -> p e n", p=128)))
                   for q in range(2)] +
                  [lambda e: e.dma_start(out=Wp, in_=ple_proj[layer].rearrange("(e p) n -> p e n", p=128))], [r_Wg, r_Wp])
            load([lambda e: e.dma_start(out=gpost, in_=norm_post[layer:layer + 1, :].to_broadcast([128, D]))], [], [r_gpost])

            o_sb, r_o_sb = A.alloc("o_sb", [D], F32)

            def L(t):
                nr = nrows(t)
                xt, r_xt = xts[t % NS]
                yt, r_yt = yts[t % NS]
                pt, r_pt = pts[t % NS]
                load([lambda e: e.dma_start(out=xt[0:nr], in_=xsrc(layer, t)),
                      lambda e: e.dma_start(out=yt[0:nr], in_=Ysc[t * 128:t * 128 + nr, :]),
                      lambda e: e.dma_start(out=pt[0:nr], in_=psrc(layer, t))],
                     [XR[t], YS[t]], [r_xt, r_yt, r_pt])

            def a_ytr(t):
                nr = nrows(t)
                yt, r_yt = yts[t % NS]
                yT, r_yT = yTs[t % 2]
                pTb = pair_bf(0)
                P.op("pe", [(lambda e, c=c: e.transpose(out=pTb[:, c * 128:c * 128 + nr], in_=yt[0:nr, c * 128:(c + 1) * 128],
                                                        identity=idb[0:nr, 0:nr])) for c in range(16)],
                     reads=[r_yt, r_idb], writes=[psr[0], psr[1]])
                P.op("act", lambda e: e.activation(out=yT[:, :, 0:nr], in_=pTb.rearrange("p (a b) -> p a b", b=128)[:, :, 0:nr], func=AF.Copy),
                     reads=[psr[0], psr[1]], writes=[r_yT])

            def a_o(t):
                nr = nrows(t)
                yT, r_yT = yTs[t % 2]
                po = pair(1)
                P.op("pe", [(lambda e, n=n, c=c: e.matmul(po[0:nr, n * 512:(n + 1) * 512], lhsT=yT[:, c, 0:nr],
                                                          rhs=Wout[:, c, n * 512:(n + 1) * 512], start=(c == 0), stop=(c == 15)))
                            for n in range(2) for c in range(16)],
                     reads=[r_yT, r_Wout], writes=[psr[2], psr[3]])
                P.op("act", lambda e: e.activation(out=o_sb[0:nr], in_=po[0:nr], func=AF.Copy), reads=[psr[2], psr[3]], writes=[r_o_sb])

            def b_chain(t):
                nr = nrows(t)
                s = t % 2
                xt, r_xt = xts[t % NS]
                pt, r_pt = pts[t % NS]
                ss, r_ss = sss[s]
                v1, r_v1 = v1s[s]
                rstd, r_rstd = rss[s]
                x1, r_x1 = x1s[s]
                P.op("act", lambda e: e.activation(out=junk[0:nr], in_=o_sb[0:nr], func=AF.Square, accum_out=ss[0:nr]),
                     reads=[r_o_sb], writes=[r_junk, r_ss])
                rstd_ops(ss, r_ss, v1, r_v1, rstd, r_rstd, nr, 1.0 / D, RMS_EPS)
                P.op("dve", lambda e: e.scalar_tensor_tensor(out=tmp[0:nr], in0=o_sb[0:nr], scalar=rstd[0:nr], in1=gpost[0:nr],
                                                              op0=ALU.mult, op1=ALU.mult),
                     reads=[r_o_sb, r_rstd, r_gpost], writes=[r_tmp])
                P.op("dve", lambda e: e.tensor_tensor(out=x1[0:nr], in0=tmp[0:nr], in1=xt[0:nr], op=ALU.add),
                     reads=[r_tmp, r_xt], writes=[r_x1])
                P.op("dve", lambda e: e.tensor_copy(out=x1b[0:nr], in_=x1[0:nr]), reads=[r_x1], writes=[r_x1b])
                P.op("dve", lambda e: e.tensor_copy(out=pb[0:nr], in_=pt[0:nr]), reads=[r_pt], writes=[r_pb])

            def c_tr(t):
                nr = nrows(t)
                pTb = bank_bf(4)
                P.op("pe", [(lambda e, c=c: e.transpose(out=pTb[:, c * 128:c * 128 + nr], in_=x1b[0:nr, c * 128:(c + 1) * 128],
                                                        identity=idb[0:nr, 0:nr])) for c in range(8)],
                     reads=[r_x1b, r_idb], writes=[psr[4]])
                P.op("act", lambda e: e.activation(out=x1T[:, :, 0:nr], in_=pTb.rearrange("p (a b) -> p a b", b=128)[:, :, 0:nr], func=AF.Copy),
                     reads=[psr[4]], writes=[r_x1T])
                pPb = bank_bf(5)
                P.op("pe", [(lambda e, c=c: e.transpose(out=pPb[:, c * 128:c * 128 + nr], in_=pb[0:nr, c * 128:(c + 1) * 128],
                                                        identity=idb[0:nr, 0:nr])) for c in range(2)],
                     reads=[r_pb, r_idb], writes=[psr[5]])
                P.op("act", lambda e: e.activation(out=pT[:, :, 0:nr], in_=pPb[:, 0:256].rearrange("p (a b) -> p a b", b=128)[:, :, 0:nr], func=AF.Copy),
                     reads=[psr[5]], writes=[r_pT])

            def c_gate(t):
                nr = nrows(t)
                pg = pair(3)
                P.op("pe", [(lambda e, n=n, c=c: e.matmul(pg[0:nr, n * 512:(n + 1) * 512], lhsT=x1T[:, c, 0:nr],
                                                          rhs=Wg[:, c, n * 512:(n + 1) * 512], start=(c == 0), stop=(c == 7)))
                            for n in range(2) for c in range(8)],
                     reads=[r_x1T, r_Wg], writes=[psr[6], psr[7]])

            def d_sig(t):
                nr = nrows(t)
                pg = pair(3)
                P.op("act", lambda e: e.activation(out=sg[0:nr], in_=pg[0:nr], func=AF.Sigmoid), reads=[psr[6], psr[7]], writes=[r_sg])

            def d_pp(t):
                nr = nrows(t)
                s = t % 2
                x1, r_x1 = x1s[s]
                x2, r_x2 = x2s[s]
                pg = pair(3)
                P.op("pe", [(lambda e, n=n, c=c: e.matmul(pg[0:nr, n * 512:(n + 1) * 512], lhsT=pT[:, c, 0:nr],
                                                          rhs=Wp[:, c, n * 512:(n + 1) * 512], start=(c == 0), stop=(c == 1)))
                            for n in range(2) for c in range(2)],
                     reads=[r_pT, r_Wp], writes=[psr[6], psr[7]])
                P.op("dve", lambda e: e.tensor_tensor(out=tmp[0:nr], in0=pg[0:nr], in1=sg[0:nr], op=ALU.mult),
                     reads=[r_sg, psr[6], psr[7]], writes=[r_tmp])
                P.op("dve", lambda e: e.tensor_tensor(out=x2[0:nr], in0=tmp[0:nr], in1=x1[0:nr], op=ALU.add),
                     reads=[r_tmp, r_x1], writes=[r_x2])
                store([lambda e: e.dma_start(out=xdst(t), in_=x2[0:nr])], [r_x2], [XR[t]])

            n_t = len(TILES)
            for step in range(n_t + 4):
                def tl(k):
                    i = step - k
                    return TILES[i] if 0 <= i < n_t else None
                tD, tC, tB, tA, tL = tl(4), tl(3), tl(2), tl(1), tl(0)
                if tD is not None:
                    d_sig(tD)
                if tA is not None:
                    a_ytr(tA)
                if tD is not None:
                    d_pp(tD)
                if tC is not None:
                    c_tr(tC)
                if tB is not None:
                    b_chain(tB)
                if tA is not None:
                    a_o(tA)
                if tC is not None:
                    c_gate(tC)
                if tL is not None:
                    L(tL)

        def phase_F(layer):
            jl = layer // 2
            P.new_epoch()
            A.reset(PH0)
            B = {}
            Wf, r_Wf = A.alloc("Wf", [8, FW], BF16)
            B["gpre"] = A.alloc("gpre", [D], F32)
            gpre, r_gpre = B["gpre"]
            bfb, r_bfb = A.alloc("bfb", [NH], F32)
            common_small(B)
            xts = [A.alloc(f"xt{i}", [D], F32) for i in range(2)]
            B["hb"] = A.alloc("hb", [D], BF16)
            B["hT"] = [A.alloc(f"hT{i}", [8, 128], BF16) for i in range(2)]
            qb, r_qb = A.alloc("qb", [E], BF16)
            kf, r_kf = A.alloc("kf", [E], F32)
            kb_, r_kb = A.alloc("kb", [E], BF16)
            vf, r_vf = A.alloc("vf", [E], F32)
            vb, r_vb = A.alloc("vb", [E], BF16)
            szt, r_szt = A.alloc("szt", [E], F32)
            qT, r_qT = A.alloc("qT", [16, 128], BF16)
            kT, r_kT = A.alloc("kT", [16, 128], BF16)
            t16, r_t16 = A.alloc("t16", [NH], F32)
            e16, r_e16 = A.alloc("e16", [NH], F32)
            l16, r_l16 = A.alloc("l16", [NH], F32)
            lfs_ = [A.alloc(f"lf{i}", [NH], F32) for i in range(2)]

            wload([(lambda e, c=c: e.dma_start(out=Wf[:, c, :], in_=f_w_in[jl, c * 128:(c + 1) * 128, :])) for c in range(8)], [r_Wf])
            load([lambda e: e.dma_start(out=gpre, in_=norm_pre[layer:layer + 1, :].to_broadcast([128, D])),
                  lambda e: e.dma_start(out=bfb, in_=f_b_f[jl:jl + 1, :].to_broadcast([128, NH]))], [], [r_gpre, r_bfb])

            def sL(t):
                nr = nrows(t)
                xt, r_xt = xts[t % 2]
                load([lambda e: e.dma_start(out=xt[0:nr], in_=xsrc(layer, t))], [XR[t]], [r_xt])

            def s0a(t):
                xt, r_xt = xts[t % 2]
                head_a(B, t, xt, r_xt)

            def s0b(t):
                head_b(B, t, 0)

            def s1(t):
                nr = nrows(t)
                r0 = t * 128
                hT, r_hT = B["hT"][t % 2]
                lf, r_lf = lfs_[t % 2]
                for n in range(17):
                    bk = 1 + (n % 3)
                    ncol = 512 if n < 16 else NH
                    P.op("pe", [(lambda e, c=c, n=n, bk=bk, ncol=ncol: e.matmul(bank(bk)[0:nr, 0:ncol], lhsT=hT[:, c, 0:nr],
                                                                                   rhs=Wf[:, c, n * 512:n * 512 + ncol],
                                                                                   start=(c == 0), stop=(c == 7))) for c in range(8)],
                         reads=[r_hT, r_Wf], writes=[psr[bk]])
                    c0 = (n % 4) * 512
                    if n < 4:
                        P.op("act", lambda e, c0=c0, bk=bk: e.activation(out=qb[0:nr, c0:c0 + 512], in_=bank(bk)[0:nr, :], func=AF.Copy),
                             reads=[psr[bk]], writes=[r_qb])
                    elif n < 8:
                        P.op("act", lambda e, c0=c0, bk=bk: e.activation(out=kf[0:nr, c0:c0 + 512], in_=bank(bk)[0:nr, :], func=AF.Copy),
                             reads=[psr[bk]], writes=[r_kf])
                        P.op("dve", lambda e, c0=c0, bk=bk: e.tensor_copy(out=kb_[0:nr, c0:c0 + 512], in_=bank(bk)[0:nr, :]),
                             reads=[], writes=[psr[bk], r_kb])
                    elif n < 12:
                        P.op("dve", lambda e, c0=c0, bk=bk: e.tensor_copy(out=vf[0:nr, c0:c0 + 512], in_=bank(bk)[0:nr, :]),
                             reads=[psr[bk]], writes=[r_vf])
                        P.op("act", lambda e, c0=c0, bk=bk: e.activation(out=vb[0:nr, c0:c0 + 512], in_=bank(bk)[0:nr, :], func=AF.Copy),
                             reads=[], writes=[psr[bk], r_vb])
                    elif n < 16:
                        P.op("act", lambda e, c0=c0, bk=bk: e.activation(out=szt[0:nr, c0:c0 + 512], in_=bank(bk)[0:nr, :], func=AF.Silu),
                             reads=[psr[bk]], writes=[r_szt])
                    else:
                        P.op("dve", lambda e, bk=bk: e.tensor_tensor(out=t16[0:nr], in0=bank(bk)[0:nr, 0:NH], in1=bfb[0:nr], op=ALU.add),
                             reads=[psr[bk], r_bfb], writes=[r_t16])
                    if n == 5:
                        pTb = pair_bf(2)
                        P.op("pe", [(lambda e, c=c: e.transpose(out=pTb[:, c * 128:c * 128 + nr], in_=qb[0:nr, c * 128:(c + 1) * 128],
                                                                identity=idb[0:nr, 0:nr])) for c in range(16)],
                             reads=[r_qb, r_idb], writes=[psr[4], psr[5]])
                        P.op("act", lambda e, pTb=pTb: e.activation(out=qT[:, :, 0:nr], in_=pTb.rearrange("p (a b) -> p a b", b=128)[:, :, 0:nr], func=AF.Copy),
                             reads=[psr[4], psr[5]], writes=[r_qT])
                        store([lambda e: e.dma_start(out=QTs[:, :, r0:r0 + nr].rearrange("h d r -> d h r"), in_=qT[:, :, 0:nr])],
                              [r_qT], [QTr[t]])
                    if n == 7:
                        if t < 32:
                            store([lambda e: e.dma_start(out=fk_p[jl, r0:r0 + nr, :], in_=kf[0:nr])], [r_kf], [])
                        else:
                            store([lambda e: e.dma_start(out=fk_s[jl, :, :], in_=kf[0:nr])], [r_kf], [])
                    if n == 9:
                        pTb2 = pair_bf(3)
                        P.op("pe", [(lambda e, c=c: e.transpose(out=pTb2[:, c * 128:c * 128 + nr], in_=kb_[0:nr, c * 128:(c + 1) * 128],
                                                                identity=idb[0:nr, 0:nr])) for c in range(16)],
                             reads=[r_kb, r_idb], writes=[psr[6], psr[7]])
                        P.op("act", lambda e, pTb2=pTb2: e.activation(out=kT[:, :, 0:nr], in_=pTb2.rearrange("p (a b) -> p a b", b=128)[:, :, 0:nr], func=AF.Copy),
                             reads=[psr[6], psr[7]], writes=[r_kT])
                        store([lambda e: e.dma_start(out=KTs[:, :, r0:r0 + nr].rearrange("h d r -> d h r"), in_=kT[:, :, 0:nr])],
                              [r_kT], [KTr[t]])
                    if n == 11:
                        if t < 32:
                            store([lambda e: e.dma_start(out=fv_p[jl, r0:r0 + nr, :], in_=vf[0:nr])], [r_vf], [])
                        else:
                            store([lambda e: e.dma_start(out=fv_s[jl, :, :], in_=vf[0:nr])], [r_vf], [])
                        store([lambda e: e.dma_start(out=Vbs[:, r0:r0 + nr, :].rearrange("h r e -> r h e"),
                                                     in_=vb[0:nr].rearrange("p (h e) -> p h e", e=DH))], [r_vb], [VBr[t]])
                    if n == 15:
                        store([lambda e: e.dma_start(out=SZs[:, r0:r0 + nr, :].rearrange("h r e -> r h e"),
                                                     in_=szt[0:nr].rearrange("p (h e) -> p h e", e=DH))], [r_szt], [SZr[t]])
                    if n == 16:
                        P.op("act", lambda e: e.activation(out=e16[0:nr], in_=t16[0:nr], func=AF.Exp, scale=-1.0), reads=[r_t16], writes=[r_e16])
                        P.op("act", lambda e: e.activation(out=l16[0:nr], in_=e16[0:nr], func=AF.Ln, bias=1.0, scale=1.0), reads=[r_e16], writes=[r_l16])
                        P.op("dve", lambda e: e.tensor_scalar(out=lf[0:nr], in0=l16[0:nr], scalar1=-1.0, scalar2=None, op0=ALU.mult),
                             reads=[r_l16], writes=[r_lf])
                        if t < 32:
                            store([lambda e: e.dma_start(out=flf_p[jl, r0:r0 + nr, :], in_=lf[0:nr])], [r_lf], [])
                        else:
                            store([lambda e: e.dma_start(out=flf_s[jl, :, :], in_=lf[0:nr])], [r_lf], [LFSr])

            def s2(t):
                if t >= 32:
                    return
                lf, r_lf = lfs_[t % 2]
                bk = 1 + (t % 3)
                P.op("pe", [lambda e: e.matmul(bank(bk)[:, 0:NH], lhsT=tstr, rhs=lf, start=True, stop=True),
                            lambda e: e.matmul(bank(bk)[:, NH:2 * NH], lhsT=ones, rhs=lf, start=True, stop=True)],
                     reads=[r_cs, r_lf], writes=[psr[bk]])
                if t == 0:
                    P.op("dve", lambda e: e.tensor_copy(out=TcTab[:, 0, :], in_=bank(bk)[:, NH:2 * NH]), reads=[psr[bk]], writes=[r_TcTab])
                else:
                    P.op("dve", lambda e: e.tensor_tensor(out=TcTab[:, t, :], in0=bank(bk)[:, NH:2 * NH], in1=TcTab[:, t - 1, :], op=ALU.add),
                         reads=[psr[bk], r_TcTab], writes=[r_TcTab])
                P.op("dve", lambda e: e.tensor_tensor(out=ETab[:, t, :], in0=bank(bk)[:, 0:NH], in1=TcTab[:, t, :], op=ALU.subtract),
                     reads=[psr[bk], r_TcTab], writes=[r_ETab])

            run_stages([sL, s0a, s0b, s1, s2], TILES)

        def phase_A(layer):
            P.new_epoch()
            A.reset(PH0)
            QTh = [A.alloc(f"QTh{i}", [SEQ], BF16) for i in range(2)]
            KTh = [A.alloc(f"KTh{i}", [SEQ], BF16) for i in range(2)]
            Vh = [A.alloc(f"Vh{i}", [32, 132], BF16) for i in range(2)]
            SZh = [A.alloc(f"SZh{i}", [32, 128], F32) for i in range(2)]
            Yh = [A.alloc(f"Yh{i}", [32, 128], BF16) for i in range(2)]
            bT = [A.alloc(f"bT{i}", [32, 8], F32) for i in range(2)]
            Pb = [A.alloc(f"Pb{i}", [512], BF16) for i in range(3)]
            rinv = [A.alloc(f"rinv{i}", [1], F32) for i in range(4)]
            for i in range(2):
                P.op("dve", lambda e, i=i: e.memset(Vh[i][0][:, :, 128:129], 1.0), writes=[Vh[i][1]])

            def load_head(h):
                s = h % 2
                load([lambda e: e.dma_start(out=QTh[s][0], in_=QTs[h, :, 0:SEQ]),
                      lambda e: e.dma_start(out=KTh[s][0], in_=KTs[h, :, 0:SEQ])],
                     QTr[0:32] + KTr[0:32], [QTh[s][1], KTh[s][1]])
                load([lambda e: e.dma_start(out=Vh[s][0][:, :, 0:128], in_=Vbs[h, 0:SEQ, :].rearrange("(t p) e -> p t e", p=128)),
                      lambda e: e.dma_start(out=SZh[s][0], in_=SZs[h, 0:SEQ, :].rearrange("(t p) e -> p t e", p=128))],
                     VBr[0:32] + SZr[0:32], [Vh[s][1], SZh[s][1]])
                for qg in range(8):
                    P.op("dve", lambda e, qg=qg: e.tensor_scalar(out=bT[s][0][:, :, qg], in0=ETab[:, :, h],
                                                                  scalar1=TcTab[:, 4 * qg + 3, h:h + 1], scalar2=None, op0=ALU.add),
                         reads=[r_ETab, r_TcTab], writes=[bT[s][1]])

            its = []
            for h in range(NH):
                for qg in range(8):
                    for kb in range(4 * qg + 4):
                        its.append((h, qg, kb))

            def psO(qg, qt):
                b = (qg % 2) * 2 + qt // 2
                return bank(b)[:, (qt % 2) * 256:(qt % 2) * 256 + 129], b

            def do_S(i):
                h, qg, kb = its[i]
                s = h % 2
                d = max(0, kb - 4 * qg)
                n = (4 - d) * 128
                q0 = (4 * qg + d) * 128
                bk = 4 + i % 3
                P.op("pe", lambda e: e.matmul(bank(bk)[:, 0:n], lhsT=KTh[s][0][:, kb * 128:(kb + 1) * 128],
                                              rhs=QTh[s][0][:, q0:q0 + n], start=True, stop=True),
                     reads=[KTh[s][1], QTh[s][1]], writes=[psr[bk]])

            def do_rest(i):
                h, qg, kb = its[i]
                s = h % 2
                d = max(0, kb - 4 * qg)
                n = (4 - d) * 128
                bk = 4 + i % 3
                pb_, r_pb = Pb[i % 3]
                P.op("act", lambda e: e.activation(out=pb_[:, 0:n], in_=bank(bk)[:, 0:n], func=AF.Exp,
                                                   bias=bT[s][0][:, kb, qg:qg + 1], scale=SCALE),
                     reads=[psr[bk], bT[s][1]], writes=[r_pb])
                if kb >= 4 * qg:
                    P.op("pool", lambda e: e.tensor_tensor(out=pb_[:, 0:128], in0=pb_[:, 0:128], in1=mTb, op=ALU.mult),
                         reads=[r_pb, r_mTb], writes=[r_pb])
                fns = []
                banks = set()
                for qt in range(d, 4):
                    o_ap, b = psO(qg, qt)
                    banks.add(b)
                    fns.append(lambda e, qt=qt, o_ap=o_ap: e.matmul(o_ap, lhsT=pb_[:, (qt - d) * 128:(qt - d + 1) * 128],
                                                                     rhs=Vh[s][0][:, kb, 0:129], start=(kb == 0 and qt % 2 == 0), stop=(kb == 4 * qg + qt),
                                                                     skip_group_check=True))
                P.op("pe", fns, reads=[r_pb, Vh[s][1]], writes=[psr[b] for b in sorted(banks)])
                if kb == 4 * qg + 3:
                    for qt in range(4):
                        o_ap, b = psO(qg, qt)
                        ri, r_ri = rinv[qt]
                        P.op("dve", lambda e, o_ap=o_ap, ri=ri: e.reciprocal(out=ri, in_=o_ap[:, 128:129]), reads=[psr[b]], writes=[r_ri])
                        P.op("dve", lambda e, o_ap=o_ap, ri=ri, qt=qt: e.scalar_tensor_tensor(
                            out=Yh[s][0][:, 4 * qg + qt, :], in0=o_ap[:, 0:128], scalar=ri, in1=SZh[s][0][:, 4 * qg + qt, :],
                            op0=ALU.mult, op1=ALU.mult), reads=[psr[b], r_ri, SZh[s][1]], writes=[Yh[s][1]])
                    if qg == 7:
                        store([lambda e: e.dma_start(out=Ysc[0:SEQ, h * 128:(h + 1) * 128].rearrange("(t p) e -> p t e", p=128), in_=Yh[s][0])],
                              [Yh[s][1]], YS[0:32])

            load_head(0)
            n_it = len(its)
            do_S(0)
            for i in range(n_it):
                h, qg, kb = its[i]
                if qg == 0 and kb == 0 and h + 1 < NH:
                    load_head(h + 1)
                if i + 1 < n_it:
                    do_S(i + 1)
                do_rest(i)

        def phase_S(layer):
            jl = layer // 2
            P.new_epoch()
            A.reset(PH0)
            qTs, r_qTs = A.alloc("qTs", [16, 16], BF16)
            kTs, r_kTs = A.alloc("kTs", [16, 16], BF16)
            vs, r_vs = A.alloc("vs", [16, 132], BF16)
            szs, r_szs = A.alloc("szs", [16, 128], F32)
            lfs, r_lfs = A.alloc("lfs", [NH], F32)
            lfc, r_lfc = A.alloc("lfc", [16, 16], F32)
            sfxc, r_sfxc = A.alloc("sfxc", [16, 16], F32)
            totc, r_totc = A.alloc("totc", [16, 16], F32)
            biasS, r_biasS = A.alloc("biasS", [16, 16], F32)
            biasN, r_biasN = A.alloc("biasN", [NH], F32)
            Rs = [A.alloc(f"R{i}", [NH], F32) for i in range(17)]
            kc = [A.alloc(f"kc{i}", [1024], F32) for i in range(2)]
            vc = [A.alloc(f"vc{i}", [1024], F32) for i in range(2)]
            kcT = [A.alloc(f"kcT{i}", [8, 128], BF16) for i in range(2)]
            Vc = [A.alloc(f"Vc{i}", [8, 132], BF16) for i in range(2)]
            tmpS, r_tmpS = A.alloc("tmpS", [128], F32)
            Ps = [A.alloc(f"Ps{i}", [128], BF16) for i in range(2)]
            rinv = [A.alloc(f"rinvs{i}", [1], F32) for i in range(2)]
            Ysb, r_Ysb = A.alloc("Ysb", [E], BF16)

            load([lambda e: e.dma_start(out=qTs, in_=QTs[:, :, SEQ:SEQ + TS].rearrange("h d r -> d h r")),
                  lambda e: e.dma_start(out=kTs, in_=KTs[:, :, SEQ:SEQ + TS].rearrange("h d r -> d h r")),
                  lambda e: e.dma_start(out=vs[0:TS, :, 0:128], in_=Vbs[:, SEQ:SEQ + TS, :].rearrange("h r e -> r h e")),
                  lambda e: e.dma_start(out=szs[0:TS], in_=SZs[:, SEQ:SEQ + TS, :].rearrange("h r e -> r h e")),
                  lambda e: e.dma_start(out=lfs[0:TS], in_=flf_s[jl, :, :]),
                  lambda e: e.dma_start(out=lfc, in_=clf[jl].rearrange("(t p) h -> p t h", p=128))],
                 [QTr[32], KTr[32], VBr[32], SZr[32], LFSr], [r_qTs, r_kTs, r_vs, r_szs, r_lfs, r_lfc])
            P.op("dve", lambda e: e.memset(vs[0:TS, :, 128:129], 1.0), writes=[r_vs])
            for i in range(2):
                P.op("dve", lambda e, i=i: e.memset(Vc[i][0][:, :, 128:129], 1.0), writes=[Vc[i][1]])
            lfc2 = lfc.rearrange("p a b -> p (a b)")
            P.op("pe", [lambda e: e.matmul(bank(6)[:, 0:256], lhsT=tstr, rhs=lfc2, start=True, stop=True),
                        lambda e: e.matmul(bank(6)[:, 256:512], lhsT=ones, rhs=lfc2, start=True, stop=True),
                        lambda e: e.matmul(bank(7)[0:TS, 0:NH], lhsT=tstr[0:TS, 0:TS], rhs=lfs[0:TS], start=True, stop=True),
                        lambda e: e.matmul(bank(7)[:, NH:2 * NH], lhsT=ones[0:TS, :], rhs=lfs[0:TS], start=True, stop=True)],
                 reads=[r_cs, r_lfc, r_lfs], writes=[psr[6], psr[7]])
            P.op("dve", lambda e: e.tensor_copy(out=sfxc.rearrange("p a b -> p (a b)"), in_=bank(6)[:, 0:256]), reads=[psr[6]], writes=[r_sfxc])
            P.op("dve", lambda e: e.tensor_copy(out=totc.rearrange("p a b -> p (a b)"), in_=bank(6)[:, 256:512]), reads=[psr[6]], writes=[r_totc])
            P.op("dve", lambda e: e.tensor_copy(out=biasN[0:TS], in_=bank(7)[0:TS, 0:NH]), reads=[psr[7]], writes=[r_biasN])
            P.op("dve", lambda e: e.tensor_copy(out=Rs[16][0], in_=bank(7)[:, NH:2 * NH]), reads=[psr[7]], writes=[Rs[16][1]])
            for kt in range(15, -1, -1):
                P.op("dve", lambda e, kt=kt: e.tensor_tensor(out=biasS[:, kt, :], in0=sfxc[:, kt, :], in1=Rs[kt + 1][0], op=ALU.add),
                     reads=[r_sfxc, Rs[kt + 1][1]], writes=[r_biasS])
                P.op("dve", lambda e, kt=kt: e.tensor_tensor(out=Rs[kt][0], in0=Rs[kt + 1][0], in1=totc[:, kt, :], op=ALU.add),
                     reads=[Rs[kt + 1][1], r_totc], writes=[Rs[kt][1]])

            def psOs(hl):
                b = hl // 3
                return bank(b)[0:TS, (hl % 3) * 160:(hl % 3) * 160 + 129], b

            def s_iter(hh, kt):
                if True:
                    s = kt % 2
                    if kt < 16:
                        load([lambda e, kt=kt, s=s: e.dma_start(out=kc[s][0], in_=ck[jl, kt * 128:(kt + 1) * 128, hh * 1024:(hh + 1) * 1024]),
                              lambda e, kt=kt, s=s: e.dma_start(out=vc[s][0], in_=cv[jl, kt * 128:(kt + 1) * 128, hh * 1024:(hh + 1) * 1024])],
                             [], [kc[s][1], vc[s][1]])
                        pk = pair(2)
                        P.op("pe", [(lambda e, hl=hl, s=s: e.transpose(out=pk[:, hl * 128:(hl + 1) * 128], in_=kc[s][0][:, hl * 128:(hl + 1) * 128], identity=ident))
                                    for hl in range(8)], reads=[kc[s][1], r_cs], writes=[psr[4], psr[5]])
                        P.op("act", lambda e, s=s: e.activation(out=kcT[s][0].rearrange("p a b -> p (a b)"), in_=pk, func=AF.Copy),
                             reads=[psr[4], psr[5]], writes=[kcT[s][1]])
                        P.op("pool", lambda e, s=s: e.tensor_copy(out=Vc[s][0][:, :, 0:128], in_=vc[s][0].rearrange("p (h e) -> p h e", e=128)),
                             reads=[vc[s][1]], writes=[Vc[s][1]])
                        P.op("pe", [(lambda e, hl=hl, s=s: e.matmul(bank(6)[:, hl * 16:(hl + 1) * 16], lhsT=kcT[s][0][:, hl, :],
                                                                    rhs=qTs[:, hh * 8 + hl, :], start=True, stop=True)) for hl in range(8)],
                             reads=[kcT[s][1], r_qTs], writes=[psr[6]])
                        P.op("dve", lambda e, kt=kt: e.scalar_tensor_tensor(
                            out=tmpS.rearrange("p (h q) -> p h q", q=16), in0=bank(6)[:, 0:128].rearrange("p (h q) -> p h q", q=16), scalar=SCALE,
                            in1=biasS[:, kt, hh * 8:(hh + 1) * 8].unsqueeze(2).to_broadcast([128, 8, 16]), op0=ALU.mult, op1=ALU.add),
                            reads=[psr[6], r_biasS], writes=[r_tmpS])
                        P.op("act", lambda e, s=s: e.activation(out=Ps[s][0], in_=tmpS, func=AF.Exp), reads=[r_tmpS], writes=[Ps[s][1]])
                        fns = []
                        banks = set()
                        for hl in range(8):
                            o_ap, b = psOs(hl)
                            banks.add(b)
                            fns.append(lambda e, hl=hl, o_ap=o_ap, s=s, kt=kt: e.matmul(o_ap, lhsT=Ps[s][0][:, hl * 16:(hl + 1) * 16],
                                                                                       rhs=Vc[s][0][:, hl, 0:129], start=(kt == 0 and hl % 3 == 0), stop=False,
                                                                                       skip_group_check=True))
                        P.op("pe", fns, reads=[Ps[s][1], Vc[s][1]], writes=[psr[b] for b in sorted(banks)])
                    else:
                        P.op("pe", [(lambda e, hl=hl: e.matmul(bank(6)[0:TS, hl * 16:(hl + 1) * 16], lhsT=kTs[:, hh * 8 + hl, :],
                                                               rhs=qTs[:, hh * 8 + hl, :], start=True, stop=True)) for hl in range(8)],
                             reads=[r_kTs, r_qTs], writes=[psr[6]])
                        P.op("dve", lambda e: e.scalar_tensor_tensor(
                            out=tmpS[0:TS].rearrange("p (h q) -> p h q", q=16), in0=bank(6)[0:TS, 0:128].rearrange("p (h q) -> p h q", q=16), scalar=SCALE,
                            in1=biasN[0:TS, hh * 8:(hh + 1) * 8].unsqueeze(2).to_broadcast([TS, 8, 16]), op0=ALU.mult, op1=ALU.add),
                            reads=[psr[6], r_biasN], writes=[r_tmpS])
                        P.op("act", lambda e, s=s: e.activation(out=Ps[s][0][0:TS], in_=tmpS[0:TS], func=AF.Exp), reads=[r_tmpS], writes=[Ps[s][1]])
                        P.op("pool", lambda e, s=s: e.tensor_tensor(
                            out=Ps[s][0][0:TS].rearrange("p (h q) -> p h q", q=16), in0=Ps[s][0][0:TS].rearrange("p (h q) -> p h q", q=16),
                            in1=mTb[0:TS, 0:TS].unsqueeze(1).to_broadcast([TS, 8, 16]), op=ALU.mult),
                            reads=[Ps[s][1], r_mTb], writes=[Ps[s][1]])
                        fns = []
                        banks = set()
                        for hl in range(8):
                            o_ap, b = psOs(hl)
                            banks.add(b)
                            fns.append(lambda e, hl=hl, o_ap=o_ap, s=s: e.matmul(o_ap, lhsT=Ps[s][0][0:TS, hl * 16:(hl + 1) * 16],
                                                                                rhs=vs[0:TS, hh * 8 + hl, 0:129], start=False, stop=True, skip_group_check=True))
                        P.op("pe", fns, reads=[Ps[s][1], r_vs], writes=[psr[b] for b in sorted(banks)])

            for hh in range(2):
                for kt in range(17):
                    s_iter(hh, kt)
                for hl in range(8):
                    o_ap, b = psOs(hl)
                    ri, r_ri = rinv[hl % 2]
                    hg = hh * 8 + hl
                    P.op("dve", lambda e, o_ap=o_ap, ri=ri: e.reciprocal(out=ri[0:TS], in_=o_ap[:, 128:129]), reads=[psr[b]], writes=[r_ri])
                    P.op("dve", lambda e, o_ap=o_ap, ri=ri, hg=hg: e.scalar_tensor_tensor(
                        out=Ysb[0:TS, hg * 128:(hg + 1) * 128], in0=o_ap[:, 0:128], scalar=ri[0:TS], in1=szs[0:TS, hg, :],
                        op0=ALU.mult, op1=ALU.mult), reads=[psr[b], r_ri, r_szs], writes=[r_Ysb])
            store([lambda e: e.dma_start(out=Ysc[SEQ:SEQ + TS, :], in_=Ysb[0:TS])], [r_Ysb], [YS[32]])

        for layer in range(n_layers):
            jl = layer // 2
            if layer % 2 == 0:
                if 'G' in PHASES:
                    phase_G(layer)
                if 'T' in PHASES:
                    phase_T(layer, g_w_out[jl])
            else:
                if 'F' in PHASES:
                    phase_F(layer)
                if 'A' in PHASES:
                    phase_A(layer)
                if 'S' in PHASES:
                    phase_S(layer)
                if 'T' in PHASES:
                    phase_T(layer, f_w_out[jl])

        P.emit(final_ops=stores)
    return nc


def _consts():
    c = np.zeros((128, 640), np.float32)
    i = np.arange(128)
    c[:, 0:128] = np.eye(128, dtype=np.float32)
    c[:, 128:256] = (i[:, None] > i[None, :])
    c[:, 256:384] = 1.0
    c[:, 384:512] = (i[:, None] <= i[None, :])
    c[:, 512:640] = ((i[:, None] // 64) <= (i[None, :] // 64))
    return c


_NC_CACHE = {}


def kernel(x_prompt, x_sample, cache_fox_k, cache_fox_v, cache_fox_logf, p_prompt, p_sample,
           norm_pre, norm_post, gmlp_w_in, gmlp_ln_g, gmlp_ln_b, gmlp_w_s, gmlp_b_s, gmlp_w_out,
           fox_w_in, fox_b_f, fox_w_out, ple_w_proj, ple_w_gate, _n_layers=4, _cores=8):
    f = lambda a: np.ascontiguousarray(np.asarray(a), dtype=np.float32)
    n = _cores
    if _n_layers not in _NC_CACHE:
        _NC_CACHE[_n_layers] = build(_n_layers)
    nc = _NC_CACHE[_n_layers]
    shared = {
        "norm_pre": f(norm_pre), "norm_post": f(norm_post),
        "g_w_in": f(gmlp_w_in), "g_ln_g": f(gmlp_ln_g), "g_ln_b": f(gmlp_ln_b), "g_w_s": f(gmlp_w_s),
        "g_b_s": f(gmlp_b_s), "g_w_out": f(gmlp_w_out), "f_w_in": f(fox_w_in), "f_b_f": f(fox_b_f),
        "f_w_out": f(fox_w_out), "ple_proj": f(ple_w_proj), "ple_gate": f(ple_w_gate), "consts": _consts(),
    }
    xp = np.asarray(x_prompt); xs = np.asarray(x_sample)
    ckk = np.asarray(cache_fox_k); cvv = np.asarray(cache_fox_v); clff = np.asarray(cache_fox_logf)
    pp = np.asarray(p_prompt); psm = np.asarray(p_sample)
    in_maps = []
    for b in range(n):
        m = dict(shared)
        m["x_p"] = f(xp[b]); m["x_s"] = f(xs[b])
        m["ck"] = f(ckk[:, b].reshape(2, PAST, E)); m["cv"] = f(cvv[:, b].reshape(2, PAST, E)); m["clf"] = f(clff[:, b])
        m["p_p"] = f(pp[:, b]); m["p_s"] = f(psm[:, b])
        in_maps.append(m)
    res = run_bass_kernel_spmd(nc, in_maps, core_ids=list(range(n)))
    R = res.results
    st = lambda k, ax: np.stack([np.asarray(R[b][k], dtype=np.float32) for b in range(n)], axis=ax)
    y_prompt = st("y_p", 0)
    y_sample = st("y_s", 0)
    gv = st("gv_s", 1)
    fk_p = st("fk_p", 1).reshape(2, n, SEQ, NH, DH)
    fv_p = st("fv_p", 1).reshape(2, n, SEQ, NH, DH)
    flf_p = st("flf_p", 1)
    fk_s = st("fk_s", 1).reshape(2, n, TS, NH, DH)
    fv_s = st("fv_s", 1).reshape(2, n, TS, NH, DH)
    flf_s = st("flf_s", 1)
    return (y_prompt, y_sample, gv, fk_p, fv_p, flf_p, fk_s, fv_s, flf_s)
```

```python
import contextlib
import numpy as np
import concourse.bass as bass
import concourse.mybir as mybir
from concourse.bass_utils import run_bass_kernel_spmd

F32 = mybir.dt.float32
BF16 = mybir.dt.bfloat16
AF = mybir.ActivationFunctionType
ALU = mybir.AluOpType

D = 1024
E = 2048
SEQ = 4096
TS = 16
PAST = 2048
NH = 16
DH = 128
PLE = 256
NT = 33
TOK = 4224
SCALE = float(DH) ** -0.5
RMS_EPS = 1e-6
LN_EPS = 1e-5
FW = 4 * E + NH


class Sem:
    def __init__(self, h):
        self.h = h
        self.count = 0
        self.last_op = None


class Op:
    def __init__(self, eng, fns, is_dma):
        self.eng = eng
        self.fns = fns
        self.deps = []
        self.needs_inc = False
        self.sem = None
        self.ticket = None
        self.is_dma = is_dma


class Res:
    def __init__(self, name, arena=None, lo=0, hi=0):
        self.name = name
        self.arena = arena
        self.lo = lo
        self.hi = hi
        self.writers = []
        self.readers = []
        self.overlaps = []


class Prog:
    ENGS = ("sync", "act", "dve", "pool", "pe")
    BLK = {"sync": "sync", "act": "scalar", "dve": "vector", "pool": "gpsimd", "pe": "tensor"}

    def __init__(self, nc, stack):
        self.nc = nc
        self.stack = stack
        self.ops = {e: [] for e in self.ENGS}
        self.eng_sem = {}
        self.pools = {}
        self.pool_idx = {}
        self.arena_res = {}
        self.nsem = 0
        self.new_epoch()

    def new_sem(self, name):
        self.nsem += 1
        return Sem(self.stack.enter_context(self.nc.semaphore(f"{name}_{self.nsem}")))

    def new_epoch(self):
        for e in ("act", "dve", "pool", "pe"):
            self.eng_sem[e] = self.new_sem("e_" + e)

    def dma_pool(self, name, n):
        self.pools[name] = [self.new_sem("d_" + name) for _ in range(n)]
        self.pool_idx[name] = 0

    def res(self, name, arena=None, lo=0, hi=0):
        r = Res(name, arena, lo, hi)
        if arena is not None:
            lst = self.arena_res.setdefault(arena, [])
            for o in lst:
                if o.lo < hi and lo < o.hi:
                    r.overlaps.append(o)
                    o.overlaps.append(r)
            lst.append(r)
        return r

    def _dep(self, op, prod, raw):
        if prod is op:
            return
        if (not prod.is_dma) and (not op.is_dma) and prod.eng == op.eng:
            if not raw:
                return
            if op.eng == "pe":
                return
        prod.needs_inc = True
        op.deps.append(prod)

    def op(self, eng, fns, reads=(), writes=(), pool=None):
        if not isinstance(fns, (list, tuple)):
            fns = [fns]
        is_dma = pool is not None
        o = Op(eng, list(fns), is_dma)
        if is_dma:
            ps = self.pools[pool]
            i = self.pool_idx[pool]
            self.pool_idx[pool] = (i + 1) % len(ps)
            o.sem = ps[i]
            if o.sem.last_op is not None:
                o.sem.last_op.needs_inc = True
                o.deps.append(o.sem.last_op)
            o.sem.last_op = o
        else:
            o.sem = self.eng_sem[eng]
        for r in reads:
            for rr in [r] + r.overlaps:
                for w in rr.writers:
                    self._dep(o, w, True)
        for r in writes:
            for rr in [r] + r.overlaps:
                for w in rr.writers:
                    self._dep(o, w, False)
                for w in rr.readers:
                    self._dep(o, w, False)
        for r in reads:
            r.readers.append(o)
        for r in writes:
            r.writers = [o]
            r.readers = []
        self.ops[eng].append(o)
        return o

    def emit(self, final_ops=()):
        nc = self.nc
        for o in final_ops:
            o.needs_inc = True
        for e in self.ENGS:
            for o in self.ops[e]:
                if o.is_dma:
                    o.sem.count += 16 * len(o.fns)
                    o.ticket = o.sem.count
                elif o.needs_inc:
                    o.sem.count += 1
                    o.ticket = o.sem.count

        import os as _os
        if _os.environ.get("KCHECK"):
            self.check()

        def need_of(deps):
            need = {}
            for d in deps:
                k = id(d.sem)
                if d.ticket > need.get(k, (None, 0))[1]:
                    need[k] = (d.sem, d.ticket)
            return need

        with nc.Block() as block:
            for e in self.ENGS:
                ops = self.ops[e]
                if not ops and e != "sync":
                    continue

                def body(eng, ops=ops, e=e):
                    waited = {}
                    for o in ops:
                        for k, (s, v) in need_of(o.deps).items():
                            if waited.get(k, 0) < v:
                                eng.wait_ge(s.h, v)
                                waited[k] = v
                        n = len(o.fns)
                        for i, f in enumerate(o.fns):
                            ins = f(eng)
                            if o.is_dma:
                                ins.then_inc(o.sem.h, 16)
                            elif o.needs_inc and i == n - 1:
                                ins.then_inc(o.sem.h, 1)
                    if e == "sync":
                        for k, (s, v) in need_of(final_ops).items():
                            if waited.get(k, 0) < v:
                                eng.wait_ge(s.h, v)
                                waited[k] = v

                getattr(block, self.BLK[e])(body)


def _prog_check(self):
    pos = {e: 0 for e in self.ENGS}
    done = set()
    semval = {}
    total = sum(len(v) for v in self.ops.values())
    ndone = 0
    progress = True
    while progress:
        progress = False
        for e in self.ENGS:
            while pos[e] < len(self.ops[e]):
                o = self.ops[e][pos[e]]
                ok = True
                for d in o.deps:
                    if d.ticket is None:
                        raise RuntimeError("dep without ticket")
                    if semval.get(id(d.sem), 0) < d.ticket:
                        ok = False
                        break
                if not ok:
                    break
                if o.ticket is not None:
                    prev = semval.get(id(o.sem), 0)
                    exp = o.ticket - (16 * len(o.fns) if o.is_dma else 1)
                    if prev != exp:
                        raise RuntimeError(f"ticket order violation on {e}: prev={prev} exp={exp}")
                    semval[id(o.sem)] = o.ticket
                pos[e] += 1
                ndone += 1
                progress = True
    print("CHECK: done", ndone, "of", total, {e: (pos[e], len(self.ops[e])) for e in self.ENGS})
    if ndone != total:
        raise RuntimeError("DEADLOCK in abstract simulation")


Prog.check = _prog_check


class Arena:
    def __init__(self, P, name, ap2d_bf16, nbytes):
        self.P = P
        self.name = name
        self.ap = ap2d_bf16
        self.nbytes = nbytes
        self.off = 0

    def reset(self, off):
        self.off = off

    def alloc(self, name, free_shape, dtype):
        esz = 4 if dtype == F32 else 2
        n = int(np.prod(free_shape))
        nb = n * esz
        lo = (self.off + 63) // 64 * 64
        hi = lo + nb
        assert hi <= self.nbytes, f"arena overflow {name}: {hi} > {self.nbytes}"
        self.off = hi
        v = self.ap[:, lo // 2:hi // 2]
        if dtype == F32:
            v = v.bitcast(F32)
        if len(free_shape) == 2:
            v = v.rearrange("p (a b) -> p a b", b=free_shape[1])
        elif len(free_shape) == 3:
            v = v.rearrange("p (a b c) -> p a b c", b=free_shape[1], c=free_shape[2])
        r = self.P.res(name, self.name, lo, hi)
        return v, r


def run_stages(stages, tiles):
    K = len(stages)
    n = len(tiles)
    for step in range(n + K - 1):
        for k in range(K - 1, -1, -1):
            i = step - k
            if 0 <= i < n:
                stages[k](tiles[i])


def nrows(t):
    return 128 if t < 32 else TS


ARENA_BYTES = 207 * 1024


def build(n_layers=4, dbg=None):
    dbg = dbg or {}
    TILES = dbg.get('tiles', list(range(NT)))
    PHASES = dbg.get('phases', 'GTFAS')
    nc = bass.Bass("TRN2", target_bir_lowering=False)

    def din(name, shape, dt=F32):
        return nc.dram_tensor(name, shape, dt, kind="ExternalInput").ap()

    def dout(name, shape, dt=F32):
        return nc.dram_tensor(name, shape, dt, kind="ExternalOutput").ap()

    def dscr(name, shape, dt):
        return nc.dram_tensor(name, shape, dt, kind="Internal").ap()

    x_p = din("x_p", [SEQ, D]); x_s = din("x_s", [TS, D])
    ck = din("ck", [2, PAST, E]); cv = din("cv", [2, PAST, E]); clf = din("clf", [2, PAST, NH])
    p_p = din("p_p", [4, SEQ, PLE]); p_s = din("p_s", [4, TS, PLE])
    norm_pre = din("norm_pre", [4, D]); norm_post = din("norm_post", [4, D])
    g_w_in = din("g_w_in", [2, D, 3 * E]); g_ln_g = din("g_ln_g", [2, E]); g_ln_b = din("g_ln_b", [2, E])
    g_w_s = din("g_w_s", [2, 16, 128, 128]); g_b_s = din("g_b_s", [2, 16, 128]); g_w_out = din("g_w_out", [2, E, D])
    f_w_in = din("f_w_in", [2, D, FW]); f_b_f = din("f_b_f", [2, NH]); f_w_out = din("f_w_out", [2, E, D])
    ple_proj = din("ple_proj", [4, PLE, D]); ple_gate = din("ple_gate", [4, D, D])
    consts = din("consts", [128, 640])

    y_p = dout("y_p", [SEQ, D]); y_s = dout("y_s", [TS, D]); gv_s = dout("gv_s", [2, TS, E])
    fk_p = dout("fk_p", [2, SEQ, E]); fv_p = dout("fv_p", [2, SEQ, E]); flf_p = dout("flf_p", [2, SEQ, NH])
    fk_s = dout("fk_s", [2, TS, E]); fv_s = dout("fv_s", [2, TS, E]); flf_s = dout("flf_s", [2, TS, NH])

    Ysc = dscr("Ysc", [TOK, E], BF16)
    QTs = dscr("QTs", [NH, DH, TOK], BF16)
    KTs = dscr("KTs", [NH, DH, TOK], BF16)
    Vbs = dscr("Vbs", [NH, TOK, DH], BF16)
    SZs = dscr("SZs", [NH, TOK, DH], F32)

    with contextlib.ExitStack() as st:
        P = Prog(nc, st)
        P.dma_pool("ld", 6)
        P.dma_pool("st", 8)
        P.dma_pool("w", 4)
        ar_t = st.enter_context(nc.sbuf_tensor("arena", [128, ARENA_BYTES // 2], BF16))
        A = Arena(P, "sb", ar_t[:], ARENA_BYTES)
        ps_t = [st.enter_context(nc.psum_tensor(f"ps{i}", [128, 1024], F32)) for i in range(4)]
        psr = [P.res(f"bank{i}", "psum", i, i + 1) for i in range(8)]

        def bank(i):
            return ps_t[i // 2][:, (i % 2) * 512:(i % 2 + 1) * 512]

        def bank_bf(i):
            return ps_t[i // 2][:].bitcast(BF16)[:, (i % 2) * 1024:(i % 2 + 1) * 1024]

        def pair(i):
            return ps_t[i][:]

        def pair_bf(i):
            return ps_t[i][:].bitcast(BF16)

        XR = [P.res(f"xr{t}") for t in range(NT)]
        YS = [P.res(f"ysc{t}") for t in range(NT)]
        QTr = [P.res(f"qts{t}") for t in range(NT)]
        KTr = [P.res(f"kts{t}") for t in range(NT)]
        VBr = [P.res(f"vbs{t}") for t in range(NT)]
        SZr = [P.res(f"szs{t}") for t in range(NT)]
        LFSr = P.res("flf_s")
        OUTr = P.res("outs")
        stores = []

        def store(fns, reads, writes):
            o = P.op("sync", fns, reads=reads, writes=writes, pool="st")
            stores.append(o)
            return o

        def load(fns, reads, writes):
            return P.op("sync", fns, reads=reads, writes=writes, pool="ld")

        def wload(fns, writes):
            return P.op("pool", fns, reads=[], writes=writes, pool="w")

        def xsrc(layer, t):
            nr = nrows(t)
            if layer == 0:
                return x_p[t * 128:(t + 1) * 128, :] if t < 32 else x_s[0:nr, :]
            return y_p[t * 128:(t + 1) * 128, :] if t < 32 else y_s[0:nr, :]

        def xdst(t):
            return y_p[t * 128:(t + 1) * 128, :] if t < 32 else y_s[0:TS, :]

        def psrc(layer, t):
            return p_p[layer, t * 128:(t + 1) * 128, :] if t < 32 else p_s[layer, 0:TS, :]

        cs, r_cs = A.alloc("cs", [640], F32)
        idb, r_idb = A.alloc("idb", [128], BF16)
        mTb, r_mTb = A.alloc("mTb", [128], BF16)
        nh, r_nh = A.alloc("nh", [1], F32)
        ETab, r_ETab = A.alloc("ETab", [32, 16], F32)
        TcTab, r_TcTab = A.alloc("TcTab", [32, 16], F32)
        ident = cs[:, 0:128]
        tstr = cs[:, 128:256]
        ones = cs[:, 256:384]
        maskT = cs[:, 384:512]
        maskC = cs[:, 512:640]
        PH0 = A.off

        load([lambda e: e.dma_start(out=cs, in_=consts[:, :])], [], [r_cs])
        P.op("dve", lambda e: e.tensor_copy(out=idb, in_=ident), reads=[r_cs], writes=[r_idb])
        P.op("dve", lambda e: e.tensor_copy(out=mTb, in_=maskT), reads=[r_cs], writes=[r_mTb])
        P.op("pool", lambda e: e.memset(nh, -0.5), writes=[r_nh])

        def rstd_ops(ss, r_ss, v1, r_v1, rstd, r_rstd, nr, mul, eps):
            P.op("pool", lambda e: e.tensor_scalar(out=v1[0:nr], in0=ss[0:nr], scalar1=mul, scalar2=eps,
                                                    op0=ALU.mult, op1=ALU.add), reads=[r_ss], writes=[r_v1])
            P.op("pool", lambda e: e.tensor_tensor(out=rstd[0:nr], in0=v1[0:nr], in1=nh[0:nr], op=ALU.pow),
                 reads=[r_v1, r_nh], writes=[r_rstd])

        def head_a(B, t, xt, r_xt):
            nr = nrows(t)
            s = t % 2
            junk, r_junk = B["junk"]
            ss, r_ss = B["ss"][s]
            v1, r_v1 = B["v1"][s]
            rstd, r_rstd = B["rstd"][s]
            hb, r_hb = B["hb"]
            hT, r_hT = B["hT"][s]
            gpre, r_gpre = B["gpre"]
            P.op("act", lambda e: e.activation(out=junk[0:nr], in_=xt[0:nr], func=AF.Square, accum_out=ss[0:nr]),
                 reads=[r_xt], writes=[r_junk, r_ss])
            rstd_ops(ss, r_ss, v1, r_v1, rstd, r_rstd, nr, 1.0 / D, RMS_EPS)
            P.op("dve", lambda e: e.scalar_tensor_tensor(out=hb[0:nr], in0=xt[0:nr], scalar=rstd[0:nr], in1=gpre[0:nr],
                                                          op0=ALU.mult, op1=ALU.mult),
                 reads=[r_xt, r_rstd, r_gpre], writes=[r_hb])

        def head_b(B, t, tb):
            nr = nrows(t)
            s = t % 2
            hb, r_hb = B["hb"]
            hT, r_hT = B["hT"][s]
            pT = bank_bf(tb)
            P.op("pe", [(lambda e, c=c: e.transpose(out=pT[:, c * 128:c * 128 + nr], in_=hb[0:nr, c * 128:(c + 1) * 128],
                                                    identity=idb[0:nr, 0:nr])) for c in range(8)],
                 reads=[r_hb, r_idb], writes=[psr[tb]])
            P.op("act", lambda e: e.activation(out=hT[:, :, 0:nr], in_=pT.rearrange("p (a b) -> p a b", b=128)[:, :, 0:nr],
                                               func=AF.Copy), reads=[psr[tb]], writes=[r_hT])
            return hT, r_hT

        def common_small(B):
            B["junk"] = A.alloc("junk", [1024], BF16)
            for k in ("ss", "v1", "rstd"):
                B[k] = [A.alloc(f"{k}{i}", [1], F32) for i in range(2)]

        def phase_G(layer):
            jl = layer // 2
            P.new_epoch()
            A.reset(PH0)
            B = {}
            Win, r_Win = A.alloc("Win", [8, 3 * E], BF16)
            wsT, r_wsT = A.alloc("wsT", [16, 128], BF16)
            lng, r_lng = A.alloc("lng", [E], F32)
            lnb, r_lnb = A.alloc("lnb", [E], F32)
            bsT, r_bsT = A.alloc("bsT", [16], F32)
            B["gpre"] = A.alloc("gpre", [D], F32)
            gpre, r_gpre = B["gpre"]
            common_small(B)
            xts = [A.alloc(f"xt{i}", [D], F32) for i in range(2)]
            B["hb"] = A.alloc("hb", [D], BF16)
            B["hT"] = [A.alloc(f"hT{i}", [8, 128], BF16) for i in range(2)]
            us = [A.alloc(f"u{i}", [E], F32) for i in range(2)]
            u, r_u = us[0]
            vg, r_vg = A.alloc("vg", [E], F32)
            szs_ = [A.alloc(f"sz{i}", [E], F32) for i in range(2)]
            vb, r_vb = A.alloc("vb", [E], BF16)
            ys = [A.alloc(f"y{i}", [E], BF16) for i in range(2)]
            stats, r_stats = A.alloc("stats", [4, 6], F32)
            mv, r_mv = A.alloc("mv", [2], F32)
            v2, r_v2 = A.alloc("v2", [1], F32)
            rs2, r_rs2 = A.alloc("rs2", [1], F32)

            wload([(lambda e, c=c: e.dma_start(out=Win[:, c, :], in_=g_w_in[jl, c * 128:(c + 1) * 128, :])) for c in range(8)],
                  [r_Win])
            load([lambda e: e.dma_start(out=lng, in_=g_ln_g[jl:jl + 1, :].to_broadcast([128, E])),
                  lambda e: e.dma_start(out=lnb, in_=g_ln_b[jl:jl + 1, :].to_broadcast([128, E])),
                  lambda e: e.dma_start(out=gpre, in_=norm_pre[layer:layer + 1, :].to_broadcast([128, D])),
                  lambda e: e.dma_start(out=bsT, in_=g_b_s[jl].rearrange("g i -> i g"), allow_slow_non_contiguous=True)],
                 [], [r_lng, r_lnb, r_gpre, r_bsT])
            wst = u.rearrange("p (g j) -> p g j", j=128)
            load([lambda e: e.dma_start(out=wst, in_=g_w_s[jl].rearrange("g i j -> i g j"))], [], [r_u])
            for hf in range(2):
                pp_ = pair(hf)
                P.op("pe", [(lambda e, g=g, pp_=pp_: e.transpose(out=pp_[:, (g % 8) * 128:(g % 8 + 1) * 128], in_=wst[:, g, :], identity=ident))
                            for g in range(hf * 8, hf * 8 + 8)], reads=[r_u, r_cs], writes=[psr[2 * hf], psr[2 * hf + 1]])
                P.op("dve", lambda e, hf=hf, pp_=pp_: e.tensor_tensor(
                    out=wsT[:, hf * 8:(hf + 1) * 8, :], in0=pp_.rearrange("p (g i) -> p g i", i=128),
                    in1=maskC.unsqueeze(1).to_broadcast([128, 8, 128]), op=ALU.mult),
                    reads=[psr[2 * hf], psr[2 * hf + 1], r_cs], writes=[r_wsT])

            def sL(t):
                nr = nrows(t)
                xt, r_xt = xts[t % 2]
                load([lambda e: e.dma_start(out=xt[0:nr], in_=xsrc(layer, t))], [XR[t]], [r_xt])

            def s0a(t):
                xt, r_xt = xts[t % 2]
                head_a(B, t, xt, r_xt)

            def s0b(t):
                head_b(B, t, 0)

            prev_tile = [None]

            def s1(t):
                nr = nrows(t)
                hT, r_hT = B["hT"][t % 2]
                u, r_u = us[t % 2]
                sz, r_sz = szs_[t % 2]
                for n in range(12):
                    if n == 6 and prev_tile[0] is not None:
                        s2(prev_tile[0])
                    bk = 1 + (n % 3)
                    P.op("pe", [(lambda e, c=c, n=n, bk=bk: e.matmul(bank(bk)[0:nr, :], lhsT=hT[:, c, 0:nr],
                                                                      rhs=Win[:, c, n * 512:(n + 1) * 512],
                                                                      start=(c == 0), stop=(c == 7))) for c in range(8)],
                         reads=[r_hT, r_Win], writes=[psr[bk]])
                    if n < 4:
                        dst, r_dst, fn = u, r_u, AF.Gelu_apprx_tanh
                    elif n < 8:
                        dst, r_dst, fn = vg, r_vg, AF.Gelu_apprx_tanh
                    else:
                        dst, r_dst, fn = sz, r_sz, AF.Silu
                    c0 = (n % 4) * 512
                    P.op("act", lambda e, dst=dst, fn=fn, c0=c0, bk=bk: e.activation(out=dst[0:nr, c0:c0 + 512], in_=bank(bk)[0:nr, :], func=fn),
                         reads=[psr[bk]], writes=[r_dst])
                    if n == 7:
                        P.op("dve", [(lambda e, q=q: e.bn_stats(out=stats[0:nr, q, :], in_=vg[0:nr, q * 512:(q + 1) * 512])) for q in range(4)],
                             reads=[r_vg], writes=[r_stats])
                        P.op("dve", lambda e: e.bn_aggr(out=mv[0:nr], in_=stats[0:nr].rearrange("p a b -> p (a b)")),
                             reads=[r_stats], writes=[r_mv])
                        P.op("pool", lambda e: e.tensor_scalar(out=v2[0:nr], in0=mv[0:nr, 1:2], scalar1=1.0, scalar2=LN_EPS,
                                                                op0=ALU.mult, op1=ALU.add), reads=[r_mv], writes=[r_v2])
                        P.op("pool", lambda e: e.tensor_tensor(out=rs2[0:nr], in0=v2[0:nr], in1=nh[0:nr], op=ALU.pow),
                             reads=[r_v2, r_nh], writes=[r_rs2])
                        P.op("dve", lambda e: e.tensor_scalar(out=vg[0:nr], in0=vg[0:nr], scalar1=mv[0:nr, 0:1], scalar2=rs2[0:nr],
                                                               op0=ALU.subtract, op1=ALU.mult), reads=[r_vg, r_mv, r_rs2], writes=[r_vg])
                        P.op("pool", lambda e: e.tensor_tensor(out=vg[0:nr], in0=vg[0:nr], in1=lng[0:nr], op=ALU.mult),
                             reads=[r_vg, r_lng], writes=[r_vg])
                        if t < 32:
                            P.op("dve", lambda e: e.tensor_tensor(out=vb[0:nr], in0=vg[0:nr], in1=lnb[0:nr], op=ALU.add),
                                 reads=[r_vg, r_lnb], writes=[r_vb])
                        else:
                            P.op("dve", lambda e: e.tensor_tensor(out=vg[0:nr], in0=vg[0:nr], in1=lnb[0:nr], op=ALU.add),
                                 reads=[r_vg, r_lnb], writes=[r_vg])
                            P.op("dve", lambda e: e.tensor_copy(out=vb[0:nr], in_=vg[0:nr]), reads=[r_vg], writes=[r_vb])
                            store([lambda e: e.dma_start(out=gv_s[jl, :, :], in_=vg[0:nr])], [r_vg], [])
                prev_tile[0] = t

            def s2(t):
                nr = nrows(t)
                y, r_y = ys[t % 2]
                u, r_u = us[t % 2]
                sz, r_sz = szs_[t % 2]
                P.op("pe", [(lambda e, g=g: e.matmul(bank(4 + g // 4)[0:nr, (g % 4) * 128:(g % 4 + 1) * 128],
                                                     lhsT=wsT[0:nr, g, 0:nr], rhs=vb[0:nr, g * 128:(g + 1) * 128],
                                                     start=True, stop=True)) for g in range(16)],
                     reads=[r_wsT, r_vb], writes=[psr[4], psr[5], psr[6], psr[7]])
                P.op("dve", [(lambda e, g=g: e.scalar_tensor_tensor(
                    out=u[0:nr, g * 128:(g + 1) * 128], in0=bank(4 + g // 4)[0:nr, (g % 4) * 128:(g % 4 + 1) * 128],
                    scalar=bsT[0:nr, g:g + 1], in1=u[0:nr, g * 128:(g + 1) * 128], op0=ALU.add, op1=ALU.mult)) for g in range(16)],
                    reads=[psr[4], psr[5], psr[6], psr[7], r_bsT, r_u], writes=[r_u])
                P.op("dve", lambda e: e.tensor_tensor(out=y[0:nr], in0=u[0:nr], in1=sz[0:nr], op=ALU.mult),
                     reads=[r_u, r_sz], writes=[r_y])
                store([lambda e: e.dma_start(out=Ysc[t * 128:t * 128 + nr, :], in_=y[0:nr])], [r_y], [YS[t]])

            run_stages([sL, s0a, s0b, s1], TILES)
            s2(prev_tile[0])

        def phase_T(layer, w_out_ap):
            P.new_epoch()
            A.reset(PH0)
            Wout, r_Wout = A.alloc("Wout", [16, D], BF16)
            Wg, r_Wg = A.alloc("Wg", [8, D], BF16)
            Wp, r_Wp = A.alloc("Wp", [2, D], BF16)
            gpost, r_gpost = A.alloc("gpost", [D], F32)
            junk, r_junk = A.alloc("junk", [D], BF16)
            NS = 4
            xts = [A.alloc(f"xt{i}", [D], F32) for i in range(NS)]
            yts = [A.alloc(f"yt{i}", [E], BF16) for i in range(NS)]
            pts = [A.alloc(f"pt{i}", [PLE], F32) for i in range(NS)]
            yTs = [A.alloc(f"yT{i}", [16, 128], BF16) for i in range(2)]
            sss = [A.alloc(f"ss{i}", [1], F32) for i in range(2)]
            v1s = [A.alloc(f"v1{i}", [1], F32) for i in range(2)]
            rss = [A.alloc(f"rstd{i}", [1], F32) for i in range(2)]
            tmp, r_tmp = A.alloc("tmp", [D], F32)
            x1s = [A.alloc(f"x1{i}", [D], F32) for i in range(2)]
            x1b, r_x1b = A.alloc("x1b", [D], BF16)
            x1T, r_x1T = A.alloc("x1T", [8, 128], BF16)
            sg, r_sg = A.alloc("sg", [D], F32)
            pb, r_pb = A.alloc("pb", [PLE], BF16)
            pT, r_pT = A.alloc("pT", [2, 128], BF16)
            x2s = [A.alloc(f"x2{i}", [D], F32) for i in range(2)]

            wload([(lambda e, q=q: e.dma_start(out=Wout[:, q * 4:(q + 1) * 4, :],
                                               in_=w_out_ap[q * 512:(q + 1) * 512, :].rearrange("(e p) n -> p e n", p=128)))
                   for q in range(4)], [r_Wout])
            wload([(lambda e, q=q: e.dma_start(out=Wg[:, q * 4:(q + 1) * 4, :],
                                               in_=ple_gate[layer, q * 512:(q + 1) * 512, :].rearrange("(e p) n -> p e n", p=128)))
                   for q in range(2)] +
                  [lambda e: e.dma_start(out=Wp, in_=ple_proj[layer].rearrange("(e p) n -> p e n", p=128))], [r_Wg, r_Wp])
            load([lambda e: e.dma_start(out=gpost, in_=norm_post[layer:layer + 1, :].to_broadcast([128, D]))], [], [r_gpost])

            o_sb, r_o_sb = A.alloc("o_sb", [D], F32)

            def L(t):
                nr = nrows(t)
                xt, r_xt = xts[t % NS]
                yt, r_yt = yts[t % NS]
                pt, r_pt = pts[t % NS]
                load([lambda e: e.dma_start(out=xt[0:nr], in_=xsrc(layer, t)),
                      lambda e: e.dma_start(out=yt[0:nr], in_=Ysc[t * 128:t * 128 + nr, :]),
                      lambda e: e.dma_start(out=pt[0:nr], in_=psrc(layer, t))],
                     [XR[t], YS[t]], [r_xt, r_yt, r_pt])

            def a_ytr(t):
                nr = nrows(t)
                yt, r_yt = yts[t % NS]
                yT, r_yT = yTs[t % 2]
                pTb = pair_bf(0)
                P.op("pe", [(lambda e, c=c: e.transpose(out=pTb[:, c * 128:c * 128 + nr], in_=yt[0:nr, c * 128:(c + 1) * 128],
                                                        identity=idb[0:nr, 0:nr])) for c in range(16)],
                     reads=[r_yt, r_idb], writes=[psr[0], psr[1]])
                P.op("act", lambda e: e.activation(out=yT[:, :, 0:nr], in_=pTb.rearrange("p (a b) -> p a b", b=128)[:, :, 0:nr], func=AF.Copy),
                     reads=[psr[0], psr[1]], writes=[r_yT])

            def a_o(t):
                nr = nrows(t)
                yT, r_yT = yTs[t % 2]
                po = pair(1)
                P.op("pe", [(lambda e, n=n, c=c: e.matmul(po[0:nr, n * 512:(n + 1) * 512], lhsT=yT[:, c, 0:nr],
                                                          rhs=Wout[:, c, n * 512:(n + 1) * 512], start=(c == 0), stop=(c == 15)))
                            for n in range(2) for c in range(16)],
                     reads=[r_yT, r_Wout], writes=[psr[2], psr[3]])
                P.op("act", lambda e: e.activation(out=o_sb[0:nr], in_=po[0:nr], func=AF.Copy), reads=[psr[2], psr[3]], writes=[r_o_sb])

            def b_chain(t):
                nr = nrows(t)
                s = t % 2
                xt, r_xt = xts[t % NS]
                pt, r_pt = pts[t % NS]
                ss, r_ss = sss[s]
                v1, r_v1 = v1s[s]
                rstd, r_rstd = rss[s]
                x1, r_x1 = x1s[s]
                P.op("act", lambda e: e.activation(out=junk[0:nr], in_=o_sb[0:nr], func=AF.Square, accum_out=ss[0:nr]),
                     reads=[r_o_sb], writes=[r_junk, r_ss])
                rstd_ops(ss, r_ss, v1, r_v1, rstd, r_rstd, nr, 1.0 / D, RMS_EPS)
                P.op("dve", lambda e: e.scalar_tensor_tensor(out=tmp[0:nr], in0=o_sb[0:nr], scalar=rstd[0:nr], in1=gpost[0:nr],
                                                              op0=ALU.mult, op1=ALU.mult),
                     reads=[r_o_sb, r_rstd, r_gpost], writes=[r_tmp])
                P.op("dve", lambda e: e.tensor_tensor(out=x1[0:nr], in0=tmp[0:nr], in1=xt[0:nr], op=ALU.add),
                     reads=[r_tmp, r_xt], writes=[r_x1])
                P.op("dve", lambda e: e.tensor_copy(out=x1b[0:nr], in_=x1[0:nr]), reads=[r_x1], writes=[r_x1b])
                P.op("dve", lambda e: e.tensor_copy(out=pb[0:nr], in_=pt[0:nr]), reads=[r_pt], writes=[r_pb])

            def c_tr(t):
                nr = nrows(t)
                pTb = bank_bf(4)
                P.op("pe", [(lambda e, c=c: e.transpose(out=pTb[:, c * 128:c * 128 + nr], in_=x1b[0:nr, c * 128:(c + 1) * 128],
                                                        identity=idb[0:nr, 0:nr])) for c in range(8)],
                     reads=[r_x1b, r_idb], writes=[psr[4]])
                P.op("act", lambda e: e.activation(out=x1T[:, :, 0:nr], in_=pTb.rearrange("p (a b) -> p a b", b=128)[:, :, 0:nr], func=AF.Copy),
                     reads=[psr[4]], writes=[r_x1T])
                pPb = bank_bf(5)
                P.op("pe", [(lambda e, c=c: e.transpose(out=pPb[:, c * 128:c * 128 + nr], in_=pb[0:nr, c * 128:(c + 1) * 128],
                                                        identity=idb[0:nr, 0:nr])) for c in range(2)],
                     reads=[r_pb, r_idb], writes=[psr[5]])
                P.op("act", lambda e: e.activation(out=pT[:, :, 0:nr], in_=pPb[:, 0:256].rearrange("p (a b) -> p a b", b=128)[:, :, 0:nr], func=AF.Copy),
                     reads=[psr[5]], writes=[r_pT])

            def c_gate(t):
                nr = nrows(t)
                pg = pair(3)
                P.op("pe", [(lambda e, n=n, c=c: e.matmul(pg[0:nr, n * 512:(n + 1) * 512], lhsT=x1T[:, c, 0:nr],
                                                          rhs=Wg[:, c, n * 512:(n + 1) * 512], start=(c == 0), stop=(c == 7)))
                            for n in range(2) for c in range(8)],
                     reads=[r_x1T, r_Wg], writes=[psr[6], psr[7]])

            def d_sig(t):
                nr = nrows(t)
                pg = pair(3)
                P.op("act", lambda e: e.activation(out=sg[0:nr], in_=pg[0:nr], func=AF.Sigmoid), reads=[psr[6], psr[7]], writes=[r_sg])

            def d_pp(t):
                nr = nrows(t)
                s = t % 2
                x1, r_x1 = x1s[s]
                x2, r_x2 = x2s[s]
                pg = pair(3)
                P.op("pe", [(lambda e, n=n, c=c: e.matmul(pg[0:nr, n * 512:(n + 1) * 512], lhsT=pT[:, c, 0:nr],
                                                          rhs=Wp[:, c, n * 512:(n + 1) * 512], start=(c == 0), stop=(c == 1)))
                            for n in range(2) for c in range(2)],
                     reads=[r_pT, r_Wp], writes=[psr[6], psr[7]])
                P.op("dve", lambda e: e.tensor_tensor(out=tmp[0:nr], in0=pg[0:nr], in1=sg[0:nr], op=ALU.mult),
                     reads=[r_sg, psr[6], psr[7]], writes=[r_tmp])
                P.op("dve", lambda e: e.tensor_tensor(out=x2[0:nr], in0=tmp[0:nr], in1=x1[0:nr], op=ALU.add),
                     reads=[r_tmp, r_x1], writes=[r_x2])
                store([lambda e: e.dma_start(out=xdst(t), in_=x2[0:nr])], [r_x2], [XR[t]])

            n_t = len(TILES)
            for step in range(n_t + 4):
                def tl(k):
                    i = step - k
                    return TILES[i] if 0 <= i < n_t else None
                tD, tC, tB, tA, tL = tl(4), tl(3), tl(2), tl(1), tl(0)
                if tD is not None:
                    d_sig(tD)
                if tA is not None:
                    a_ytr(tA)
                if tD is not None:
                    d_pp(tD)
                if tC is not None:
                    c_tr(tC)
                if tB is not None:
                    b_chain(tB)
                if tA is not None:
                    a_o(tA)
                if tC is not None:
                    c_gate(tC)
                if tL is not None:
                    L(tL)

        def phase_F(layer):
            jl = layer // 2
            P.new_epoch()
            A.reset(PH0)
            B = {}
            Wf, r_Wf = A.alloc("Wf", [8, FW], BF16)
            B["gpre"] = A.alloc("gpre", [D], F32)
            gpre, r_gpre = B["gpre"]
            bfb, r_bfb = A.alloc("bfb", [NH], F32)
            common_small(B)
            xts = [A.alloc(f"xt{i}", [D], F32) for i in range(2)]
            B["hb"] = A.alloc("hb", [D], BF16)
            B["hT"] = [A.alloc(f"hT{i}", [8, 128], BF16) for i in range(2)]
            qb, r_qb = A.alloc("qb", [E], BF16)
            kf, r_kf = A.alloc("kf", [E], F32)
            kb_, r_kb = A.alloc("kb", [E], BF16)
            vf, r_vf = A.alloc("vf", [E], F32)
            vb, r_vb = A.alloc("vb", [E], BF16)
            szt, r_szt = A.alloc("szt", [E], F32)
            qT, r_qT = A.alloc("qT", [16, 128], BF16)
            kT, r_kT = A.alloc("kT", [16, 128], BF16)
            t16, r_t16 = A.alloc("t16", [NH], F32)
            e16, r_e16 = A.alloc("e16", [NH], F32)
            l16, r_l16 = A.alloc("l16", [NH], F32)
            lfs_ = [A.alloc(f"lf{i}", [NH], F32) for i in range(2)]

            wload([(lambda e, c=c: e.dma_start(out=Wf[:, c, :], in_=f_w_in[jl, c * 128:(c + 1) * 128, :])) for c in range(8)], [r_Wf])
            load([lambda e: e.dma_start(out=gpre, in_=norm_pre[layer:layer + 1, :].to_broadcast([128, D])),
                  lambda e: e.dma_start(out=bfb, in_=f_b_f[jl:jl + 1, :].to_broadcast([128, NH]))], [], [r_gpre, r_bfb])

            def sL(t):
                nr = nrows(t)
                xt, r_xt = xts[t % 2]
                load([lambda e: e.dma_start(out=xt[0:nr], in_=xsrc(layer, t))], [XR[t]], [r_xt])

            def s0a(t):
                xt, r_xt = xts[t % 2]
                head_a(B, t, xt, r_xt)

            def s0b(t):
                head_b(B, t, 0)

            def s1(t):
                nr = nrows(t)
                r0 = t * 128
                hT, r_hT = B["hT"][t % 2]
                lf, r_lf = lfs_[t % 2]
                for n in range(17):
                    bk = 1 + (n % 3)
                    ncol = 512 if n < 16 else NH
                    P.op("pe", [(lambda e, c=c, n=n, bk=bk, ncol=ncol: e.matmul(bank(bk)[0:nr, 0:ncol], lhsT=hT[:, c, 0:nr],
                                                                                   rhs=Wf[:, c, n * 512:n * 512 + ncol],
                                                                                   start=(c == 0), stop=(c == 7))) for c in range(8)],
                         reads=[r_hT, r_Wf], writes=[psr[bk]])
                    c0 = (n % 4) * 512
                    if n < 4:
                        P.op("act", lambda e, c0=c0, bk=bk: e.activation(out=qb[0:nr, c0:c0 + 512], in_=bank(bk)[0:nr, :], func=AF.Copy),
                             reads=[psr[bk]], writes=[r_qb])
                    elif n < 8:
                        P.op("act", lambda e, c0=c0, bk=bk: e.activation(out=kf[0:nr, c0:c0 + 512], in_=bank(bk)[0:nr, :], func=AF.Copy),
                             reads=[psr[bk]], writes=[r_kf])
                        P.op("dve", lambda e, c0=c0, bk=bk: e.tensor_copy(out=kb_[0:nr, c0:c0 + 512], in_=bank(bk)[0:nr, :]),
                             reads=[], writes=[psr[bk], r_kb])
                    elif n < 12:
                        P.op("dve", lambda e, c0=c0, bk=bk: e.tensor_copy(out=vf[0:nr, c0:c0 + 512], in_=bank(bk)[0:nr, :]),
                             reads=[psr[bk]], writes=[r_vf])
                        P.op("act", lambda e, c0=c0, bk=bk: e.activation(out=vb[0:nr, c0:c0 + 512], in_=bank(bk)[0:nr, :], func=AF.Copy),
                             reads=[], writes=[psr[bk], r_vb])
                    elif n < 16:
                        P.op("act", lambda e, c0=c0, bk=bk: e.activation(out=szt[0:nr, c0:c0 + 512], in_=bank(bk)[0:nr, :], func=AF.Silu),
                             reads=[psr[bk]], writes=[r_szt])
                    else:
                        P.op("dve", lambda e, bk=bk: e.tensor_tensor(out=t16[0:nr], in0=bank(bk)[0:nr, 0:NH], in1=bfb[0:nr], op=ALU.add),
                             reads=[psr[bk], r_bfb], writes=[r_t16])
                    if n == 5:
                        pTb = pair_bf(2)
                        P.op("pe", [(lambda e, c=c: e.transpose(out=pTb[:, c * 128:c * 128 + nr], in_=qb[0:nr, c * 128:(c + 1) * 128],
                                                                identity=idb[0:nr, 0:nr])) for c in range(16)],
                             reads=[r_qb, r_idb], writes=[psr[4], psr[5]])
                        P.op("act", lambda e, pTb=pTb: e.activation(out=qT[:, :, 0:nr], in_=pTb.rearrange("p (a b) -> p a b", b=128)[:, :, 0:nr], func=AF.Copy),
                             reads=[psr[4], psr[5]], writes=[r_qT])
                        store([lambda e: e.dma_start(out=QTs[:, :, r0:r0 + nr].rearrange("h d r -> d h r"), in_=qT[:, :, 0:nr])],
                              [r_qT], [QTr[t]])
                    if n == 7:
                        if t < 32:
                            store([lambda e: e.dma_start(out=fk_p[jl, r0:r0 + nr, :], in_=kf[0:nr])], [r_kf], [])
                        else:
                            store([lambda e: e.dma_start(out=fk_s[jl, :, :], in_=kf[0:nr])], [r_kf], [])
                    if n == 9:
                        pTb2 = pair_bf(3)
                        P.op("pe", [(lambda e, c=c: e.transpose(out=pTb2[:, c * 128:c * 128 + nr], in_=kb_[0:nr, c * 128:(c + 1) * 128],
                                                                identity=idb[0:nr, 0:nr])) for c in range(16)],
                             reads=[r_kb, r_idb], writes=[psr[6], psr[7]])
                        P.op("act", lambda e, pTb2=pTb2: e.activation(out=kT[:, :, 0:nr], in_=pTb2.rearrange("p (a b) -> p a b", b=128)[:, :, 0:nr], func=AF.Copy),
                             reads=[psr[6], psr[7]], writes=[r_kT])
                        store([lambda e: e.dma_start(out=KTs[:, :, r0:r0 + nr].rearrange("h d r -> d h r"), in_=kT[:, :, 0:nr])],
                              [r_kT], [KTr[t]])
                    if n == 11:
                        if t < 32:
                            store([lambda e: e.dma_start(out=fv_p[jl, r0:r0 + nr, :], in_=vf[0:nr])], [r_vf], [])
                        else:
                            store([lambda e: e.dma_start(out=fv_s[jl, :, :], in_=vf[0:nr])], [r_vf], [])
                        store([lambda e: e.dma_start(out=Vbs[:, r0:r0 + nr, :].rearrange("h r e -> r h e"),
                                                     in_=vb[0:nr].rearrange("p (h e) -> p h e", e=DH))], [r_vb], [VBr[t]])
                    if n == 15:
                        store([lambda e: e.dma_start(out=SZs[:, r0:r0 + nr, :].rearrange("h r e -> r h e"),
                                                     in_=szt[0:nr].rearrange("p (h e) -> p h e", e=DH))], [r_szt], [SZr[t]])
                    if n == 16:
                        P.op("act", lambda e: e.activation(out=e16[0:nr], in_=t16[0:nr], func=AF.Exp, scale=-1.0), reads=[r_t16], writes=[r_e16])
                        P.op("act", lambda e: e.activation(out=l16[0:nr], in_=e16[0:nr], func=AF.Ln, bias=1.0, scale=1.0), reads=[r_e16], writes=[r_l16])
                        P.op("dve", lambda e: e.tensor_scalar(out=lf[0:nr], in0=l16[0:nr], scalar1=-1.0, scalar2=None, op0=ALU.mult),
                             reads=[r_l16], writes=[r_lf])
                        if t < 32:
                            store([lambda e: e.dma_start(out=flf_p[jl, r0:r0 + nr, :], in_=lf[0:nr])], [r_lf], [])
                        else:
                            store([lambda e: e.dma_start(out=flf_s[jl, :, :], in_=lf[0:nr])], [r_lf], [LFSr])

            def s2(t):
                if t >= 32:
                    return
                lf, r_lf = lfs_[t % 2]
                bk = 1 + (t % 3)
                P.op("pe", [lambda e: e.matmul(bank(bk)[:, 0:NH], lhsT=tstr, rhs=lf, start=True, stop=True),
                            lambda e: e.matmul(bank(bk)[:, NH:2 * NH], lhsT=ones, rhs=lf, start=True, stop=True)],
                     reads=[r_cs, r_lf], writes=[psr[bk]])
                if t == 0:
                    P.op("dve", lambda e: e.tensor_copy(out=TcTab[:, 0, :], in_=bank(bk)[:, NH:2 * NH]), reads=[psr[bk]], writes=[r_TcTab])
                else:
                    P.op("dve", lambda e: e.tensor_tensor(out=TcTab[:, t, :], in0=bank(bk)[:, NH:2 * NH], in1=TcTab[:, t - 1, :], op=ALU.add),
                         reads=[psr[bk], r_TcTab], writes=[r_TcTab])
                P.op("dve", lambda e: e.tensor_tensor(out=ETab[:, t, :], in0=bank(bk)[:, 0:NH], in1=TcTab[:, t, :], op=ALU.subtract),
                     reads=[psr[bk], r_TcTab], writes=[r_ETab])

            run_stages([sL, s0a, s0b, s1, s2], TILES)

        def phase_A(layer):
            P.new_epoch()
            A.reset(PH0)
            QTh = [A.alloc(f"QTh{i}", [SEQ], BF16) for i in range(2)]
            KTh = [A.alloc(f"KTh{i}", [SEQ], BF16) for i in range(2)]
            Vh = [A.alloc(f"Vh{i}", [32, 132], BF16) for i in range(2)]
            SZh = [A.alloc(f"SZh{i}", [32, 128], F32) for i in range(2)]
            Yh = [A.alloc(f"Yh{i}", [32, 128], BF16) for i in range(2)]
            bT = [A.alloc(f"bT{i}", [32, 8], F32) for i in range(2)]
            Pb = [A.alloc(f"Pb{i}", [512], BF16) for i in range(3)]
            rinv = [A.alloc(f"rinv{i}", [1], F32) for i in range(4)]
            for i in range(2):
                P.op("dve", lambda e, i=i: e.memset(Vh[i][0][:, :, 128:129], 1.0), writes=[Vh[i][1]])

            def load_head(h):
                s = h % 2
                load([lambda e: e.dma_start(out=QTh[s][0], in_=QTs[h, :, 0:SEQ]),
                      lambda e: e.dma_start(out=KTh[s][0], in_=KTs[h, :, 0:SEQ])],
                     QTr[0:32] + KTr[0:32], [QTh[s][1], KTh[s][1]])
                load([lambda e: e.dma_start(out=Vh[s][0][:, :, 0:128], in_=Vbs[h, 0:SEQ, :].rearrange("(t p) e -> p t e", p=128)),
                      lambda e: e.dma_start(out=SZh[s][0], in_=SZs[h, 0:SEQ, :].rearrange("(t p) e -> p t e", p=128))],
                     VBr[0:32] + SZr[0:32], [Vh[s][1], SZh[s][1]])
                for qg in range(8):
                    P.op("dve", lambda e, qg=qg: e.tensor_scalar(out=bT[s][0][:, :, qg], in0=ETab[:, :, h],
                                                                  scalar1=TcTab[:, 4 * qg + 3, h:h + 1], scalar2=None, op0=ALU.add),
                         reads=[r_ETab, r_TcTab], writes=[bT[s][1]])

            its = []
            for h in range(NH):
                for qg in range(8):
                    for kb in range(4 * qg + 4):
                        its.append((h, qg, kb))

            def psO(qg, qt):
                b = (qg % 2) * 2 + qt // 2
                return bank(b)[:, (qt % 2) * 256:(qt % 2) * 256 + 129], b

            def do_S(i):
                h, qg, kb = its[i]
                s = h % 2
                d = max(0, kb - 4 * qg)
                n = (4 - d) * 128
                q0 = (4 * qg + d) * 128
                bk = 4 + i % 3
                P.op("pe", lambda e: e.matmul(bank(bk)[:, 0:n], lhsT=KTh[s][0][:, kb * 128:(kb + 1) * 128],
                                              rhs=QTh[s][0][:, q0:q0 + n], start=True, stop=True),
                     reads=[KTh[s][1], QTh[s][1]], writes=[psr[bk]])

            def do_rest(i):
                h, qg, kb = its[i]
                s = h % 2
                d = max(0, kb - 4 * qg)
                n = (4 - d) * 128
                bk = 4 + i % 3
                pb_, r_pb = Pb[i % 3]
                P.op("act", lambda e: e.activation(out=pb_[:, 0:n], in_=bank(bk)[:, 0:n], func=AF.Exp,
                                                   bias=bT[s][0][:, kb, qg:qg + 1], scale=SCALE),
                     reads=[psr[bk], bT[s][1]], writes=[r_pb])
                if kb >= 4 * qg:
                    P.op("pool", lambda e: e.tensor_tensor(out=pb_[:, 0:128], in0=pb_[:, 0:128], in1=mTb, op=ALU.mult),
                         reads=[r_pb, r_mTb], writes=[r_pb])
                fns = []
                banks = set()
                for qt in range(d, 4):
                    o_ap, b = psO(qg, qt)
                    banks.add(b)
                    fns.append(lambda e, qt=qt, o_ap=o_ap: e.matmul(o_ap, lhsT=pb_[:, (qt - d) * 128:(qt - d + 1) * 128],
                                                                     rhs=Vh[s][0][:, kb, 0:129], start=(kb == 0 and qt % 2 == 0), stop=(kb == 4 * qg + qt),
                                                                     skip_group_check=True))
                P.op("pe", fns, reads=[r_pb, Vh[s][1]], writes=[psr[b] for b in sorted(banks)])
                if kb == 4 * qg + 3:
                    for qt in range(4):
                        o_ap, b = psO(qg, qt)
                        ri, r_ri = rinv[qt]
                        P.op("dve", lambda e, o_ap=o_ap, ri=ri: e.reciprocal(out=ri, in_=o_ap[:, 128:129]), reads=[psr[b]], writes=[r_ri])
                        P.op("dve", lambda e, o_ap=o_ap, ri=ri, qt=qt: e.scalar_tensor_tensor(
                            out=Yh[s][0][:, 4 * qg + qt, :], in0=o_ap[:, 0:128], scalar=ri, in1=SZh[s][0][:, 4 * qg + qt, :],
                            op0=ALU.mult, op1=ALU.mult), reads=[psr[b], r_ri, SZh[s][1]], writes=[Yh[s][1]])
                    if qg == 7:
                        store([lambda e: e.dma_start(out=Ysc[0:SEQ, h * 128:(h + 1) * 128].rearrange("(t p) e -> p t e", p=128), in_=Yh[s][0])],
                              [Yh[s][1]], YS[0:32])

            load_head(0)
            n_it = len(its)
            do_S(0)
            do_S(1)
            for i in range(n_it):
                h, qg, kb = its[i]
                if qg == 0 and kb == 0 and h + 1 < NH:
                    load_head(h + 1)
                if i + 2 < n_it:
                    do_S(i + 2)
                do_rest(i)

        def phase_S(layer):
            jl = layer // 2
            P.new_epoch()
            A.reset(PH0)
            qTs, r_qTs = A.alloc("qTs", [16, 16], BF16)
            kTs, r_kTs = A.alloc("kTs", [16, 16], BF16)
            vs, r_vs = A.alloc("vs", [16, 132], BF16)
            szs, r_szs = A.alloc("szs", [16, 128], F32)
            lfs, r_lfs = A.alloc("lfs", [NH], F32)
            lfc, r_lfc = A.alloc("lfc", [16, 16], F32)
            sfxc, r_sfxc = A.alloc("sfxc", [16, 16], F32)
            totc, r_totc = A.alloc("totc", [16, 16], F32)
            biasS, r_biasS = A.alloc("biasS", [16, 16], F32)
            biasN, r_biasN = A.alloc("biasN", [NH], F32)
            Rs = [A.alloc(f"R{i}", [NH], F32) for i in range(17)]
            kc = [A.alloc(f"kc{i}", [1024], F32) for i in range(2)]
            vc = [A.alloc(f"vc{i}", [1024], F32) for i in range(2)]
            kcT = [A.alloc(f"kcT{i}", [8, 128], BF16) for i in range(2)]
            Vc = [A.alloc(f"Vc{i}", [8, 132], BF16) for i in range(2)]
            tmpS, r_tmpS = A.alloc("tmpS", [128], F32)
            Ps = [A.alloc(f"Ps{i}", [128], BF16) for i in range(2)]
            rinv = [A.alloc(f"rinvs{i}", [1], F32) for i in range(2)]
            Ysb, r_Ysb = A.alloc("Ysb", [E], BF16)

            load([lambda e: e.dma_start(out=qTs, in_=QTs[:, :, SEQ:SEQ + TS].rearrange("h d r -> d h r")),
                  lambda e: e.dma_start(out=kTs, in_=KTs[:, :, SEQ:SEQ + TS].rearrange("h d r -> d h r")),
                  lambda e: e.dma_start(out=vs[0:TS, :, 0:128], in_=Vbs[:, SEQ:SEQ + TS, :].rearrange("h r e -> r h e")),
                  lambda e: e.dma_start(out=szs[0:TS], in_=SZs[:, SEQ:SEQ + TS, :].rearrange("h r e -> r h e")),
                  lambda e: e.dma_start(out=lfs[0:TS], in_=flf_s[jl, :, :]),
                  lambda e: e.dma_start(out=lfc, in_=clf[jl].rearrange("(t p) h -> p t h", p=128))],
                 [QTr[32], KTr[32], VBr[32], SZr[32], LFSr], [r_qTs, r_kTs, r_vs, r_szs, r_lfs, r_lfc])
            P.op("dve", lambda e: e.memset(vs[0:TS, :, 128:129], 1.0), writes=[r_vs])
            for i in range(2):
                P.op("dve", lambda e, i=i: e.memset(Vc[i][0][:, :, 128:129], 1.0), writes=[Vc[i][1]])
            lfc2 = lfc.rearrange("p a b -> p (a b)")
            P.op("pe", [lambda e: e.matmul(bank(6)[:, 0:256], lhsT=tstr, rhs=lfc2, start=True, stop=True),
                        lambda e: e.matmul(bank(6)[:, 256:512], lhsT=ones, rhs=lfc2, start=True, stop=True),
                        lambda e: e.matmul(bank(7)[0:TS, 0:NH], lhsT=tstr[0:TS, 0:TS], rhs=lfs[0:TS], start=True, stop=True),
                        lambda e: e.matmul(bank(7)[:, NH:2 * NH], lhsT=ones[0:TS, :], rhs=lfs[0:TS], start=True, stop=True)],
                 reads=[r_cs, r_lfc, r_lfs], writes=[psr[6], psr[7]])
            P.op("dve", lambda e: e.tensor_copy(out=sfxc.rearrange("p a b -> p (a b)"), in_=bank(6)[:, 0:256]), reads=[psr[6]], writes=[r_sfxc])
            P.op("dve", lambda e: e.tensor_copy(out=totc.rearrange("p a b -> p (a b)"), in_=bank(6)[:, 256:512]), reads=[psr[6]], writes=[r_totc])
            P.op("dve", lambda e: e.tensor_copy(out=biasN[0:TS], in_=bank(7)[0:TS, 0:NH]), reads=[psr[7]], writes=[r_biasN])
            P.op("dve", lambda e: e.tensor_copy(out=Rs[16][0], in_=bank(7)[:, NH:2 * NH]), reads=[psr[7]], writes=[Rs[16][1]])
            for kt in range(15, -1, -1):
                P.op("dve", lambda e, kt=kt: e.tensor_tensor(out=biasS[:, kt, :], in0=sfxc[:, kt, :], in1=Rs[kt + 1][0], op=ALU.add),
                     reads=[r_sfxc, Rs[kt + 1][1]], writes=[r_biasS])
                P.op("dve", lambda e, kt=kt: e.tensor_tensor(out=Rs[kt][0], in0=Rs[kt + 1][0], in1=totc[:, kt, :], op=ALU.add),
                     reads=[Rs[kt + 1][1], r_totc], writes=[Rs[kt][1]])

            def psOs(hl):
                b = hl // 3
                return bank(b)[0:TS, (hl % 3) * 160:(hl % 3) * 160 + 129], b

            def s_iter(hh, kt):
                if True:
                    s = kt % 2
                    if kt < 16:
                        load([lambda e, kt=kt, s=s: e.dma_start(out=kc[s][0], in_=ck[jl, kt * 128:(kt + 1) * 128, hh * 1024:(hh + 1) * 1024]),
                              lambda e, kt=kt, s=s: e.dma_start(out=vc[s][0], in_=cv[jl, kt * 128:(kt + 1) * 128, hh * 1024:(hh + 1) * 1024])],
                             [], [kc[s][1], vc[s][1]])
                        pk = pair(2)
                        P.op("pe", [(lambda e, hl=hl, s=s: e.transpose(out=pk[:, hl * 128:(hl + 1) * 128], in_=kc[s][0][:, hl * 128:(hl + 1) * 128], identity=ident))
                                    for hl in range(8)], reads=[kc[s][1], r_cs], writes=[psr[4], psr[5]])
                        P.op("act", lambda e, s=s: e.activation(out=kcT[s][0].rearrange("p a b -> p (a b)"), in_=pk, func=AF.Copy),
                             reads=[psr[4], psr[5]], writes=[kcT[s][1]])
                        P.op("pool", lambda e, s=s: e.tensor_copy(out=Vc[s][0][:, :, 0:128], in_=vc[s][0].rearrange("p (h e) -> p h e", e=128)),
                             reads=[vc[s][1]], writes=[Vc[s][1]])
                        P.op("pe", [(lambda e, hl=hl, s=s: e.matmul(bank(6)[:, hl * 16:(hl + 1) * 16], lhsT=kcT[s][0][:, hl, :],
                                                                    rhs=qTs[:, hh * 8 + hl, :], start=True, stop=True)) for hl in range(8)],
                             reads=[kcT[s][1], r_qTs], writes=[psr[6]])
                        P.op("dve", lambda e, kt=kt: e.scalar_tensor_tensor(
                            out=tmpS.rearrange("p (h q) -> p h q", q=16), in0=bank(6)[:, 0:128].rearrange("p (h q) -> p h q", q=16), scalar=SCALE,
                            in1=biasS[:, kt, hh * 8:(hh + 1) * 8].unsqueeze(2).to_broadcast([128, 8, 16]), op0=ALU.mult, op1=ALU.add),
                            reads=[psr[6], r_biasS], writes=[r_tmpS])
                        P.op("act", lambda e, s=s: e.activation(out=Ps[s][0], in_=tmpS, func=AF.Exp), reads=[r_tmpS], writes=[Ps[s][1]])
                        fns = []
                        banks = set()
                        for hl in range(8):
                            o_ap, b = psOs(hl)
                            banks.add(b)
                            fns.append(lambda e, hl=hl, o_ap=o_ap, s=s, kt=kt: e.matmul(o_ap, lhsT=Ps[s][0][:, hl * 16:(hl + 1) * 16],
                                                                                       rhs=Vc[s][0][:, hl, 0:129], start=(kt == 0 and hl % 3 == 0), stop=False,
                                                                                       skip_group_check=True))
                        P.op("pe", fns, reads=[Ps[s][1], Vc[s][1]], writes=[psr[b] for b in sorted(banks)])
                    else:
                        P.op("pe", [(lambda e, hl=hl: e.matmul(bank(6)[0:TS, hl * 16:(hl + 1) * 16], lhsT=kTs[:, hh * 8 + hl, :],
                                                               rhs=qTs[:, hh * 8 + hl, :], start=True, stop=True)) for hl in range(8)],
                             reads=[r_kTs, r_qTs], writes=[psr[6]])
                        P.op("dve", lambda e: e.scalar_tensor_tensor(
                            out=tmpS[0:TS].rearrange("p (h q) -> p h q", q=16), in0=bank(6)[0:TS, 0:128].rearrange("p (h q) -> p h q", q=16), scalar=SCALE,
                            in1=biasN[0:TS, hh * 8:(hh + 1) * 8].unsqueeze(2).to_broadcast([TS, 8, 16]), op0=ALU.mult, op1=ALU.add),
                            reads=[psr[6], r_biasN], writes=[r_tmpS])
                        P.op("act", lambda e, s=s: e.activation(out=Ps[s][0][0:TS], in_=tmpS[0:TS], func=AF.Exp), reads=[r_tmpS], writes=[Ps[s][1]])
                        P.op("pool", lambda e, s=s: e.tensor_tensor(
                            out=Ps[s][0][0:TS].rearrange("p (h q) -> p h q", q=16), in0=Ps[s][0][0:TS].rearrange("p (h q) -> p h q", q=16),
                            in1=mTb[0:TS, 0:TS].unsqueeze(1).to_broadcast([TS, 8, 16]), op=ALU.mult),
                            reads=[Ps[s][1], r_mTb], writes=[Ps[s][1]])
                        fns = []
                        banks = set()
                        for hl in range(8):
                            o_ap, b = psOs(hl)
                            banks.add(b)
                            fns.append(lambda e, hl=hl, o_ap=o_ap, s=s: e.matmul(o_ap, lhsT=Ps[s][0][0:TS, hl * 16:(hl + 1) * 16],
                                                                                rhs=vs[0:TS, hh * 8 + hl, 0:129], start=False, stop=True, skip_group_check=True))
                        P.op("pe", fns, reads=[Ps[s][1], r_vs], writes=[psr[b] for b in sorted(banks)])

            for hh in range(2):
                for kt in range(17):
                    s_iter(hh, kt)
                for hl in range(8):
                    o_ap, b = psOs(hl)
                    ri, r_ri = rinv[hl % 2]
                    hg = hh * 8 + hl
                    P.op("dve", lambda e, o_ap=o_ap, ri=ri: e.reciprocal(out=ri[0:TS], in_=o_ap[:, 128:129]), reads=[psr[b]], writes=[r_ri])
                    P.op("dve", lambda e, o_ap=o_ap, ri=ri, hg=hg: e.scalar_tensor_tensor(
                        out=Ysb[0:TS, hg * 128:(hg + 1) * 128], in0=o_ap[:, 0:128], scalar=ri[0:TS], in1=szs[0:TS, hg, :],
                        op0=ALU.mult, op1=ALU.mult), reads=[psr[b], r_ri, r_szs], writes=[r_Ysb])
            store([lambda e: e.dma_start(out=Ysc[SEQ:SEQ + TS, :], in_=Ysb[0:TS])], [r_Ysb], [YS[32]])

        for layer in range(n_layers):
            jl = layer // 2
            if layer % 2 == 0:
                if 'G' in PHASES:
                    phase_G(layer)
                if 'T' in PHASES:
                    phase_T(layer, g_w_out[jl])
            else:
                if 'F' in PHASES:
                    phase_F(layer)
                if 'A' in PHASES:
                    phase_A(layer)
                if 'S' in PHASES:
                    phase_S(layer)
                if 'T' in PHASES:
                    phase_T(layer, f_w_out[jl])

        P.emit(final_ops=stores)
    return nc


def _consts():
    c = np.zeros((128, 640), np.float32)
    i = np.arange(128)
    c[:, 0:128] = np.eye(128, dtype=np.float32)
    c[:, 128:256] = (i[:, None] > i[None, :])
    c[:, 256:384] = 1.0
    c[:, 384:512] = (i[:, None] <= i[None, :])
    c[:, 512:640] = ((i[:, None] // 64) <= (i[None, :] // 64))
    return c


_NC_CACHE = {}


def kernel(x_prompt, x_sample, cache_fox_k, cache_fox_v, cache_fox_logf, p_prompt, p_sample,
           norm_pre, norm_post, gmlp_w_in, gmlp_ln_g, gmlp_ln_b, gmlp_w_s, gmlp_b_s, gmlp_w_out,
           fox_w_in, fox_b_f, fox_w_out, ple_w_proj, ple_w_gate, _n_layers=4, _cores=8):
    f = lambda a: np.ascontiguousarray(np.asarray(a), dtype=np.float32)
    n = _cores
    if _n_layers not in _NC_CACHE:
        _NC_CACHE[_n_layers] = build(_n_layers)
    nc = _NC_CACHE[_n_layers]
    shared = {
        "norm_pre": f(norm_pre), "norm_post": f(norm_post),
        "g_w_in": f(gmlp_w_in), "g_ln_g": f(gmlp_ln_g), "g_ln_b": f(gmlp_ln_b), "g_w_s": f(gmlp_w_s),
        "g_b_s": f(gmlp_b_s), "g_w_out": f(gmlp_w_out), "f_w_in": f(fox_w_in), "f_b_f": f(fox_b_f),
        "f_w_out": f(fox_w_out), "ple_proj": f(ple_w_proj), "ple_gate": f(ple_w_gate), "consts": _consts(),
    }
    xp = np.asarray(x_prompt); xs = np.asarray(x_sample)
    ckk = np.asarray(cache_fox_k); cvv = np.asarray(cache_fox_v); clff = np.asarray(cache_fox_logf)
    pp = np.asarray(p_prompt); psm = np.asarray(p_sample)
    in_maps = []
    for b in range(n):
        m = dict(shared)
        m["x_p"] = f(xp[b]); m["x_s"] = f(xs[b])
        m["ck"] = f(ckk[:, b].reshape(2, PAST, E)); m["cv"] = f(cvv[:, b].reshape(2, PAST, E)); m["clf"] = f(clff[:, b])
        m["p_p"] = f(pp[:, b]); m["p_s"] = f(psm[:, b])
        in_maps.append(m)
    res = run_bass_kernel_spmd(nc, in_maps, core_ids=list(range(n)))
    R = res.results
    st = lambda k, ax: np.stack([np.asarray(R[b][k], dtype=np.float32) for b in range(n)], axis=ax)
    y_prompt = st("y_p", 0)
    y_sample = st("y_s", 0)
    gv = st("gv_s", 1)
    fk_p = st("fk_p", 1).reshape(2, n, SEQ, NH, DH)
    fv_p = st("fv_p", 1).reshape(2, n, SEQ, NH, DH)
    flf_p = st("flf_p", 1)
    fk_s = st("fk_s", 1).reshape(2, n, TS, NH, DH)
    fv_s = st("fv_s", 1).reshape(2, n, TS, NH, DH)
    flf_s = st("flf_s", 1)
    return (y_prompt, y_sample, gv, fk_p, fv_p, flf_p, fk_s, fv_s, flf_s)
```

```python
import contextlib
import numpy as np
import concourse.bass as bass
import concourse.mybir as mybir
from concourse.bass_utils import run_bass_kernel_spmd

F32 = mybir.dt.float32
BF16 = mybir.dt.bfloat16
AF = mybir.ActivationFunctionType
ALU = mybir.AluOpType

D = 1024
E = 2048
SEQ = 4096
TS = 16
PAST = 2048
NH = 16
DH = 128
PLE = 256
NT = 33
TOK = 4224
SCALE = float(DH) ** -0.5
RMS_EPS = 1e-6
LN_EPS = 1e-5
FW = 4 * E + NH


class Sem:
    def __init__(self, h):
        self.h = h
        self.count = 0
        self.last_op = None


class Op:
    def __init__(self, eng, fns, is_dma):
        self.eng = eng
        self.fns = fns
        self.deps = []
        self.needs_inc = False
        self.sem = None
        self.ticket = None
        self.is_dma = is_dma


class Res:
    def __init__(self, name, arena=None, lo=0, hi=0):
        self.name = name
        self.arena = arena
        self.lo = lo
        self.hi = hi
        self.writers = []
        self.readers = []
        self.overlaps = []


class Prog:
    ENGS = ("sync", "act", "dve", "pool", "pe")
    BLK = {"sync": "sync", "act": "scalar", "dve": "vector", "pool": "gpsimd", "pe": "tensor"}

    def __init__(self, nc, stack):
        self.nc = nc
        self.stack = stack
        self.ops = {e: [] for e in self.ENGS}
        self.eng_sem = {}
        self.pools = {}
        self.pool_idx = {}
        self.arena_res = {}
        self.nsem = 0
        self.new_epoch()

    def new_sem(self, name):
        self.nsem += 1
        return Sem(self.stack.enter_context(self.nc.semaphore(f"{name}_{self.nsem}")))

    def new_epoch(self):
        for e in ("act", "dve", "pool", "pe"):
            self.eng_sem[e] = self.new_sem("e_" + e)

    def dma_pool(self, name, n):
        self.pools[name] = [self.new_sem("d_" + name) for _ in range(n)]
        self.pool_idx[name] = 0

    def res(self, name, arena=None, lo=0, hi=0):
        r = Res(name, arena, lo, hi)
        if arena is not None:
            lst = self.arena_res.setdefault(arena, [])
            for o in lst:
                if o.lo < hi and lo < o.hi:
                    r.overlaps.append(o)
                    o.overlaps.append(r)
            lst.append(r)
        return r

    def _dep(self, op, prod, raw):
        if prod is op:
            return
        if (not prod.is_dma) and (not op.is_dma) and prod.eng == op.eng:
            if not raw:
                return
            if op.eng == "pe":
                return
        prod.needs_inc = True
        op.deps.append(prod)

    def op(self, eng, fns, reads=(), writes=(), pool=None):
        if not isinstance(fns, (list, tuple)):
            fns = [fns]
        is_dma = pool is not None
        o = Op(eng, list(fns), is_dma)
        if is_dma:
            ps = self.pools[pool]
            i = self.pool_idx[pool]
            self.pool_idx[pool] = (i + 1) % len(ps)
            o.sem = ps[i]
            if o.sem.last_op is not None:
                o.sem.last_op.needs_inc = True
                o.deps.append(o.sem.last_op)
            o.sem.last_op = o
        else:
            o.sem = self.eng_sem[eng]
        for r in reads:
            for rr in [r] + r.overlaps:
                for w in rr.writers:
                    self._dep(o, w, True)
        for r in writes:
            for rr in [r] + r.overlaps:
                for w in rr.writers:
                    self._dep(o, w, False)
                for w in rr.readers:
                    self._dep(o, w, False)
        for r in reads:
            r.readers.append(o)
        for r in writes:
            r.writers = [o]
            r.readers = []
        self.ops[eng].append(o)
        return o

    def emit(self, final_ops=()):
        nc = self.nc
        for o in final_ops:
            o.needs_inc = True
        for e in self.ENGS:
            for o in self.ops[e]:
                if o.is_dma:
                    o.sem.count += 16 * len(o.fns)
                    o.ticket = o.sem.count
                elif o.needs_inc:
                    o.sem.count += 1
                    o.ticket = o.sem.count

        import os as _os
        if _os.environ.get("KCHECK"):
            self.check()

        def need_of(deps):
            need = {}
            for d in deps:
                k = id(d.sem)
                if d.ticket > need.get(k, (None, 0))[1]:
                    need[k] = (d.sem, d.ticket)
            return need

        with nc.Block() as block:
            for e in self.ENGS:
                ops = self.ops[e]
                if not ops and e != "sync":
                    continue

                def body(eng, ops=ops, e=e):
                    waited = {}
                    for o in ops:
                        for k, (s, v) in need_of(o.deps).items():
                            if waited.get(k, 0) < v:
                                eng.wait_ge(s.h, v)
                                waited[k] = v
                        n = len(o.fns)
                        for i, f in enumerate(o.fns):
                            ins = f(eng)
                            if o.is_dma:
                                ins.then_inc(o.sem.h, 16)
                            elif o.needs_inc and i == n - 1:
                                ins.then_inc(o.sem.h, 1)
                    if e == "sync":
                        for k, (s, v) in need_of(final_ops).items():
                            if waited.get(k, 0) < v:
                                eng.wait_ge(s.h, v)
                                waited[k] = v

                getattr(block, self.BLK[e])(body)


def _prog_check(self):
    pos = {e: 0 for e in self.ENGS}
    done = set()
    semval = {}
    total = sum(len(v) for v in self.ops.values())
    ndone = 0
    progress = True
    while progress:
        progress = False
        for e in self.ENGS:
            while pos[e] < len(self.ops[e]):
                o = self.ops[e][pos[e]]
                ok = True
                for d in o.deps:
                    if d.ticket is None:
                        raise RuntimeError("dep without ticket")
                    if semval.get(id(d.sem), 0) < d.ticket:
                        ok = False
                        break
                if not ok:
                    break
                if o.ticket is not None:
                    prev = semval.get(id(o.sem), 0)
                    exp = o.ticket - (16 * len(o.fns) if o.is_dma else 1)
                    if prev != exp:
                        raise RuntimeError(f"ticket order violation on {e}: prev={prev} exp={exp}")
                    semval[id(o.sem)] = o.ticket
                pos[e] += 1
                ndone += 1
                progress = True
    print("CHECK: done", ndone, "of", total, {e: (pos[e], len(self.ops[e])) for e in self.ENGS})
    if ndone != total:
        raise RuntimeError("DEADLOCK in abstract simulation")


Prog.check = _prog_check


class Arena:
    def __init__(self, P, name, ap2d_bf16, nbytes):
        self.P = P
        self.name = name
        self.ap = ap2d_bf16
        self.nbytes = nbytes
        self.off = 0

    def reset(self, off):
        self.off = off

    def alloc(self, name, free_shape, dtype):
        esz = 4 if dtype == F32 else 2
        n = int(np.prod(free_shape))
        nb = n * esz
        lo = (self.off + 63) // 64 * 64
        hi = lo + nb
        assert hi <= self.nbytes, f"arena overflow {name}: {hi} > {self.nbytes}"
        self.off = hi
        v = self.ap[:, lo // 2:hi // 2]
        if dtype == F32:
            v = v.bitcast(F32)
        if len(free_shape) == 2:
            v = v.rearrange("p (a b) -> p a b", b=free_shape[1])
        elif len(free_shape) == 3:
            v = v.rearrange("p (a b c) -> p a b c", b=free_shape[1], c=free_shape[2])
        r = self.P.res(name, self.name, lo, hi)
        return v, r


def run_stages(stages, tiles):
    K = len(stages)
    n = len(tiles)
    for step in range(n + K - 1):
        for k in range(K - 1, -1, -1):
            i = step - k
            if 0 <= i < n:
                stages[k](tiles[i])


def nrows(t):
    return 128 if t < 32 else TS


ARENA_BYTES = 207 * 1024


def build(n_layers=4, dbg=None):
    dbg = dbg or {}
    TILES = dbg.get('tiles', list(range(NT)))
    PHASES = dbg.get('phases', 'GTFAS')
    nc = bass.Bass("TRN2", target_bir_lowering=False)

    def din(name, shape, dt=F32):
        return nc.dram_tensor(name, shape, dt, kind="ExternalInput").ap()

    def dout(name, shape, dt=F32):
        return nc.dram_tensor(name, shape, dt, kind="ExternalOutput").ap()

    def dscr(name, shape, dt):
        return nc.dram_tensor(name, shape, dt, kind="Internal").ap()

    x_p = din("x_p", [SEQ, D]); x_s = din("x_s", [TS, D])
    ck = din("ck", [2, PAST, E]); cv = din("cv", [2, PAST, E]); clf = din("clf", [2, PAST, NH])
    p_p = din("p_p", [4, SEQ, PLE]); p_s = din("p_s", [4, TS, PLE])
    norm_pre = din("norm_pre", [4, D]); norm_post = din("norm_post", [4, D])
    g_w_in = din("g_w_in", [2, D, 3 * E]); g_ln_g = din("g_ln_g", [2, E]); g_ln_b = din("g_ln_b", [2, E])
    g_w_s = din("g_w_s", [2, 16, 128, 128]); g_b_s = din("g_b_s", [2, 16, 128]); g_w_out = din("g_w_out", [2, E, D])
    f_w_in = din("f_w_in", [2, D, FW]); f_b_f = din("f_b_f", [2, NH]); f_w_out = din("f_w_out", [2, E, D])
    ple_proj = din("ple_proj", [4, PLE, D]); ple_gate = din("ple_gate", [4, D, D])
    consts = din("consts", [128, 640])

    y_p = dout("y_p", [SEQ, D]); y_s = dout("y_s", [TS, D]); gv_s = dout("gv_s", [2, TS, E])
    fk_p = dout("fk_p", [2, SEQ, E]); fv_p = dout("fv_p", [2, SEQ, E]); flf_p = dout("flf_p", [2, SEQ, NH])
    fk_s = dout("fk_s", [2, TS, E]); fv_s = dout("fv_s", [2, TS, E]); flf_s = dout("flf_s", [2, TS, NH])

    Ysc = dscr("Ysc", [TOK, E], BF16)
    QTs = dscr("QTs", [NH, DH, TOK], BF16)
    KTs = dscr("KTs", [NH, DH, TOK], BF16)
    Vbs = dscr("Vbs", [NH, TOK, DH], BF16)
    SZs = dscr("SZs", [NH, TOK, DH], F32)

    with contextlib.ExitStack() as st:
        P = Prog(nc, st)
        P.dma_pool("ld", 6)
        P.dma_pool("st", 8)
        P.dma_pool("w", 4)
        ar_t = st.enter_context(nc.sbuf_tensor("arena", [128, ARENA_BYTES // 2], BF16))
        A = Arena(P, "sb", ar_t[:], ARENA_BYTES)
        ps_t = [st.enter_context(nc.psum_tensor(f"ps{i}", [128, 1024], F32)) for i in range(4)]
        psr = [P.res(f"bank{i}", "psum", i, i + 1) for i in range(8)]

        def bank(i):
            return ps_t[i // 2][:, (i % 2) * 512:(i % 2 + 1) * 512]

        def bank_bf(i):
            return ps_t[i // 2][:].bitcast(BF16)[:, (i % 2) * 1024:(i % 2 + 1) * 1024]

        def pair(i):
            return ps_t[i][:]

        def pair_bf(i):
            return ps_t[i][:].bitcast(BF16)

        XR = [P.res(f"xr{t}") for t in range(NT)]
        YS = [P.res(f"ysc{t}") for t in range(NT)]
        QTr = [P.res(f"qts{t}") for t in range(NT)]
        KTr = [P.res(f"kts{t}") for t in range(NT)]
        VBr = [P.res(f"vbs{t}") for t in range(NT)]
        SZr = [P.res(f"szs{t}") for t in range(NT)]
        LFSr = P.res("flf_s")
        OUTr = P.res("outs")
        stores = []

        def store(fns, reads, writes):
            o = P.op("sync", fns, reads=reads, writes=writes, pool="st")
            stores.append(o)
            return o

        def load(fns, reads, writes):
            return P.op("sync", fns, reads=reads, writes=writes, pool="ld")

        def wload(fns, writes):
            return P.op("pool", fns, reads=[], writes=writes, pool="w")

        def xsrc(layer, t):
            nr = nrows(t)
            if layer == 0:
                return x_p[t * 128:(t + 1) * 128, :] if t < 32 else x_s[0:nr, :]
            return y_p[t * 128:(t + 1) * 128, :] if t < 32 else y_s[0:nr, :]

        def xdst(t):
            return y_p[t * 128:(t + 1) * 128, :] if t < 32 else y_s[0:TS, :]

        def psrc(layer, t):
            return p_p[layer, t * 128:(t + 1) * 128, :] if t < 32 else p_s[layer, 0:TS, :]

        cs, r_cs = A.alloc("cs", [640], F32)
        idb, r_idb = A.alloc("idb", [128], BF16)
        mTb, r_mTb = A.alloc("mTb", [128], BF16)
        nh, r_nh = A.alloc("nh", [1], F32)
        ETab, r_ETab = A.alloc("ETab", [32, 16], F32)
        TcTab, r_TcTab = A.alloc("TcTab", [32, 16], F32)
        ident = cs[:, 0:128]
        tstr = cs[:, 128:256]
        ones = cs[:, 256:384]
        maskT = cs[:, 384:512]
        maskC = cs[:, 512:640]
        PH0 = A.off

        load([lambda e: e.dma_start(out=cs, in_=consts[:, :])], [], [r_cs])
        P.op("dve", lambda e: e.tensor_copy(out=idb, in_=ident), reads=[r_cs], writes=[r_idb])
        P.op("dve", lambda e: e.tensor_copy(out=mTb, in_=maskT), reads=[r_cs], writes=[r_mTb])
        P.op("pool", lambda e: e.memset(nh, -0.5), writes=[r_nh])

        def rstd_ops(ss, r_ss, v1, r_v1, rstd, r_rstd, nr, mul, eps):
            P.op("pool", lambda e: e.tensor_scalar(out=v1[0:nr], in0=ss[0:nr], scalar1=mul, scalar2=eps,
                                                    op0=ALU.mult, op1=ALU.add), reads=[r_ss], writes=[r_v1])
            P.op("pool", lambda e: e.tensor_tensor(out=rstd[0:nr], in0=v1[0:nr], in1=nh[0:nr], op=ALU.pow),
                 reads=[r_v1, r_nh], writes=[r_rstd])

        def head_a(B, t, xt, r_xt):
            nr = nrows(t)
            s = t % 2
            junk, r_junk = B["junk"]
            ss, r_ss = B["ss"][s]
            v1, r_v1 = B["v1"][s]
            rstd, r_rstd = B["rstd"][s]
            hb, r_hb = B["hb"]
            hT, r_hT = B["hT"][s]
            gpre, r_gpre = B["gpre"]
            P.op("act", lambda e: e.activation(out=junk[0:nr], in_=xt[0:nr], func=AF.Square, accum_out=ss[0:nr]),
                 reads=[r_xt], writes=[r_junk, r_ss])
            rstd_ops(ss, r_ss, v1, r_v1, rstd, r_rstd, nr, 1.0 / D, RMS_EPS)
            P.op("dve", lambda e: e.scalar_tensor_tensor(out=hb[0:nr], in0=xt[0:nr], scalar=rstd[0:nr], in1=gpre[0:nr],
                                                          op0=ALU.mult, op1=ALU.mult),
                 reads=[r_xt, r_rstd, r_gpre], writes=[r_hb])

        def head_b(B, t, tb):
            nr = nrows(t)
            s = t % 2
            hb, r_hb = B["hb"]
            hT, r_hT = B["hT"][s]
            pT = bank_bf(tb)
            P.op("pe", [(lambda e, c=c: e.transpose(out=pT[:, c * 128:c * 128 + nr], in_=hb[0:nr, c * 128:(c + 1) * 128],
                                                    identity=idb[0:nr, 0:nr])) for c in range(8)],
                 reads=[r_hb, r_idb], writes=[psr[tb]])
            P.op("act", lambda e: e.activation(out=hT[:, :, 0:nr], in_=pT.rearrange("p (a b) -> p a b", b=128)[:, :, 0:nr],
                                               func=AF.Copy), reads=[psr[tb]], writes=[r_hT])
            return hT, r_hT

        def common_small(B):
            B["junk"] = A.alloc("junk", [1024], BF16)
            for k in ("ss", "v1", "rstd"):
                B[k] = [A.alloc(f"{k}{i}", [1], F32) for i in range(2)]

        def phase_G(layer):
            jl = layer // 2
            P.new_epoch()
            A.reset(PH0)
            B = {}
            Wb = [A.alloc(f"Win{i}", [8, E], BF16) for i in range(3)]
            wsT, r_wsT = A.alloc("wsT", [16, 128], BF16)
            lng, r_lng = A.alloc("lng", [E], F32)
            lnb, r_lnb = A.alloc("lnb", [E], F32)
            bsT, r_bsT = A.alloc("bsT", [16], F32)
            B["gpre"] = A.alloc("gpre", [D], F32)
            gpre, r_gpre = B["gpre"]
            common_small(B)
            xts = [A.alloc(f"xt{i}", [D], F32) for i in range(2)]
            B["hb"] = A.alloc("hb", [D], BF16)
            B["hT"] = [A.alloc(f"hT{i}", [8, 128], BF16) for i in range(2)]
            us = [A.alloc(f"u{i}", [E], F32) for i in range(2)]
            u, r_u = us[0]
            vg, r_vg = A.alloc("vg", [E], F32)
            szs_ = [A.alloc(f"sz{i}", [E], F32) for i in range(2)]
            vb, r_vb = A.alloc("vb", [E], BF16)
            ys = [A.alloc(f"y{i}", [E], BF16) for i in range(2)]
            stats, r_stats = A.alloc("stats", [4, 6], F32)
            mv, r_mv = A.alloc("mv", [2], F32)
            v2, r_v2 = A.alloc("v2", [1], F32)
            rs2, r_rs2 = A.alloc("rs2", [1], F32)

            for bi in range(3):
                wload([(lambda e, c=c, bi=bi: e.dma_start(out=Wb[bi][0][:, c, :], in_=g_w_in[jl, c * 128:(c + 1) * 128, bi * E:(bi + 1) * E]))
                       for c in range(8)], [Wb[bi][1]])
            load([lambda e: e.dma_start(out=lng, in_=g_ln_g[jl:jl + 1, :].to_broadcast([128, E])),
                  lambda e: e.dma_start(out=lnb, in_=g_ln_b[jl:jl + 1, :].to_broadcast([128, E])),
                  lambda e: e.dma_start(out=gpre, in_=norm_pre[layer:layer + 1, :].to_broadcast([128, D])),
                  lambda e: e.dma_start(out=bsT, in_=g_b_s[jl].rearrange("g i -> i g"), allow_slow_non_contiguous=True)],
                 [], [r_lng, r_lnb, r_gpre, r_bsT])
            wst = u.rearrange("p (g j) -> p g j", j=128)
            load([lambda e: e.dma_start(out=wst, in_=g_w_s[jl].rearrange("g i j -> i g j"))], [], [r_u])
            for hf in range(2):
                pp_ = pair(hf)
                P.op("pe", [(lambda e, g=g, pp_=pp_: e.transpose(out=pp_[:, (g % 8) * 128:(g % 8 + 1) * 128], in_=wst[:, g, :], identity=ident))
                            for g in range(hf * 8, hf * 8 + 8)], reads=[r_u, r_cs], writes=[psr[2 * hf], psr[2 * hf + 1]])
                P.op("dve", lambda e, hf=hf, pp_=pp_: e.tensor_tensor(
                    out=wsT[:, hf * 8:(hf + 1) * 8, :], in0=pp_.rearrange("p (g i) -> p g i", i=128),
                    in1=maskC.unsqueeze(1).to_broadcast([128, 8, 128]), op=ALU.mult),
                    reads=[psr[2 * hf], psr[2 * hf + 1], r_cs], writes=[r_wsT])

            def sL(t):
                nr = nrows(t)
                xt, r_xt = xts[t % 2]
                load([lambda e: e.dma_start(out=xt[0:nr], in_=xsrc(layer, t))], [XR[t]], [r_xt])

            def s0a(t):
                xt, r_xt = xts[t % 2]
                head_a(B, t, xt, r_xt)

            def s0b(t):
                head_b(B, t, 0)

            prev_tile = [None]

            def s1(t):
                nr = nrows(t)
                hT, r_hT = B["hT"][t % 2]
                u, r_u = us[t % 2]
                sz, r_sz = szs_[t % 2]
                for n in range(12):
                    if n == 6 and prev_tile[0] is not None:
                        s2(prev_tile[0])
                    bk = 1 + (n % 3)
                    P.op("pe", [(lambda e, c=c, n=n, bk=bk: e.matmul(bank(bk)[0:nr, :], lhsT=hT[:, c, 0:nr],
                                                                      rhs=Wb[n // 4][0][:, c, (n % 4) * 512:(n % 4 + 1) * 512],
                                                                      start=(c == 0), stop=(c == 7))) for c in range(8)],
                         reads=[r_hT, Wb[n // 4][1]], writes=[psr[bk]])
                    if n < 4:
                        dst, r_dst, fn = u, r_u, AF.Gelu_apprx_tanh
                    elif n < 8:
                        dst, r_dst, fn = vg, r_vg, AF.Gelu_apprx_tanh
                    else:
                        dst, r_dst, fn = sz, r_sz, AF.Silu
                    c0 = (n % 4) * 512
                    P.op("act", lambda e, dst=dst, fn=fn, c0=c0, bk=bk: e.activation(out=dst[0:nr, c0:c0 + 512], in_=bank(bk)[0:nr, :], func=fn),
                         reads=[psr[bk]], writes=[r_dst])
                    if n == 7:
                        P.op("dve", [(lambda e, q=q: e.bn_stats(out=stats[0:nr, q, :], in_=vg[0:nr, q * 512:(q + 1) * 512])) for q in range(4)],
                             reads=[r_vg], writes=[r_stats])
                        P.op("dve", lambda e: e.bn_aggr(out=mv[0:nr], in_=stats[0:nr].rearrange("p a b -> p (a b)")),
                             reads=[r_stats], writes=[r_mv])
                        P.op("pool", lambda e: e.tensor_scalar(out=v2[0:nr], in0=mv[0:nr, 1:2], scalar1=1.0, scalar2=LN_EPS,
                                                                op0=ALU.mult, op1=ALU.add), reads=[r_mv], writes=[r_v2])
                        P.op("pool", lambda e: e.tensor_tensor(out=rs2[0:nr], in0=v2[0:nr], in1=nh[0:nr], op=ALU.pow),
                             reads=[r_v2, r_nh], writes=[r_rs2])
                        P.op("dve", lambda e: e.tensor_scalar(out=vg[0:nr], in0=vg[0:nr], scalar1=mv[0:nr, 0:1], scalar2=rs2[0:nr],
                                                               op0=ALU.subtract, op1=ALU.mult), reads=[r_vg, r_mv, r_rs2], writes=[r_vg])
                        P.op("pool", lambda e: e.tensor_tensor(out=vg[0:nr], in0=vg[0:nr], in1=lng[0:nr], op=ALU.mult),
                             reads=[r_vg, r_lng], writes=[r_vg])
                        if t < 32:
                            P.op("dve", lambda e: e.tensor_tensor(out=vb[0:nr], in0=vg[0:nr], in1=lnb[0:nr], op=ALU.add),
                                 reads=[r_vg, r_lnb], writes=[r_vb])
                        else:
                            P.op("dve", lambda e: e.tensor_tensor(out=vg[0:nr], in0=vg[0:nr], in1=lnb[0:nr], op=ALU.add),
                                 reads=[r_vg, r_lnb], writes=[r_vg])
                            P.op("dve", lambda e: e.tensor_copy(out=vb[0:nr], in_=vg[0:nr]), reads=[r_vg], writes=[r_vb])
                            store([lambda e: e.dma_start(out=gv_s[jl, :, :], in_=vg[0:nr])], [r_vg], [])
                prev_tile[0] = t

            def s2(t):
                nr = nrows(t)
                y, r_y = ys[t % 2]
                u, r_u = us[t % 2]
                sz, r_sz = szs_[t % 2]
                P.op("pe", [(lambda e, g=g: e.matmul(bank(4 + g // 4)[0:nr, (g % 4) * 128:(g % 4 + 1) * 128],
                                                     lhsT=wsT[0:nr, g, 0:nr], rhs=vb[0:nr, g * 128:(g + 1) * 128],
                                                     start=True, stop=True)) for g in range(16)],
                     reads=[r_wsT, r_vb], writes=[psr[4], psr[5], psr[6], psr[7]])
                P.op("dve", [(lambda e, g=g: e.scalar_tensor_tensor(
                    out=u[0:nr, g * 128:(g + 1) * 128], in0=bank(4 + g // 4)[0:nr, (g % 4) * 128:(g % 4 + 1) * 128],
                    scalar=bsT[0:nr, g:g + 1], in1=u[0:nr, g * 128:(g + 1) * 128], op0=ALU.add, op1=ALU.mult)) for g in range(16)],
                    reads=[psr[4], psr[5], psr[6], psr[7], r_bsT, r_u], writes=[r_u])
                P.op("dve", lambda e: e.tensor_tensor(out=y[0:nr], in0=u[0:nr], in1=sz[0:nr], op=ALU.mult),
                     reads=[r_u, r_sz], writes=[r_y])
                store([lambda e: e.dma_start(out=Ysc[t * 128:t * 128 + nr, :], in_=y[0:nr])], [r_y], [YS[t]])

            run_stages([sL, s0a, s0b, s1], TILES)
            s2(prev_tile[0])

        def phase_T(layer, w_out_ap):
            P.new_epoch()
            A.reset(PH0)
            Wout, r_Wout = A.alloc("Wout", [16, D], BF16)
            Wg, r_Wg = A.alloc("Wg", [8, D], BF16)
            Wp, r_Wp = A.alloc("Wp", [2, D], BF16)
            gpost, r_gpost = A.alloc("gpost", [D], F32)
            junk, r_junk = A.alloc("junk", [D], BF16)
            NS = 4
            xts = [A.alloc(f"xt{i}", [D], F32) for i in range(NS)]
            yts = [A.alloc(f"yt{i}", [E], BF16) for i in range(NS)]
            pts = [A.alloc(f"pt{i}", [PLE], F32) for i in range(NS)]
            yTs = [A.alloc(f"yT{i}", [16, 128], BF16) for i in range(2)]
            sss = [A.alloc(f"ss{i}", [1], F32) for i in range(2)]
            v1s = [A.alloc(f"v1{i}", [1], F32) for i in range(2)]
            rss = [A.alloc(f"rstd{i}", [1], F32) for i in range(2)]
            tmp, r_tmp = A.alloc("tmp", [D], F32)
            x1s = [A.alloc(f"x1{i}", [D], F32) for i in range(2)]
            x1b, r_x1b = A.alloc("x1b", [D], BF16)
            x1T, r_x1T = A.alloc("x1T", [8, 128], BF16)
            sg, r_sg = A.alloc("sg", [D], F32)
            pb, r_pb = A.alloc("pb", [PLE], BF16)
            pT, r_pT = A.alloc("pT", [2, 128], BF16)
            x2s = [A.alloc(f"x2{i}", [D], F32) for i in range(2)]

            wload([(lambda e, q=q: e.dma_start(out=Wout[:, q * 4:(q + 1) * 4, :],
                                               in_=w_out_ap[q * 512:(q + 1) * 512, :].rearrange("(e p) n -> p e n", p=128)))
                   for q in range(4)], [r_Wout])
            wload([(lambda e, q=q: e.dma_start(out=Wg[:, q * 4:(q + 1) * 4, :],
                                               in_=ple_gate[layer, q * 512:(q + 1) * 512, :].rearrange("(e p) n -> p e n", p=128)))
                   for q in range(2)] +
                  [lambda e: e.dma_start(out=Wp, in_=ple_proj[layer].rearrange("(e p) n -> p e n", p=128))], [r_Wg, r_Wp])
            load([lambda e: e.dma_start(out=gpost, in_=norm_post[layer:layer + 1, :].to_broadcast([128, D]))], [], [r_gpost])

            o_sb, r_o_sb = A.alloc("o_sb", [D], F32)

            def L(t):
                nr = nrows(t)
                xt, r_xt = xts[t % NS]
                yt, r_yt = yts[t % NS]
                pt, r_pt = pts[t % NS]
                load([lambda e: e.dma_start(out=xt[0:nr], in_=xsrc(layer, t)),
                      lambda e: e.dma_start(out=yt[0:nr], in_=Ysc[t * 128:t * 128 + nr, :]),
                      lambda e: e.dma_start(out=pt[0:nr], in_=psrc(layer, t))],
                     [XR[t], YS[t]], [r_xt, r_yt, r_pt])

            def a_ytr(t):
                nr = nrows(t)
                yt, r_yt = yts[t % NS]
                yT, r_yT = yTs[t % 2]
                pTb = pair_bf(0)
                P.op("pe", [(lambda e, c=c: e.transpose(out=pTb[:, c * 128:c * 128 + nr], in_=yt[0:nr, c * 128:(c + 1) * 128],
                                                        identity=idb[0:nr, 0:nr])) for c in range(16)],
                     reads=[r_yt, r_idb], writes=[psr[0], psr[1]])
                P.op("act", lambda e: e.activation(out=yT[:, :, 0:nr], in_=pTb.rearrange("p (a b) -> p a b", b=128)[:, :, 0:nr], func=AF.Copy),
                     reads=[psr[0], psr[1]], writes=[r_yT])

            def a_o(t):
                nr = nrows(t)
                yT, r_yT = yTs[t % 2]
                po = pair(1)
                P.op("pe", [(lambda e, n=n, c=c: e.matmul(po[0:nr, n * 512:(n + 1) * 512], lhsT=yT[:, c, 0:nr],
                                                          rhs=Wout[:, c, n * 512:(n + 1) * 512], start=(c == 0), stop=(c == 15)))
                            for n in range(2) for c in range(16)],
                     reads=[r_yT, r_Wout], writes=[psr[2], psr[3]])
                P.op("act", lambda e: e.activation(out=o_sb[0:nr], in_=po[0:nr], func=AF.Copy), reads=[psr[2], psr[3]], writes=[r_o_sb])

            def b_chain(t):
                nr = nrows(t)
                s = t % 2
                xt, r_xt = xts[t % NS]
                pt, r_pt = pts[t % NS]
                ss, r_ss = sss[s]
                v1, r_v1 = v1s[s]
                rstd, r_rstd = rss[s]
                x1, r_x1 = x1s[s]
                P.op("act", lambda e: e.activation(out=junk[0:nr], in_=o_sb[0:nr], func=AF.Square, accum_out=ss[0:nr]),
                     reads=[r_o_sb], writes=[r_junk, r_ss])
                rstd_ops(ss, r_ss, v1, r_v1, rstd, r_rstd, nr, 1.0 / D, RMS_EPS)
                P.op("dve", lambda e: e.scalar_tensor_tensor(out=tmp[0:nr], in0=o_sb[0:nr], scalar=rstd[0:nr], in1=gpost[0:nr],
                                                              op0=ALU.mult, op1=ALU.mult),
                     reads=[r_o_sb, r_rstd, r_gpost], writes=[r_tmp])
                P.op("dve", lambda e: e.tensor_tensor(out=x1[0:nr], in0=tmp[0:nr], in1=xt[0:nr], op=ALU.add),
                     reads=[r_tmp, r_xt], writes=[r_x1])
                P.op("dve", lambda e: e.tensor_copy(out=x1b[0:nr], in_=x1[0:nr]), reads=[r_x1], writes=[r_x1b])
                P.op("dve", lambda e: e.tensor_copy(out=pb[0:nr], in_=pt[0:nr]), reads=[r_pt], writes=[r_pb])

            def c_tr(t):
                nr = nrows(t)
                pTb = bank_bf(4)
                P.op("pe", [(lambda e, c=c: e.transpose(out=pTb[:, c * 128:c * 128 + nr], in_=x1b[0:nr, c * 128:(c + 1) * 128],
                                                        identity=idb[0:nr, 0:nr])) for c in range(8)],
                     reads=[r_x1b, r_idb], writes=[psr[4]])
                P.op("act", lambda e: e.activation(out=x1T[:, :, 0:nr], in_=pTb.rearrange("p (a b) -> p a b", b=128)[:, :, 0:nr], func=AF.Copy),
                     reads=[psr[4]], writes=[r_x1T])
                pPb = bank_bf(5)
                P.op("pe", [(lambda e, c=c: e.transpose(out=pPb[:, c * 128:c * 128 + nr], in_=pb[0:nr, c * 128:(c + 1) * 128],
                                                        identity=idb[0:nr, 0:nr])) for c in range(2)],
                     reads=[r_pb, r_idb], writes=[psr[5]])
                P.op("act", lambda e: e.activation(out=pT[:, :, 0:nr], in_=pPb[:, 0:256].rearrange("p (a b) -> p a b", b=128)[:, :, 0:nr], func=AF.Copy),
                     reads=[psr[5]], writes=[r_pT])

            def c_gate(t):
                nr = nrows(t)
                pg = pair(3)
                P.op("pe", [(lambda e, n=n, c=c: e.matmul(pg[0:nr, n * 512:(n + 1) * 512], lhsT=x1T[:, c, 0:nr],
                                                          rhs=Wg[:, c, n * 512:(n + 1) * 512], start=(c == 0), stop=(c == 7)))
                            for n in range(2) for c in range(8)],
                     reads=[r_x1T, r_Wg], writes=[psr[6], psr[7]])

            def d_sig(t):
                nr = nrows(t)
                pg = pair(3)
                P.op("act", lambda e: e.activation(out=sg[0:nr], in_=pg[0:nr], func=AF.Sigmoid), reads=[psr[6], psr[7]], writes=[r_sg])

            def d_pp(t):
                nr = nrows(t)
                s = t % 2
                x1, r_x1 = x1s[s]
                x2, r_x2 = x2s[s]
                pg = pair(3)
                P.op("pe", [(lambda e, n=n, c=c: e.matmul(pg[0:nr, n * 512:(n + 1) * 512], lhsT=pT[:, c, 0:nr],
                                                          rhs=Wp[:, c, n * 512:(n + 1) * 512], start=(c == 0), stop=(c == 1)))
                            for n in range(2) for c in range(2)],
                     reads=[r_pT, r_Wp], writes=[psr[6], psr[7]])
                P.op("dve", lambda e: e.tensor_tensor(out=tmp[0:nr], in0=pg[0:nr], in1=sg[0:nr], op=ALU.mult),
                     reads=[r_sg, psr[6], psr[7]], writes=[r_tmp])
                P.op("dve", lambda e: e.tensor_tensor(out=x2[0:nr], in0=tmp[0:nr], in1=x1[0:nr], op=ALU.add),
                     reads=[r_tmp, r_x1], writes=[r_x2])
                store([lambda e: e.dma_start(out=xdst(t), in_=x2[0:nr])], [r_x2], [XR[t]])

            n_t = len(TILES)
            for step in range(n_t + 4):
                def tl(k):
                    i = step - k
                    return TILES[i] if 0 <= i < n_t else None
                tD, tC, tB, tA, tL = tl(4), tl(3), tl(2), tl(1), tl(0)
                if tD is not None:
                    d_sig(tD)
                if tA is not None:
                    a_ytr(tA)
                if tD is not None:
                    d_pp(tD)
                if tC is not None:
                    c_tr(tC)
                if tB is not None:
                    b_chain(tB)
                if tA is not None:
                    a_o(tA)
                if tC is not None:
                    c_gate(tC)
                if tL is not None:
                    L(tL)

        def phase_F(layer):
            jl = layer // 2
            P.new_epoch()
            A.reset(PH0)
            B = {}
            Wfb = [A.alloc(f"Wf{i}", [8, E if i < 3 else E + NH], BF16) for i in range(4)]
            B["gpre"] = A.alloc("gpre", [D], F32)
            gpre, r_gpre = B["gpre"]
            bfb, r_bfb = A.alloc("bfb", [NH], F32)
            common_small(B)
            xts = [A.alloc(f"xt{i}", [D], F32) for i in range(2)]
            B["hb"] = A.alloc("hb", [D], BF16)
            B["hT"] = [A.alloc(f"hT{i}", [8, 128], BF16) for i in range(2)]
            qb, r_qb = A.alloc("qb", [E], BF16)
            kf, r_kf = A.alloc("kf", [E], F32)
            kb_, r_kb = A.alloc("kb", [E], BF16)
            vf, r_vf = A.alloc("vf", [E], F32)
            vb, r_vb = A.alloc("vb", [E], BF16)
            szt, r_szt = A.alloc("szt", [E], F32)
            qT, r_qT = A.alloc("qT", [16, 128], BF16)
            kT, r_kT = A.alloc("kT", [16, 128], BF16)
            t16, r_t16 = A.alloc("t16", [NH], F32)
            e16, r_e16 = A.alloc("e16", [NH], F32)
            l16, r_l16 = A.alloc("l16", [NH], F32)
            lfs_ = [A.alloc(f"lf{i}", [NH], F32) for i in range(2)]

            for bi in range(4):
                wd = E if bi < 3 else E + NH
                wload([(lambda e, c=c, bi=bi, wd=wd: e.dma_start(out=Wfb[bi][0][:, c, :], in_=f_w_in[jl, c * 128:(c + 1) * 128, bi * E:bi * E + wd]))
                       for c in range(8)], [Wfb[bi][1]])
            load([lambda e: e.dma_start(out=gpre, in_=norm_pre[layer:layer + 1, :].to_broadcast([128, D])),
                  lambda e: e.dma_start(out=bfb, in_=f_b_f[jl:jl + 1, :].to_broadcast([128, NH]))], [], [r_gpre, r_bfb])

            def sL(t):
                nr = nrows(t)
                xt, r_xt = xts[t % 2]
                load([lambda e: e.dma_start(out=xt[0:nr], in_=xsrc(layer, t))], [XR[t]], [r_xt])

            def s0a(t):
                xt, r_xt = xts[t % 2]
                head_a(B, t, xt, r_xt)

            def s0b(t):
                head_b(B, t, 0)

            def s1(t):
                nr = nrows(t)
                r0 = t * 128
                hT, r_hT = B["hT"][t % 2]
                lf, r_lf = lfs_[t % 2]
                for n in range(17):
                    bk = 1 + (n % 3)
                    ncol = 512 if n < 16 else NH
                    wi = min(n // 4, 3)
                    wc0 = n * 512 - wi * E
                    P.op("pe", [(lambda e, c=c, bk=bk, ncol=ncol, wi=wi, wc0=wc0: e.matmul(bank(bk)[0:nr, 0:ncol], lhsT=hT[:, c, 0:nr],
                                                                                          rhs=Wfb[wi][0][:, c, wc0:wc0 + ncol],
                                                                                          start=(c == 0), stop=(c == 7))) for c in range(8)],
                         reads=[r_hT, Wfb[wi][1]], writes=[psr[bk]])
                    c0 = (n % 4) * 512
                    if n < 4:
                        P.op("act", lambda e, c0=c0, bk=bk: e.activation(out=qb[0:nr, c0:c0 + 512], in_=bank(bk)[0:nr, :], func=AF.Copy),
                             reads=[psr[bk]], writes=[r_qb])
                    elif n < 8:
                        P.op("act", lambda e, c0=c0, bk=bk: e.activation(out=kf[0:nr, c0:c0 + 512], in_=bank(bk)[0:nr, :], func=AF.Copy),
                             reads=[psr[bk]], writes=[r_kf])
                        P.op("dve", lambda e, c0=c0, bk=bk: e.tensor_copy(out=kb_[0:nr, c0:c0 + 512], in_=bank(bk)[0:nr, :]),
                             reads=[], writes=[psr[bk], r_kb])
                    elif n < 12:
                        P.op("dve", lambda e, c0=c0, bk=bk: e.tensor_copy(out=vf[0:nr, c0:c0 + 512], in_=bank(bk)[0:nr, :]),
                             reads=[psr[bk]], writes=[r_vf])
                        P.op("act", lambda e, c0=c0, bk=bk: e.activation(out=vb[0:nr, c0:c0 + 512], in_=bank(bk)[0:nr, :], func=AF.Copy),
                             reads=[], writes=[psr[bk], r_vb])
                    elif n < 16:
                        P.op("act", lambda e, c0=c0, bk=bk: e.activation(out=szt[0:nr, c0:c0 + 512], in_=bank(bk)[0:nr, :], func=AF.Silu),
                             reads=[psr[bk]], writes=[r_szt])
                    else:
                        P.op("dve", lambda e, bk=bk: e.tensor_tensor(out=t16[0:nr], in0=bank(bk)[0:nr, 0:NH], in1=bfb[0:nr], op=ALU.add),
                             reads=[psr[bk], r_bfb], writes=[r_t16])
                    if n == 5:
                        pTb = pair_bf(2)
                        P.op("pe", [(lambda e, c=c: e.transpose(out=pTb[:, c * 128:c * 128 + nr], in_=qb[0:nr, c * 128:(c + 1) * 128],
                                                                identity=idb[0:nr, 0:nr])) for c in range(16)],
                             reads=[r_qb, r_idb], writes=[psr[4], psr[5]])
                        P.op("act", lambda e, pTb=pTb: e.activation(out=qT[:, :, 0:nr], in_=pTb.rearrange("p (a b) -> p a b", b=128)[:, :, 0:nr], func=AF.Copy),
                             reads=[psr[4], psr[5]], writes=[r_qT])
                        store([lambda e: e.dma_start(out=QTs[:, :, r0:r0 + nr].rearrange("h d r -> d h r"), in_=qT[:, :, 0:nr])],
                              [r_qT], [QTr[t]])
                    if n == 7:
                        if t < 32:
                            store([lambda e: e.dma_start(out=fk_p[jl, r0:r0 + nr, :], in_=kf[0:nr])], [r_kf], [])
                        else:
                            store([lambda e: e.dma_start(out=fk_s[jl, :, :], in_=kf[0:nr])], [r_kf], [])
                    if n == 9:
                        pTb2 = pair_bf(3)
                        P.op("pe", [(lambda e, c=c: e.transpose(out=pTb2[:, c * 128:c * 128 + nr], in_=kb_[0:nr, c * 128:(c + 1) * 128],
                                                                identity=idb[0:nr, 0:nr])) for c in range(16)],
                             reads=[r_kb, r_idb], writes=[psr[6], psr[7]])
                        P.op("act", lambda e, pTb2=pTb2: e.activation(out=kT[:, :, 0:nr], in_=pTb2.rearrange("p (a b) -> p a b", b=128)[:, :, 0:nr], func=AF.Copy),
                             reads=[psr[6], psr[7]], writes=[r_kT])
                        store([lambda e: e.dma_start(out=KTs[:, :, r0:r0 + nr].rearrange("h d r -> d h r"), in_=kT[:, :, 0:nr])],
                              [r_kT], [KTr[t]])
                    if n == 11:
                        if t < 32:
                            store([lambda e: e.dma_start(out=fv_p[jl, r0:r0 + nr, :], in_=vf[0:nr])], [r_vf], [])
                        else:
                            store([lambda e: e.dma_start(out=fv_s[jl, :, :], in_=vf[0:nr])], [r_vf], [])
                        store([lambda e: e.dma_start(out=Vbs[:, r0:r0 + nr, :].rearrange("h r e -> r h e"),
                                                     in_=vb[0:nr].rearrange("p (h e) -> p h e", e=DH))], [r_vb], [VBr[t]])
                    if n == 15:
                        store([lambda e: e.dma_start(out=SZs[:, r0:r0 + nr, :].rearrange("h r e -> r h e"),
                                                     in_=szt[0:nr].rearrange("p (h e) -> p h e", e=DH))], [r_szt], [SZr[t]])
                    if n == 16:
                        P.op("act", lambda e: e.activation(out=e16[0:nr], in_=t16[0:nr], func=AF.Exp, scale=-1.0), reads=[r_t16], writes=[r_e16])
                        P.op("act", lambda e: e.activation(out=l16[0:nr], in_=e16[0:nr], func=AF.Ln, bias=1.0, scale=1.0), reads=[r_e16], writes=[r_l16])
                        P.op("dve", lambda e: e.tensor_scalar(out=lf[0:nr], in0=l16[0:nr], scalar1=-1.0, scalar2=None, op0=ALU.mult),
                             reads=[r_l16], writes=[r_lf])
                        if t < 32:
                            store([lambda e: e.dma_start(out=flf_p[jl, r0:r0 + nr, :], in_=lf[0:nr])], [r_lf], [])
                        else:
                            store([lambda e: e.dma_start(out=flf_s[jl, :, :], in_=lf[0:nr])], [r_lf], [LFSr])

            def s2(t):
                if t >= 32:
                    return
                lf, r_lf = lfs_[t % 2]
                bk = 1 + (t % 3)
                P.op("pe", [lambda e: e.matmul(bank(bk)[:, 0:NH], lhsT=tstr, rhs=lf, start=True, stop=True),
                            lambda e: e.matmul(bank(bk)[:, NH:2 * NH], lhsT=ones, rhs=lf, start=True, stop=True)],
                     reads=[r_cs, r_lf], writes=[psr[bk]])
                if t == 0:
                    P.op("dve", lambda e: e.tensor_copy(out=TcTab[:, 0, :], in_=bank(bk)[:, NH:2 * NH]), reads=[psr[bk]], writes=[r_TcTab])
                else:
                    P.op("dve", lambda e: e.tensor_tensor(out=TcTab[:, t, :], in0=bank(bk)[:, NH:2 * NH], in1=TcTab[:, t - 1, :], op=ALU.add),
                         reads=[psr[bk], r_TcTab], writes=[r_TcTab])
                P.op("dve", lambda e: e.tensor_tensor(out=ETab[:, t, :], in0=bank(bk)[:, 0:NH], in1=TcTab[:, t, :], op=ALU.subtract),
                     reads=[psr[bk], r_TcTab], writes=[r_ETab])

            run_stages([sL, s0a, s0b, s1, s2], TILES)

        def phase_A(layer):
            P.new_epoch()
            A.reset(PH0)
            QTh = [A.alloc(f"QTh{i}", [SEQ], BF16) for i in range(2)]
            KTh = [A.alloc(f"KTh{i}", [SEQ], BF16) for i in range(2)]
            Vh = [A.alloc(f"Vh{i}", [32, 132], BF16) for i in range(2)]
            SZh = [A.alloc(f"SZh{i}", [32, 128], F32) for i in range(2)]
            Yh = [A.alloc(f"Yh{i}", [32, 128], BF16) for i in range(2)]
            bT = [A.alloc(f"bT{i}", [32, 8], F32) for i in range(2)]
            Pb = [A.alloc(f"Pb{i}", [512], BF16) for i in range(3)]
            rinv = [A.alloc(f"rinv{i}", [1], F32) for i in range(4)]
            for i in range(2):
                P.op("dve", lambda e, i=i: e.memset(Vh[i][0][:, :, 128:129], 1.0), writes=[Vh[i][1]])

            def load_head(h):
                s = h % 2
                load([lambda e: e.dma_start(out=QTh[s][0], in_=QTs[h, :, 0:SEQ]),
                      lambda e: e.dma_start(out=KTh[s][0], in_=KTs[h, :, 0:SEQ])],
                     QTr[0:32] + KTr[0:32], [QTh[s][1], KTh[s][1]])
                load([lambda e: e.dma_start(out=Vh[s][0][:, :, 0:128], in_=Vbs[h, 0:SEQ, :].rearrange("(t p) e -> p t e", p=128)),
                      lambda e: e.dma_start(out=SZh[s][0], in_=SZs[h, 0:SEQ, :].rearrange("(t p) e -> p t e", p=128))],
                     VBr[0:32] + SZr[0:32], [Vh[s][1], SZh[s][1]])
                for qg in range(8):
                    P.op("dve", lambda e, qg=qg: e.tensor_scalar(out=bT[s][0][:, :, qg], in0=ETab[:, :, h],
                                                                  scalar1=TcTab[:, 4 * qg + 3, h:h + 1], scalar2=None, op0=ALU.add),
                         reads=[r_ETab, r_TcTab], writes=[bT[s][1]])

            its = []
            for h in range(NH):
                for qg in range(8):
                    for kb in range(4 * qg + 4):
                        its.append((h, qg, kb))

            def psO(qg, qt):
                b = (qg % 2) * 2 + qt // 2
                return bank(b)[:, (qt % 2) * 256:(qt % 2) * 256 + 129], b

            def do_S(i):
                h, qg, kb = its[i]
                s = h % 2
                d = max(0, kb - 4 * qg)
                n = (4 - d) * 128
                q0 = (4 * qg + d) * 128
                bk = 4 + i % 3
                P.op("pe", lambda e: e.matmul(bank(bk)[:, 0:n], lhsT=KTh[s][0][:, kb * 128:(kb + 1) * 128],
                                              rhs=QTh[s][0][:, q0:q0 + n], start=True, stop=True),
                     reads=[KTh[s][1], QTh[s][1]], writes=[psr[bk]])

            def do_rest(i):
                h, qg, kb = its[i]
                s = h % 2
                d = max(0, kb - 4 * qg)
                n = (4 - d) * 128
                bk = 4 + i % 3
                pb_, r_pb = Pb[i % 3]
                P.op("act", lambda e: e.activation(out=pb_[:, 0:n], in_=bank(bk)[:, 0:n], func=AF.Exp,
                                                   bias=bT[s][0][:, kb, qg:qg + 1], scale=SCALE),
                     reads=[psr[bk], bT[s][1]], writes=[r_pb])
                if kb >= 4 * qg:
                    P.op("pool", lambda e: e.tensor_tensor(out=pb_[:, 0:128], in0=pb_[:, 0:128], in1=mTb, op=ALU.mult),
                         reads=[r_pb, r_mTb], writes=[r_pb])
                fns = []
                banks = set()
                for qt in range(d, 4):
                    o_ap, b = psO(qg, qt)
                    banks.add(b)
                    fns.append(lambda e, qt=qt, o_ap=o_ap: e.matmul(o_ap, lhsT=pb_[:, (qt - d) * 128:(qt - d + 1) * 128],
                                                                     rhs=Vh[s][0][:, kb, 0:129], start=(kb == 0 and qt % 2 == 0), stop=(kb == 4 * qg + qt),
                                                                     skip_group_check=True))
                P.op("pe", fns, reads=[r_pb, Vh[s][1]], writes=[psr[b] for b in sorted(banks)])
                if kb == 4 * qg + 3:
                    for qt in range(4):
                        o_ap, b = psO(qg, qt)
                        ri, r_ri = rinv[qt]
                        P.op("dve", lambda e, o_ap=o_ap, ri=ri: e.reciprocal(out=ri, in_=o_ap[:, 128:129]), reads=[psr[b]], writes=[r_ri])
                        P.op("dve", lambda e, o_ap=o_ap, ri=ri, qt=qt: e.scalar_tensor_tensor(
                            out=Yh[s][0][:, 4 * qg + qt, :], in0=o_ap[:, 0:128], scalar=ri, in1=SZh[s][0][:, 4 * qg + qt, :],
                            op0=ALU.mult, op1=ALU.mult), reads=[psr[b], r_ri, SZh[s][1]], writes=[Yh[s][1]])
                    if qg == 7:
                        store([lambda e: e.dma_start(out=Ysc[0:SEQ, h * 128:(h + 1) * 128].rearrange("(t p) e -> p t e", p=128), in_=Yh[s][0])],
                              [Yh[s][1]], YS[0:32])

            load_head(0)
            n_it = len(its)
            do_S(0)
            do_S(1)
            for i in range(n_it):
                h, qg, kb = its[i]
                if qg == 0 and kb == 0 and h + 1 < NH:
                    load_head(h + 1)
                if i + 2 < n_it:
                    do_S(i + 2)
                do_rest(i)

        def phase_S(layer):
            jl = layer // 2
            P.new_epoch()
            A.reset(PH0)
            qTs, r_qTs = A.alloc("qTs", [16, 16], BF16)
            kTs, r_kTs = A.alloc("kTs", [16, 16], BF16)
            vs, r_vs = A.alloc("vs", [16, 132], BF16)
            szs, r_szs = A.alloc("szs", [16, 128], F32)
            lfs, r_lfs = A.alloc("lfs", [NH], F32)
            lfc, r_lfc = A.alloc("lfc", [16, 16], F32)
            sfxc, r_sfxc = A.alloc("sfxc", [16, 16], F32)
            totc, r_totc = A.alloc("totc", [16, 16], F32)
            biasS, r_biasS = A.alloc("biasS", [16, 16], F32)
            biasN, r_biasN = A.alloc("biasN", [NH], F32)
            Rs = [A.alloc(f"R{i}", [NH], F32) for i in range(17)]
            kc = [A.alloc(f"kc{i}", [1024], F32) for i in range(2)]
            vc = [A.alloc(f"vc{i}", [1024], F32) for i in range(2)]
            kcT = [A.alloc(f"kcT{i}", [8, 128], BF16) for i in range(2)]
            Vc = [A.alloc(f"Vc{i}", [8, 132], BF16) for i in range(2)]
            tmpS, r_tmpS = A.alloc("tmpS", [128], F32)
            Ps = [A.alloc(f"Ps{i}", [128], BF16) for i in range(2)]
            rinv = [A.alloc(f"rinvs{i}", [1], F32) for i in range(2)]
            Ysb, r_Ysb = A.alloc("Ysb", [E], BF16)

            load([lambda e: e.dma_start(out=qTs, in_=QTs[:, :, SEQ:SEQ + TS].rearrange("h d r -> d h r")),
                  lambda e: e.dma_start(out=kTs, in_=KTs[:, :, SEQ:SEQ + TS].rearrange("h d r -> d h r")),
                  lambda e: e.dma_start(out=vs[0:TS, :, 0:128], in_=Vbs[:, SEQ:SEQ + TS, :].rearrange("h r e -> r h e")),
                  lambda e: e.dma_start(out=szs[0:TS], in_=SZs[:, SEQ:SEQ + TS, :].rearrange("h r e -> r h e")),
                  lambda e: e.dma_start(out=lfs[0:TS], in_=flf_s[jl, :, :]),
                  lambda e: e.dma_start(out=lfc, in_=clf[jl].rearrange("(t p) h -> p t h", p=128))],
                 [QTr[32], KTr[32], VBr[32], SZr[32], LFSr], [r_qTs, r_kTs, r_vs, r_szs, r_lfs, r_lfc])
            P.op("dve", lambda e: e.memset(vs[0:TS, :, 128:129], 1.0), writes=[r_vs])
            for i in range(2):
                P.op("dve", lambda e, i=i: e.memset(Vc[i][0][:, :, 128:129], 1.0), writes=[Vc[i][1]])
            lfc2 = lfc.rearrange("p a b -> p (a b)")
            P.op("pe", [lambda e: e.matmul(bank(6)[:, 0:256], lhsT=tstr, rhs=lfc2, start=True, stop=True),
                        lambda e: e.matmul(bank(6)[:, 256:512], lhsT=ones, rhs=lfc2, start=True, stop=True),
                        lambda e: e.matmul(bank(7)[0:TS, 0:NH], lhsT=tstr[0:TS, 0:TS], rhs=lfs[0:TS], start=True, stop=True),
                        lambda e: e.matmul(bank(7)[:, NH:2 * NH], lhsT=ones[0:TS, :], rhs=lfs[0:TS], start=True, stop=True)],
                 reads=[r_cs, r_lfc, r_lfs], writes=[psr[6], psr[7]])
            P.op("dve", lambda e: e.tensor_copy(out=sfxc.rearrange("p a b -> p (a b)"), in_=bank(6)[:, 0:256]), reads=[psr[6]], writes=[r_sfxc])
            P.op("dve", lambda e: e.tensor_copy(out=totc.rearrange("p a b -> p (a b)"), in_=bank(6)[:, 256:512]), reads=[psr[6]], writes=[r_totc])
            P.op("dve", lambda e: e.tensor_copy(out=biasN[0:TS], in_=bank(7)[0:TS, 0:NH]), reads=[psr[7]], writes=[r_biasN])
            P.op("dve", lambda e: e.tensor_copy(out=Rs[16][0], in_=bank(7)[:, NH:2 * NH]), reads=[psr[7]], writes=[Rs[16][1]])
            for kt in range(15, -1, -1):
                P.op("dve", lambda e, kt=kt: e.tensor_tensor(out=biasS[:, kt, :], in0=sfxc[:, kt, :], in1=Rs[kt + 1][0], op=ALU.add),
                     reads=[r_sfxc, Rs[kt + 1][1]], writes=[r_biasS])
                P.op("dve", lambda e, kt=kt: e.tensor_tensor(out=Rs[kt][0], in0=Rs[kt + 1][0], in1=totc[:, kt, :], op=ALU.add),
                     reads=[Rs[kt + 1][1], r_totc], writes=[Rs[kt][1]])

            def psOs(hl):
                b = hl // 3
                return bank(b)[0:TS, (hl % 3) * 160:(hl % 3) * 160 + 129], b

            def s_iter(hh, kt):
                if True:
                    s = kt % 2
                    if kt < 16:
                        load([lambda e, kt=kt, s=s: e.dma_start(out=kc[s][0], in_=ck[jl, kt * 128:(kt + 1) * 128, hh * 1024:(hh + 1) * 1024]),
                              lambda e, kt=kt, s=s: e.dma_start(out=vc[s][0], in_=cv[jl, kt * 128:(kt + 1) * 128, hh * 1024:(hh + 1) * 1024])],
                             [], [kc[s][1], vc[s][1]])
                        pk = pair(2)
                        P.op("pe", [(lambda e, hl=hl, s=s: e.transpose(out=pk[:, hl * 128:(hl + 1) * 128], in_=kc[s][0][:, hl * 128:(hl + 1) * 128], identity=ident))
                                    for hl in range(8)], reads=[kc[s][1], r_cs], writes=[psr[4], psr[5]])
                        P.op("act", lambda e, s=s: e.activation(out=kcT[s][0].rearrange("p a b -> p (a b)"), in_=pk, func=AF.Copy),
                             reads=[psr[4], psr[5]], writes=[kcT[s][1]])
                        P.op("pool", lambda e, s=s: e.tensor_copy(out=Vc[s][0][:, :, 0:128], in_=vc[s][0].rearrange("p (h e) -> p h e", e=128)),
                             reads=[vc[s][1]], writes=[Vc[s][1]])
                        P.op("pe", [(lambda e, hl=hl, s=s: e.matmul(bank(6)[:, hl * 16:(hl + 1) * 16], lhsT=kcT[s][0][:, hl, :],
                                                                    rhs=qTs[:, hh * 8 + hl, :], start=True, stop=True)) for hl in range(8)],
                             reads=[kcT[s][1], r_qTs], writes=[psr[6]])
                        P.op("dve", lambda e, kt=kt: e.scalar_tensor_tensor(
                            out=tmpS.rearrange("p (h q) -> p h q", q=16), in0=bank(6)[:, 0:128].rearrange("p (h q) -> p h q", q=16), scalar=SCALE,
                            in1=biasS[:, kt, hh * 8:(hh + 1) * 8].unsqueeze(2).to_broadcast([128, 8, 16]), op0=ALU.mult, op1=ALU.add),
                            reads=[psr[6], r_biasS], writes=[r_tmpS])
                        P.op("act", lambda e, s=s: e.activation(out=Ps[s][0], in_=tmpS, func=AF.Exp), reads=[r_tmpS], writes=[Ps[s][1]])
                        fns = []
                        banks = set()
                        for hl in range(8):
                            o_ap, b = psOs(hl)
                            banks.add(b)
                            fns.append(lambda e, hl=hl, o_ap=o_ap, s=s, kt=kt: e.matmul(o_ap, lhsT=Ps[s][0][:, hl * 16:(hl + 1) * 16],
                                                                                       rhs=Vc[s][0][:, hl, 0:129], start=(kt == 0 and hl % 3 == 0), stop=False,
                                                                                       skip_group_check=True))
                        P.op("pe", fns, reads=[Ps[s][1], Vc[s][1]], writes=[psr[b] for b in sorted(banks)])
                    else:
                        P.op("pe", [(lambda e, hl=hl: e.matmul(bank(6)[0:TS, hl * 16:(hl + 1) * 16], lhsT=kTs[:, hh * 8 + hl, :],
                                                               rhs=qTs[:, hh * 8 + hl, :], start=True, stop=True)) for hl in range(8)],
                             reads=[r_kTs, r_qTs], writes=[psr[6]])
                        P.op("dve", lambda e: e.scalar_tensor_tensor(
                            out=tmpS[0:TS].rearrange("p (h q) -> p h q", q=16), in0=bank(6)[0:TS, 0:128].rearrange("p (h q) -> p h q", q=16), scalar=SCALE,
                            in1=biasN[0:TS, hh * 8:(hh + 1) * 8].unsqueeze(2).to_broadcast([TS, 8, 16]), op0=ALU.mult, op1=ALU.add),
                            reads=[psr[6], r_biasN], writes=[r_tmpS])
                        P.op("act", lambda e, s=s: e.activation(out=Ps[s][0][0:TS], in_=tmpS[0:TS], func=AF.Exp), reads=[r_tmpS], writes=[Ps[s][1]])
                        P.op("pool", lambda e, s=s: e.tensor_tensor(
                            out=Ps[s][0][0:TS].rearrange("p (h q) -> p h q", q=16), in0=Ps[s][0][0:TS].rearrange("p (h q) -> p h q", q=16),
                            in1=mTb[0:TS, 0:TS].unsqueeze(1).to_broadcast([TS, 8, 16]), op=ALU.mult),
                            reads=[Ps[s][1], r_mTb], writes=[Ps[s][1]])
                        fns = []
                        banks = set()
                        for hl in range(8):
                            o_ap, b = psOs(hl)
                            banks.add(b)
                            fns.append(lambda e, hl=hl, o_ap=o_ap, s=s: e.matmul(o_ap, lhsT=Ps[s][0][0:TS, hl * 16:(hl + 1) * 16],
                                                                                rhs=vs[0:TS, hh * 8 + hl, 0:129], start=False, stop=True, skip_group_check=True))
                        P.op("pe", fns, reads=[Ps[s][1], r_vs], writes=[psr[b] for b in sorted(banks)])

            for hh in range(2):
                for kt in range(17):
                    s_iter(hh, kt)
                for hl in range(8):
                    o_ap, b = psOs(hl)
                    ri, r_ri = rinv[hl % 2]
                    hg = hh * 8 + hl
                    P.op("dve", lambda e, o_ap=o_ap, ri=ri: e.reciprocal(out=ri[0:TS], in_=o_ap[:, 128:129]), reads=[psr[b]], writes=[r_ri])
                    P.op("dve", lambda e, o_ap=o_ap, ri=ri, hg=hg: e.scalar_tensor_tensor(
                        out=Ysb[0:TS, hg * 128:(hg + 1) * 128], in0=o_ap[:, 0:128], scalar=ri[0:TS], in1=szs[0:TS, hg, :],
                        op0=ALU.mult, op1=ALU.mult), reads=[psr[b], r_ri, r_szs], writes=[r_Ysb])
            store([lambda e: e.dma_start(out=Ysc[SEQ:SEQ + TS, :], in_=Ysb[0:TS])], [r_Ysb], [YS[32]])

        for layer in range(n_layers):
            jl = layer // 2
            if layer % 2 == 0:
                if 'G' in PHASES:
                    phase_G(layer)
                if 'T' in PHASES:
                    phase_T(layer, g_w_out[jl])
            else:
                if 'F' in PHASES:
                    phase_F(layer)
                if 'A' in PHASES:
                    phase_A(layer)
                if 'S' in PHASES:
                    phase_S(layer)
                if 'T' in PHASES:
                    phase_T(layer, f_w_out[jl])

        P.emit(final_ops=stores)
    return nc


def _consts():
    c = np.zeros((128, 640), np.float32)
    i = np.arange(128)
    c[:, 0:128] = np.eye(128, dtype=np.float32)
    c[:, 128:256] = (i[:, None] > i[None, :])
    c[:, 256:384] = 1.0
    c[:, 384:512] = (i[:, None] <= i[None, :])
    c[:, 512:640] = ((i[:, None] // 64) <= (i[None, :] // 64))
    return c


_NC_CACHE = {}


def kernel(x_prompt, x_sample, cache_fox_k, cache_fox_v, cache_fox_logf, p_prompt, p_sample,
           norm_pre, norm_post, gmlp_w_in, gmlp_ln_g, gmlp_ln_b, gmlp_w_s, gmlp_b_s, gmlp_w_out,
           fox_w_in, fox_b_f, fox_w_out, ple_w_proj, ple_w_gate, _n_layers=4, _cores=8):
    f = lambda a: np.ascontiguousarray(np.asarray(a), dtype=np.float32)
    n = _cores
    if _n_layers not in _NC_CACHE:
        _NC_CACHE[_n_layers] = build(_n_layers)
    nc = _NC_CACHE[_n_layers]
    shared = {
        "norm_pre": f(norm_pre), "norm_post": f(norm_post),
        "g_w_in": f(gmlp_w_in), "g_ln_g": f(gmlp_ln_g), "g_ln_b": f(gmlp_ln_b), "g_w_s": f(gmlp_w_s),
        "g_b_s": f(gmlp_b_s), "g_w_out": f(gmlp_w_out), "f_w_in": f(fox_w_in), "f_b_f": f(fox_b_f),
        "f_w_out": f(fox_w_out), "ple_proj": f(ple_w_proj), "ple_gate": f(ple_w_gate), "consts": _consts(),
    }
    xp = np.asarray(x_prompt); xs = np.asarray(x_sample)
    ckk = np.asarray(cache_fox_k); cvv = np.asarray(cache_fox_v); clff = np.asarray(cache_fox_logf)
    pp = np.asarray(p_prompt); psm = np.asarray(p_sample)
    in_maps = []
    for b in range(n):
        m = dict(shared)
        m["x_p"] = f(xp[b]); m["x_s"] = f(xs[b])
        m["ck"] = f(ckk[:, b].reshape(2, PAST, E)); m["cv"] = f(cvv[:, b].reshape(2, PAST, E)); m["clf"] = f(clff[:, b])
        m["p_p"] = f(pp[:, b]); m["p_s"] = f(psm[:, b])
        in_maps.append(m)
    res = run_bass_kernel_spmd(nc, in_maps, core_ids=list(range(n)))
    R = res.results
    st = lambda k, ax: np.stack([np.asarray(R[b][k], dtype=np.float32) for b in range(n)], axis=ax)
    y_prompt = st("y_p", 0)
    y_sample = st("y_s", 0)
    gv = st("gv_s", 1)
    fk_p = st("fk_p", 1).reshape(2, n, SEQ, NH, DH)
    fv_p = st("fv_p", 1).reshape(2, n, SEQ, NH, DH)
    flf_p = st("flf_p", 1)
    fk_s = st("fk_s", 1).reshape(2, n, TS, NH, DH)
    fv_s = st("fv_s", 1).reshape(2, n, TS, NH, DH)
    flf_s = st("flf_s", 1)
    return (y_prompt, y_sample, gv, fk_p, fv_p, flf_p, fk_s, fv_s, flf_s)
```
